# Optimizing a Trainium2 kernel written in Bass

```python
import math
import jax
import jax.numpy as jnp
from jax import lax
import numpy as np

D_MODEL = 1024
BATCH = 32
SEQ = 2048
DEPTH = 4

N_MIXERS = 2
NORM_EPS = 1e-6

D_FF = 2816
MACARON_WEIGHT = 0.5

A_PATTERNS = ((128, 1), (512, 4), (2048, 16))
A_GROUPS = len(A_PATTERNS)
A_HEADS = 8
A_HEAD_DIM = 64
A_GROUP_WIDTH = A_HEADS * A_HEAD_DIM
A_IN_WIDTH = A_GROUPS * 3 * A_GROUP_WIDTH
A_BLOCK = 128
NEG_INF = -1e30

NUM_BUCKETS = 32
MAX_DISTANCE = 2048

B_HEADS = 8
B_HEAD_DIM = 128
B_WIDTH = B_HEADS * B_HEAD_DIM
B_IN_WIDTH = 4 * B_WIDTH + 2 * B_HEADS
B_CONV = 4
B_CHUNK = 64

N_A_LAYERS = len(range(0, DEPTH, N_MIXERS))
N_B_LAYERS = len(range(1, DEPTH, N_MIXERS))

kernel_name = "hybrid_dilated_attn_gated_deltanet_macaron"


def rms_norm(x, g):
    xf = x.astype(jnp.float32)
    y = xf * lax.rsqrt(jnp.mean(xf * xf, axis=-1, keepdims=True) + NORM_EPS)
    return (y * g.astype(jnp.float32)).astype(x.dtype)


def swiglu(h, w_gate, w_up, w_down):
    return (jax.nn.silu(h @ w_gate) * (h @ w_up)) @ w_down


def t5_bucket(distance):
    max_exact = NUM_BUCKETS // 2
    n = distance.astype(jnp.float32)
    large = max_exact + (jnp.log(jnp.maximum(n, 1.0) / max_exact)
                         / math.log(MAX_DISTANCE / max_exact) * (NUM_BUCKETS - max_exact))
    large = jnp.minimum(large.astype(jnp.int32), NUM_BUCKETS - 1)
    return jnp.where(distance < max_exact, distance, large)


def dilated_group_attention(q, k, v, bias_h, window, dilation):
    B, S, H, dh = q.shape
    L = S // dilation
    nb = -(-L // A_BLOCK)
    Lp = nb * A_BLOCK
    steps = window // dilation

    def to_blocks(t):
        t = t.reshape(B, L, dilation, H, dh).transpose(0, 2, 1, 3, 4)
        t = jnp.pad(t, ((0, 0), (0, 0), (0, Lp - L), (0, 0), (0, 0)))
        return t.reshape(B, dilation, nb, A_BLOCK, H, dh)

    def with_prev(t):
        prev = jnp.pad(t, ((0, 0), (0, 0), (1, 0), (0, 0), (0, 0), (0, 0)))[:, :, :-1]
        return jnp.concatenate([prev, t], axis=3)

    qb = to_blocks(q)
    kk = with_prev(to_blocks(k))
    vv = with_prev(to_blocks(v))

    s = jnp.einsum('bcnqhe,bcnkhe->bcnhqk', qb, kk) * (A_HEAD_DIM ** -0.5)
    q_loc = jnp.arange(A_BLOCK)[:, None]
    k_loc = jnp.arange(2 * A_BLOCK)[None, :]
    rel = q_loc + A_BLOCK - k_loc
    bias = bias_h.astype(jnp.float32)[t5_bucket(jnp.maximum(rel, 0) * dilation)]
    s = s + bias.transpose(2, 0, 1)
    k_idx = jnp.arange(nb)[:, None] * A_BLOCK + k_loc - A_BLOCK
    valid = ((rel >= 0) & (rel <= steps))[None] & (k_idx >= 0)[:, None, :]
    s = jnp.where(valid[:, None], s, NEG_INF)

    m = jnp.max(s, axis=-1, keepdims=True)
    p = jnp.exp(s - m)
    den = jnp.sum(p, axis=-1)
    o = jnp.einsum('bcnhqk,bcnkhe->bcnqhe', p, vv) / den.transpose(0, 1, 2, 4, 3)[..., None]
    lse = m[..., 0] + jnp.log(den)

    o = o.reshape(B, dilation, Lp, H, dh)[:, :, :L].transpose(0, 2, 1, 3, 4).reshape(B, S, H, dh)
    lse = lse.transpose(0, 1, 2, 4, 3).reshape(B, dilation, Lp, H)[:, :, :L]
    lse = lse.transpose(0, 2, 1, 3).reshape(B, S, H)
    return o, lse


def dilated_attention_mixer(h, w_in, rel_bias, w_out):
    B, S, _ = h.shape
    proj = (h @ w_in).astype(jnp.float32).reshape(B, S, A_GROUPS, 3, A_HEADS, A_HEAD_DIM)
    outs, lses = [], []
    for g, (window, dilation) in enumerate(A_PATTERNS):
        o, lse = dilated_group_attention(proj[:, :, g, 0], proj[:, :, g, 1], proj[:, :, g, 2],
                                         rel_bias[:, g * A_HEADS:(g + 1) * A_HEADS],
                                         window, dilation)
        outs.append(o)
        lses.append(lse)
    alpha = jax.nn.softmax(jnp.stack(lses, 0), axis=0)
    o = jnp.einsum('gbsh,gbshe->bshe', alpha, jnp.stack(outs, 0))
    return o.reshape(B, S, A_GROUP_WIDTH).astype(h.dtype) @ w_out


def causal_depthwise_conv(x, w):
    K, C = w.shape
    return lax.conv_general_dilated(x, w[:, None, :].astype(x.dtype), window_strides=(1,),
                                    padding=[(K - 1, 0)],
                                    dimension_numbers=('NWC', 'WIO', 'NWC'),
                                    feature_group_count=C)


def l2_normalize(t):
    return t * lax.rsqrt(jnp.sum(t * t, axis=-1, keepdims=True) + NORM_EPS)


def chunk_gated_delta_rule(q, k, v, g, beta):
    B, S, H, dk = q.shape
    dv = v.shape[-1]
    C = B_CHUNK
    N = S // C

    def chunks(t):
        if t.ndim == 4:
            return t.reshape(B, N, C, H, t.shape[-1]).transpose(0, 3, 1, 2, 4)
        return t.reshape(B, N, C, H).transpose(0, 3, 1, 2)

    q, k, v, g, beta = map(chunks, (q, k, v, g, beta))
    gc = jnp.cumsum(g, axis=-1)
    idx = jnp.arange(C)
    tril = idx[:, None] >= idx[None, :]
    strict = idx[:, None] > idx[None, :]
    decay = jnp.exp(jnp.where(tril, gc[..., :, None] - gc[..., None, :], -jnp.inf))

    kb = k * beta[..., None]
    lower = jnp.where(strict, jnp.einsum('bhnid,bhnjd->bhnij', kb, k) * decay, 0.0)
    t_sys = jnp.eye(C, dtype=jnp.float32) + lower
    u = lax.linalg.triangular_solve(t_sys, v * beta[..., None], left_side=True, lower=True)
    w = lax.linalg.triangular_solve(t_sys, kb * jnp.exp(gc)[..., None], left_side=True, lower=True)

    attn = jnp.einsum('bhnid,bhnjd->bhnij', q, k) * decay
    q_dec = q * jnp.exp(gc)[..., None]
    k_tail = k * jnp.exp(gc[..., -1:] - gc)[..., None]
    chunk_dec = jnp.exp(gc[..., -1])

    def step(state, inp):
        u_n, w_n, attn_n, qd_n, kt_n, cd_n = inp
        v_new = u_n - jnp.einsum('bhcd,bhde->bhce', w_n, state)
        o_n = (jnp.einsum('bhcd,bhde->bhce', qd_n, state)
               + jnp.einsum('bhij,bhje->bhie', attn_n, v_new))
        state = state * cd_n[..., None, None] + jnp.einsum('bhcd,bhce->bhde', kt_n, v_new)
        return state, o_n

    xs = tuple(jnp.moveaxis(t, 2, 0) for t in (u, w, attn, q_dec, k_tail, chunk_dec))
    state0 = jnp.zeros((B, H, dk, dv), jnp.float32)
    _, o = lax.scan(step, state0, xs)
    return o.transpose(1, 0, 3, 2, 4).reshape(B, S, H, dv)


def gated_deltanet_mixer(h, w_in, conv_w, a_log, dt_bias, norm_w, w_out):
    B, S, _ = h.shape
    proj = h @ w_in
    qkv, z, a, b = jnp.split(proj, [3 * B_WIDTH, 4 * B_WIDTH, 4 * B_WIDTH + B_HEADS], axis=-1)
    qkv = jax.nn.silu(causal_depthwise_conv(qkv, conv_w).astype(jnp.float32))
    q, k, v = jnp.split(qkv, 3, axis=-1)
    q = l2_normalize(q.reshape(B, S, B_HEADS, B_HEAD_DIM)) * (B_HEAD_DIM ** -0.5)
    k = l2_normalize(k.reshape(B, S, B_HEADS, B_HEAD_DIM))
    v = v.reshape(B, S, B_HEADS, B_HEAD_DIM)
    beta = jax.nn.sigmoid(b.astype(jnp.float32))
    g = -jnp.exp(a_log.astype(jnp.float32)) * jax.nn.softplus(a.astype(jnp.float32)
                                                              + dt_bias.astype(jnp.float32))
    o = chunk_gated_delta_rule(q, k, v, g, beta)
    o = rms_norm(o, norm_w) * jax.nn.silu(z.astype(jnp.float32).reshape(B, S, B_HEADS, B_HEAD_DIM))
    return o.reshape(B, S, B_WIDTH).astype(h.dtype) @ w_out


def setup_inputs(seed: int = 0) -> dict:
    key = jax.random.key(seed)
    ks = jax.random.split(key, 16)
    f32 = jnp.float32

    def nrm(k, shape, scale):
        return jax.random.normal(k, shape, f32) * scale

    x = nrm(ks[0], (BATCH, SEQ, D_MODEL), 1.0)
    norm_g = 1.0 + nrm(ks[1], (DEPTH, 3, D_MODEL), 0.02)
    ffn_w_gate = nrm(ks[2], (DEPTH, 2, D_MODEL, D_FF), D_MODEL ** -0.5)
    ffn_w_up = nrm(ks[3], (DEPTH, 2, D_MODEL, D_FF), D_MODEL ** -0.5)
    ffn_w_down = nrm(ks[4], (DEPTH, 2, D_FF, D_MODEL), D_FF ** -0.5)
    rel_bias = nrm(ks[5], (NUM_BUCKETS, A_GROUPS * A_HEADS), 0.5)
    a_w_in = nrm(ks[6], (N_A_LAYERS, D_MODEL, A_IN_WIDTH), D_MODEL ** -0.5)
    a_w_out = nrm(ks[7], (N_A_LAYERS, A_GROUP_WIDTH, D_MODEL), A_GROUP_WIDTH ** -0.5)
    b_w_in = nrm(ks[8], (N_B_LAYERS, D_MODEL, B_IN_WIDTH), D_MODEL ** -0.5)
    b_conv_w = nrm(ks[9], (N_B_LAYERS, B_CONV, 3 * B_WIDTH), B_CONV ** -0.5)
    b_a_log = jnp.log(jax.random.uniform(ks[10], (N_B_LAYERS, B_HEADS), f32, 1.0, 16.0))
    dt = jnp.exp(jax.random.uniform(ks[11], (N_B_LAYERS, B_HEADS), f32,
                                    math.log(1e-3), math.log(1e-1)))
    b_dt_bias = dt + jnp.log(-jnp.expm1(-dt))
    b_norm_w = 1.0 + nrm(ks[12], (N_B_LAYERS, B_HEAD_DIM), 0.02)
    b_w_out = nrm(ks[13], (N_B_LAYERS, B_WIDTH, D_MODEL), B_WIDTH ** -0.5)
    final_g = 1.0 + nrm(ks[14], (D_MODEL,), 0.02)
    return {"x": x, "norm_g": norm_g, "ffn_w_gate": ffn_w_gate, "ffn_w_up": ffn_w_up,
            "ffn_w_down": ffn_w_down, "rel_bias": rel_bias, "a_w_in": a_w_in,
            "a_w_out": a_w_out, "b_w_in": b_w_in, "b_conv_w": b_conv_w,
            "b_a_log": b_a_log, "b_dt_bias": b_dt_bias, "b_norm_w": b_norm_w,
            "b_w_out": b_w_out, "final_g": final_g}


def reference(x, norm_g, ffn_w_gate, ffn_w_up, ffn_w_down, rel_bias, a_w_in, a_w_out,
              b_w_in, b_conv_w, b_a_log, b_dt_bias, b_norm_w, b_w_out, final_g):
    for i in range(DEPTH):
        j = i // N_MIXERS
        x = x + MACARON_WEIGHT * swiglu(rms_norm(x, norm_g[i, 0]),
                                        ffn_w_gate[i, 0], ffn_w_up[i, 0], ffn_w_down[i, 0])
        h = rms_norm(x, norm_g[i, 1])
        if i % N_MIXERS == 0:
            x = x + dilated_attention_mixer(h, a_w_in[j], rel_bias, a_w_out[j])
        else:
            x = x + gated_deltanet_mixer(h, b_w_in[j], b_conv_w[j], b_a_log[j], b_dt_bias[j],
                                         b_norm_w[j], b_w_out[j])
        x = x + MACARON_WEIGHT * swiglu(rms_norm(x, norm_g[i, 2]),
                                        ffn_w_gate[i, 1], ffn_w_up[i, 1], ffn_w_down[i, 1])
    return rms_norm(x, final_g)
```

```python
import numpy as np
import concourse.bass as bass
import concourse.mybir as mybir
from concourse.bass_utils import run_bass_kernel_spmd

F32 = mybir.dt.float32
BF16 = mybir.dt.bfloat16
AF = mybir.ActivationFunctionType
ALU = mybir.AluOpType

D = 1024
S = 2048
DFF = 2816
KC = D // 128
FC = DFF // 128
WIN = 512
NW = S // WIN
DEPTH = 4
EPS = 1e-6
N_CORES = 8
MIXB_STOP = None
OP_LIMIT = None


class Src:
    def __init__(self, sem, step):
        self.sem = sem
        self.step = step
        self.val = 0


class Reg:
    __slots__ = ("name", "w", "r")

    def __init__(self, name=""):
        self.name = name
        self.w = None
        self.r = {}


class Eng:
    def __init__(self, name, src):
        self.name = name
        self.src = src
        self.seen = {}
        self.ops = []


class Prog:
    def __init__(self, nc, stack):
        self.nc = nc
        self.stack = stack
        self.engs = {}
        for n in ("pe", "act", "dve", "pool", "sp"):
            sem = stack.enter_context(nc.semaphore("sem_" + n))
            self.engs[n] = Eng(n, Src(sem, 1))
        self.n_dma_src = 0

    def dma_src(self):
        self.n_dma_src += 1
        sem = self.stack.enter_context(self.nc.semaphore("dsem%d" % self.n_dma_src))
        return Src(sem, 16)

    def _waits(self, eng, reads, writes):
        deps = {}
        for r in reads:
            if r.w is not None:
                s, v = r.w
                if deps.get(s, 0) < v:
                    deps[s] = v
        for w in writes:
            if w.w is not None:
                s, v = w.w
                if deps.get(s, 0) < v:
                    deps[s] = v
            for s, v in w.r.items():
                if deps.get(s, 0) < v:
                    deps[s] = v
        for s, v in deps.items():
            if s is eng.src and eng.name == "pe":
                continue
            if eng.seen.get(s, 0) >= v:
                continue
            eng.seen[s] = v
            eng.ops.append(("wait", s.sem, v))

    def op(self, eng, fn, reads=(), writes=()):
        if getattr(self, "count_active", False):
            self.cnt = getattr(self, "cnt", 0) + 1
            if OP_LIMIT is not None and self.cnt > OP_LIMIT:
                return
        e = self.engs[eng]
        self._waits(e, reads, writes)
        e.src.val += 1
        v = e.src.val
        e.ops.append(("ins", fn, e.src.sem, 1))
        for r in reads:
            if r.r.get(e.src, 0) < v:
                r.r[e.src] = v
        for w in writes:
            w.w = (e.src, v)
            w.r = {}

    def dma(self, eng, src, out, in_, reads=(), writes=(), **kw):
        e = self.engs[eng]
        self._waits(e, reads, writes)
        src.val += 16
        v = src.val
        e.ops.append(("ins", lambda q: q.dma_start(out=out, in_=in_, **kw), src.sem, 16))
        for r in reads:
            if r.r.get(src, 0) < v:
                r.r[src] = v
        for w in writes:
            w.w = (src, v)
            w.r = {}

    def wait_all(self, eng, regs):
        e = self.engs[eng]
        self._waits(e, (), regs)

    def replay(self, block):
        handles = {"pe": block.tensor, "act": block.scalar, "dve": block.vector,
                   "pool": block.gpsimd, "sp": block.sync}
        for n, e in self.engs.items():
            ops = e.ops

            def body(q, ops=ops):
                for o in ops:
                    if o[0] == "wait":
                        q.wait_ge(o[1], o[2])
                    else:
                        o[1](q).then_inc(o[2], o[3])
            handles[n](body)


class Arena:
    def __init__(self, ap_bf16, nbytes):
        self.ap = ap_bf16
        self.nbytes = nbytes
        self.off = 0

    def mark(self):
        return self.off

    def reset(self, m=0):
        self.off = m

    def alloc(self, shape, dtype, parts=128):
        n = 1
        for s in shape:
            n *= s
        esz = 4 if dtype == F32 else 2
        nb = (n * esz + 31) // 32 * 32
        assert self.off + nb <= self.nbytes, ("arena overflow", self.off, nb, self.nbytes)
        a = self.ap[0:parts, self.off // 2:(self.off + n * esz) // 2]
        self.off += nb
        if dtype == F32:
            a = a.bitcast(F32)
        if len(shape) == 2:
            a = a.rearrange("p (a b) -> p a b", a=shape[0])
        elif len(shape) == 3:
            a = a.rearrange("p (a b c) -> p a b c", a=shape[0], b=shape[1])
        elif len(shape) == 4:
            a = a.rearrange("p (a b c d) -> p a b c d", a=shape[0], b=shape[1], c=shape[2])
        return a


def I_act(out, in_, func, **kw):
    return lambda q: q.activation(out=out, in_=in_, func=func, **kw)


def I_tt(out, in0, in1, op):
    return lambda q: q.tensor_tensor(out=out, in0=in0, in1=in1, op=op)


def I_stt(out, in0, scalar, in1, op0, op1):
    return lambda q: q.scalar_tensor_tensor(out=out, in0=in0, scalar=scalar, in1=in1, op0=op0, op1=op1)


def I_ts(out, in0, s1, s2, op0, op1=None):
    if op1 is None:
        return lambda q: q.tensor_scalar(out=out, in0=in0, scalar1=s1, scalar2=None, op0=op0)
    return lambda q: q.tensor_scalar(out=out, in0=in0, scalar1=s1, scalar2=s2, op0=op0, op1=op1)


def I_recip(out, in_):
    return lambda q: q.reciprocal(out=out, in_=in_)


def I_copy(out, in_):
    return lambda q: q.tensor_copy(out=out, in_=in_)


def I_memset(out, val):
    return lambda q: q.memset(out, val)


def I_mm(out, lhsT, rhs, start=True, stop=True):
    return lambda q: q.matmul(out, lhsT=lhsT, rhs=rhs, start=start, stop=stop)


def I_mmgroup(out, pairs):
    pairs = list(pairs)

    def f(q):
        ins = None
        n = len(pairs)
        for i, (l, r) in enumerate(pairs):
            ins = q.matmul(out, lhsT=l, rhs=r, start=(i == 0), stop=(i == n - 1))
        return ins
    return f


def I_multi(items):
    items = list(items)

    def f(q):
        ins = None
        for (o, l, r) in items:
            ins = q.matmul(o, lhsT=l, rhs=r, start=True, stop=True)
        return ins
    return f


def I_multi_tr(items):
    items = list(items)

    def f(q):
        ins = None
        for (o, i, ident) in items:
            ins = q.transpose(o, i, ident)
        return ins
    return f


def I_transpose(out, in_, ident):
    return lambda q: q.transpose(out, in_, ident)


class Builder:
    def __init__(self, n_seq, phases):
        from contextlib import ExitStack
        self.n_seq = n_seq
        self.phases = phases
        self.stack = ExitStack()
        nc = self.nc = bass.Bass("TRN2", target_bir_lowering=False)
        st = self.stack
        self.P = Prog(nc, st)
        dt = nc.dram_tensor
        self.d_xT = dt("xT", [n_seq, KC, 128, S], F32, kind="ExternalInput").ap()
        self.d_out = dt("outT", [n_seq, KC, 128, S], F32, kind="ExternalOutput").ap()
        self.d_normg = dt("normg", [128, 13 * KC], F32, kind="ExternalInput").ap()
        self.d_wgu = dt("wgu", [2 * DEPTH, FC, 128, 2 * KC * 128], F32, kind="ExternalInput").ap()
        self.d_wd = dt("wd", [2 * DEPTH, KC, 128, FC * 128], F32, kind="ExternalInput").ap()
        self.d_awin = dt("awin", [2, 4, 128, KC, 3 * 384], F32, kind="ExternalInput").ap()
        self.d_awout = dt("awout", [2, 128, 4, D], F32, kind="ExternalInput").ap()
        self.d_bt = dt("bt", [4, 128, 6, 256], F32, kind="ExternalInput").ap()
        self.d_bwin = dt("bwin", [2, 8, 128, KC, 512], F32, kind="ExternalInput").ap()
        self.d_wab = dt("wab", [2, 128, KC, 64], F32, kind="ExternalInput").ap()
        self.d_wabt = dt("wabt", [2, 128, KC, 16], F32, kind="ExternalInput").ap()
        self.d_bconv = dt("bconv", [2, 8, 128, 12], F32, kind="ExternalInput").ap()
        self.d_bsmall = dt("bsmall", [2, 128, 259], F32, kind="ExternalInput").ap()
        self.d_bwout = dt("bwout", [2, 8, 128, D], F32, kind="ExternalInput").ap()
        self.d_bconst = dt("bconst", [128, 13 * 128], F32, kind="ExternalInput").ap()
        sb = lambda name, shape, dtype: st.enter_context(nc.sbuf_tensor(name, shape, dtype))
        self.xT = sb("xTs", [128, KC, S], F32)
        self.hT = sb("hTs", [128, KC, S], BF16)
        self.normg = sb("normg_s", [128, 13 * KC], F32)
        self.ones_bf = sb("ones_bf", [128, 128], BF16)
        self.n_sq = [sb("n_sq%d" % i, [128, WIN], BF16) for i in range(2)]
        self.n_rt = sb("n_rt", [128, WIN], F32)
        self.n_rstd = sb("n_rstd", [128, WIN], F32)
        self.r_nsq = [Reg("sq0"), Reg("sq1")]
        self.r_nrt, self.r_nrstd = Reg("rt"), Reg("rstd")
        ARENA = (nc.sbuf_bytes_remaining - 2048) // 64 * 64
        self.arena_t = sb("arena", [128, ARENA // 2], BF16)
        self.arena = Arena(self.arena_t, ARENA)
        self.banks = [st.enter_context(nc.psum_tensor("bank%d" % i, [128, 512], F32)) for i in range(8)]
        self.bank_regs = [Reg("bank%d" % i) for i in range(8)]
        self.r_x = [[Reg("x%d_%d" % (k, w)) for w in range(NW)] for k in range(KC)]
        self.r_h = [[Reg("h%d_%d" % (k, w)) for w in range(NW)] for k in range(KC)]
        self.r_const = Reg("const")
        self.s_io = [self.P.dma_src() for _ in range(KC)]
        self.s_const = self.P.dma_src()

    def emit_consts(self):
        P = self.P
        P.dma("sp", self.s_const, self.normg[:], self.d_normg[:, :], writes=[self.r_const])
        ones = self.ones_bf
        P.op("dve", I_memset(ones[:], 1.0), writes=[self.r_const])

    def emit_load_x(self, s):
        P = self.P
        for k in range(KC):
            P.dma("sp", self.s_io[k], self.xT[:, k, :], self.d_xT[s, k, :, :], writes=self.r_x[k])

    def emit_store_x(self, s, src_is_h=False):
        P = self.P
        for k in range(KC):
            P.dma("sp", self.s_io[k], self.d_out[s, k, :, :], self.xT[:, k, :], reads=self.r_x[k])

    def emit_rmsnorm(self, norm_idx, windows, out_f32_inplace=False):
        P = self.P
        sq = [t[:] for t in self.n_sq]
        r_sq = self.r_nsq
        rt, rstd = self.n_rt[:], self.n_rstd[:]
        r_rt, r_rstd = self.r_nrt, self.r_nrstd
        xT, hT, ones, normg = self.xT, self.hT, self.ones_bf, self.normg
        bank = self.banks[7]
        r_bank = self.bank_regs[7]
        for w in windows:
            ws = slice(w * WIN, (w + 1) * WIN)
            for k in range(KC):
                b = k % 2
                P.op("act", I_act(sq[b], xT[:, k, ws], AF.Square),
                     reads=[self.r_x[k][w]], writes=[r_sq[b]])
                P.op("pe", I_mm(bank[:], ones[:], sq[b], start=(k == 0), stop=(k == KC - 1)),
                     reads=[r_sq[b], self.r_const], writes=[r_bank])
            P.op("act", I_act(rt, bank[:], AF.Ln, bias=EPS, scale=1.0 / D),
                 reads=[r_bank], writes=[r_rt])
            P.op("act", I_act(rstd, rt, AF.Exp, scale=-0.5), reads=[r_rt], writes=[r_rstd])
            for k in range(KC):
                g_ap = normg[:, norm_idx * KC + k:norm_idx * KC + k + 1]
                if out_f32_inplace:
                    P.op("dve", I_stt(xT[:, k, ws], xT[:, k, ws], g_ap, rstd, ALU.mult, ALU.mult),
                         reads=[r_rstd, self.r_const], writes=[self.r_x[k][w]])
                else:
                    P.op("dve", I_stt(hT[:, k, ws], xT[:, k, ws], g_ap, rstd, ALU.mult, ALU.mult),
                         reads=[self.r_x[k][w], r_rstd, self.r_const], writes=[self.r_h[k][w]])

    def emit_ffn(self, ffn_idx, norm_idx):
        P = self.P
        ar = self.arena
        xT, hT = self.xT, self.hT
        HW = 2
        for half in range(NW // HW):
            wins = [half * HW + i for i in range(HW)]
            self.emit_rmsnorm(norm_idx, wins)
            m = ar.mark()
            aT = ar.alloc([FC, HW * WIN], BF16)
            r_a = [[Reg("a") for _ in range(HW)] for _ in range(FC)]
            NGU = 4
            wgu = [ar.alloc([2, KC, 128], BF16) for _ in range(NGU)]
            r_wgu = [Reg("wgu%d" % i) for i in range(NGU)]
            s_wgu = self.s_wgu
            NWD = 3
            wd = [ar.alloc([FC, 128], BF16) for _ in range(NWD)]
            r_wd = [Reg("wd%d" % i) for i in range(NWD)]
            s_wd = self.s_wd
            sil = [ar.alloc([WIN], F32) for _ in range(2)]
            r_sil = [Reg("sil0"), Reg("sil1")]
            it = 0

            def load_wgu(f):
                sl = f % NGU
                P.dma("pool", s_wgu[sl], wgu[sl].rearrange("p a b c -> p (a b c)"),
                      self.d_wgu[ffn_idx, f, :, :], writes=[r_wgu[sl]])

            def load_wd(d):
                sl = d % NWD
                hf = FC // 2
                P.dma("pool", s_wd[sl], wd[sl][:, 0:hf, :].rearrange("p a b -> p (a b)"),
                      self.d_wd[ffn_idx, d, :, 0:hf * 128], writes=[r_wd[sl]])
                P.dma("pool", s_wd[sl], wd[sl][:, hf:FC, :].rearrange("p a b -> p (a b)"),
                      self.d_wd[ffn_idx, d, :, hf * 128:FC * 128], writes=[r_wd[sl]])

            for f in range(NGU):
                load_wgu(f)
            for d in range(NWD):
                load_wd(d)
            for f in range(FC):
                sl = f % NGU
                for wi, w in enumerate(wins):
                    ws = slice(w * WIN, (w + 1) * WIN)
                    pb = it % 2
                    it += 1
                    bg, bu = self.banks[pb * 2], self.banks[pb * 2 + 1]
                    rg, ru = self.bank_regs[pb * 2], self.bank_regs[pb * 2 + 1]
                    rd = [r_wgu[sl]] + [self.r_h[k][w] for k in range(KC)]
                    P.op("pe", I_mmgroup(bg[:], [(wgu[sl][:, 0, k, :], hT[:, k, ws]) for k in range(KC)]),
                         reads=rd, writes=[rg])
                    P.op("pe", I_mmgroup(bu[:], [(wgu[sl][:, 1, k, :], hT[:, k, ws]) for k in range(KC)]),
                         reads=rd, writes=[ru])
                    sb_ = it % 2
                    P.op("act", I_act(sil[sb_], bg[:], AF.Silu), reads=[rg], writes=[r_sil[sb_]])
                    P.op("dve", I_tt(aT[:, f, wi * WIN:(wi + 1) * WIN], sil[sb_], bu[:], ALU.mult),
                         reads=[r_sil[sb_], ru], writes=[r_a[f][wi]])
                if f + NGU < FC:
                    load_wgu(f + NGU)
            for d in range(KC):
                sl = d % NWD
                for wi, w in enumerate(wins):
                    ws = slice(w * WIN, (w + 1) * WIN)
                    pb = it % 2
                    it += 1
                    by, ry = self.banks[4 + pb], self.bank_regs[4 + pb]
                    P.op("pe", I_mmgroup(by[:], [(wd[sl][:, f, :], aT[:, f, wi * WIN:(wi + 1) * WIN]) for f in range(FC)]),
                         reads=[r_wd[sl]] + [r_a[f][wi] for f in range(FC)], writes=[ry])
                    P.op("dve", I_stt(xT[:, d, ws], by[:], 0.5, xT[:, d, ws], ALU.mult, ALU.add),
                         reads=[ry, self.r_x[d][w]], writes=[self.r_x[d][w]])
                if d + NWD < KC:
                    load_wd(d + NWD)
            self.phase_barrier([r_wgu, r_wd, r_sil, [r for rr in r_a for r in rr]])
            ar.reset(m)

    def emit_mixa(self, j, norm_idx):
        P = self.P
        ar = self.arena
        xT, hT = self.xT, self.hT
        banks, bregs = self.banks, self.bank_regs
        self.emit_rmsnorm(norm_idx, range(NW))
        r_hall = [self.r_h[k][w] for k in range(KC) for w in range(NW)]
        m = ar.mark()
        NWB = 3
        wbuf = [ar.alloc([KC, 384], BF16) for _ in range(NWB)]
        r_wbuf = [Reg("awb%d" % i) for i in range(NWB)]
        wout = ar.alloc([4, D], BF16)
        r_wout = Reg("awout")
        ebuf = [ar.alloc([6, 256], F32) for _ in range(2)]
        r_ebuf = [Reg("eb0"), Reg("eb1")]
        qT = [ar.alloc([S], BF16) for _ in range(2)]
        kT = [ar.alloc([S], BF16) for _ in range(2)]
        r_qk = [Reg("qk0"), Reg("qk1")]
        vbuf = [ar.alloc([16, 128], BF16) for _ in range(2)]
        r_v = [Reg("v0"), Reg("v1")]
        accn = ar.alloc([S], F32)
        accd = ar.alloc([S], F32)
        r_accn, r_accd = Reg("accn"), Reg("accd")
        oT = ar.alloc([S], BF16)
        r_oT = Reg("oT")
        NE = 4
        ebf = [ar.alloc([256], F32) for _ in range(NE)]
        r_ebf = [Reg("E%d" % i) for i in range(NE)]
        NPT = 8
        pT = [ar.alloc([256], BF16) for _ in range(NPT)]
        r_pT = [Reg("pT%d" % i) for i in range(NPT)]
        ones = self.ones_bf

        P.dma("pool", self.s_awout, wout, self.d_awout[j, :, :, :], writes=[r_wout])
        wl = [(pair, g) for pair in range(4) for g in range(3)]

        def load_w(i):
            pair, g = wl[i]
            sl = i % NWB
            P.dma("pool", self.s_awb[sl], wbuf[sl], self.d_awin[j, pair, :, :, g * 384:(g + 1) * 384],
                  writes=[r_wbuf[sl]])

        def load_eb(pair):
            sl = pair % 2
            P.dma("sp", self.s_eb[sl], ebuf[sl], self.d_bt[pair, :, :, :], writes=[r_ebuf[sl]])
            P.op("act", I_act(ebuf[sl], ebuf[sl], AF.Exp), reads=[r_ebuf[sl]], writes=[r_ebuf[sl]])

        for i in range(NWB):
            load_w(i)
        load_eb(0)
        pj = 0
        sj = 0
        ej = 0
        pj_t = 0
        nj = 0
        for i, (pair, g) in enumerate(wl):
            dil = (1, 4, 16)[g]
            L = S // dil
            wsl = i % NWB
            qs = i % 2
            if g == 0 and pair + 1 < 4:
                load_eb(pair + 1)
            eb = ebuf[pair % 2]
            r_eb = r_ebuf[pair % 2]
            for which, dst in ((0, qT[qs]), (1, kT[qs])):
                for w in range(NW):
                    ws = slice(w * WIN, (w + 1) * WIN)
                    b = pj % 2
                    pj += 1
                    P.op("pe", I_mmgroup(banks[b][:], [(wbuf[wsl][:, k, which * 128:(which + 1) * 128], hT[:, k, ws])
                                                       for k in range(KC)]),
                         reads=[r_wbuf[wsl]] + [self.r_h[k][w] for k in range(KC)], writes=[bregs[b]])
                    nl = WIN // dil
                    if dil == 1:
                        o_ap = dst[:, ws]
                        i_ap = banks[b][:]
                    else:
                        o_ap = dst.rearrange("p (c l) -> p c l", c=dil)[:, :, w * nl:(w + 1) * nl]
                        i_ap = banks[b][:].rearrange("p (l c) -> p c l", c=dil)
                    if which == 0:
                        P.op("act", I_act(o_ap, i_ap, AF.Copy, scale=0.125), reads=[bregs[b]], writes=[r_qk[qs]])
                    else:
                        P.op("dve", I_copy(o_ap, i_ap), reads=[bregs[b]], writes=[r_qk[qs]])
            nbl = L // 128
            for bq in range(4):
                b = pj % 2
                pj += 1
                for bi in range(4):
                    blk = bq * 4 + bi
                    c, n = blk // nbl, blk % nbl
                    t0 = n * 128 * dil + c
                    tsl = slice(t0, t0 + 127 * dil + 1, dil)
                    tw = set(t // WIN for t in (t0, t0 + 127 * dil))
                    P.op("pe", I_mmgroup(banks[b][:, bi * 128:(bi + 1) * 128],
                                         [(hT[:, k, tsl], wbuf[wsl][:, k, 256:384]) for k in range(KC)]),
                         reads=[r_wbuf[wsl]] + [self.r_h[k][w] for k in range(KC) for w in tw],
                         writes=[bregs[b]])
                P.op("dve", I_copy(vbuf[qs][:, bq * 4:(bq + 1) * 4, :].rearrange("p a b -> p (a b)"), banks[b][:]),
                     reads=[bregs[b]], writes=[r_v[qs]])
            if i + NWB < len(wl):
                load_w(i + NWB)
            groups = []
            for c in range(dil):
                for qg in range(max(1, nbl // 4)):
                    if nbl == 1:
                        if c % 4 != 0:
                            continue
                        groups.append(([(c + cc, 0) for cc in range(4)], c, qg))
                    else:
                        groups.append(([(c, qg * 4 + qq) for qq in range(4)], c, qg))
            seq = []
            for gi_, (items, c, qg) in enumerate(groups):
                for hp in range(2):
                    for col, (cc, qb) in enumerate(items):
                        seq.append((gi_, hp, col, cc, qb))
            tile_of = {}
            uses_left = {}
            slot_owner = [None] * NPT

            def need_of(cc, qb):
                return [kb for kb in (qb - 1, qb) if kb >= 0]

            def n_uses(cc, kb):
                return (1 if kb < nbl else 0) + (1 if kb + 1 < nbl else 0)

            def stage1(hp, cc, kb):
                nonlocal sj, ej, pj_t
                key = (hp, cc, kb)
                if key in tile_of:
                    return
                ps = slice(64 * hp, 64 * hp + 64)
                base = cc * L
                nq = min(256, L - kb * 128)
                sb_ = 2 + sj % 2
                sj += 1
                P.op("pe", I_mm(banks[sb_][:, 0:nq],
                                kT[qs][ps, base + kb * 128: base + kb * 128 + 128],
                                qT[qs][ps, base + kb * 128: base + kb * 128 + nq]),
                     reads=[r_qk[qs]], writes=[bregs[sb_]])
                e_i = ej % NE
                ej += 1
                P.op("act", I_act(ebf[e_i][:, 0:nq], banks[sb_][:, 0:nq], AF.Exp),
                     reads=[bregs[sb_]], writes=[r_ebf[e_i]])
                p_i = pj_t % NPT
                pj_t += 1
                assert slot_owner[p_i] is None or uses_left[slot_owner[p_i]] == 0, "pT ring too small"
                slot_owner[p_i] = key
                uses_left[key] = n_uses(cc, kb)
                P.op("pool", I_tt(pT[p_i][:, 0:nq], ebf[e_i][:, 0:nq], eb[:, g * 2 + hp, 0:nq], ALU.mult),
                     reads=[r_ebf[e_i], r_eb], writes=[r_pT[p_i]])
                tile_of[key] = p_i

            LA = 2
            gbank = {}
            for i_, (gi_, hp, col, cc, qb) in enumerate(seq):
                for la in range(i_, min(i_ + LA + 1, len(seq))):
                    _, hp2, _, cc2, qb2 = seq[la]
                    for kb in need_of(cc2, qb2):
                        stage1(hp2, cc2, kb)
                if gi_ not in gbank:
                    gbank[gi_] = nj % 2
                    nj += 1
                bsel = gbank[gi_]
                bn, bd = banks[4 + bsel], banks[6 + bsel]
                rbn, rbd = bregs[4 + bsel], bregs[6 + bsel]
                ps = slice(64 * hp, 64 * hp + 64)
                pairs_n, pairs_d, rd = [], [], [r_v[qs], self.r_const]
                for kb in need_of(cc, qb):
                    key = (hp, cc, kb)
                    p_i = tile_of[key]
                    assert slot_owner[p_i] == key
                    uses_left[key] -= 1
                    off = (qb - kb) * 128
                    blk = cc * nbl + kb
                    pairs_n.append((vbuf[qs][:, blk, 64 * hp:64 * hp + 64], pT[p_i][:, off:off + 128]))
                    pairs_d.append((ones[:, 0:64], pT[p_i][:, off:off + 128]))
                    rd.append(r_pT[p_i])
                P.op("pe", I_mmgroup(bn[ps, col * 128:(col + 1) * 128], pairs_n), reads=rd, writes=[rbn])
                P.op("pe", I_mmgroup(bd[ps, col * 128:(col + 1) * 128], pairs_d), reads=rd, writes=[rbd])
                last_of_group = (i_ + 1 == len(seq)) or (seq[i_ + 1][0] != gi_)
                if last_of_group:
                    items, c, qg = groups[gi_]
                    if dil == 1:
                        an = accn[:, qg * 512:(qg + 1) * 512]
                        ad = accd[:, qg * 512:(qg + 1) * 512]
                        sn, sd = bn[:], bd[:]
                    elif nbl == 1:
                        an = accn.rearrange("p (l c) -> p c l", c=dil)[:, c:c + 4, :]
                        ad = accd.rearrange("p (l c) -> p c l", c=dil)[:, c:c + 4, :]
                        sn = bn[:].rearrange("p (c l) -> p c l", c=4)
                        sd = bd[:].rearrange("p (c l) -> p c l", c=4)
                    else:
                        an = accn.rearrange("p (l c) -> p c l", c=dil)[:, c, qg * 512:(qg + 1) * 512]
                        ad = accd.rearrange("p (l c) -> p c l", c=dil)[:, c, qg * 512:(qg + 1) * 512]
                        sn, sd = bn[:], bd[:]
                    if g == 0:
                        P.op("dve", I_copy(an, sn), reads=[rbn], writes=[r_accn])
                        P.op("act", I_act(ad, sd, AF.Copy), reads=[rbd], writes=[r_accd])
                    else:
                        P.op("dve", I_tt(an, sn, an, ALU.add), reads=[rbn, r_accn], writes=[r_accn])
                        P.op("dve", I_tt(ad, sd, ad, ALU.add), reads=[rbd, r_accd], writes=[r_accd])
            if g == 2:
                P.op("act", I_act(accd, accd, AF.Ln), reads=[r_accd], writes=[r_accd])
                P.op("act", I_act(accd, accd, AF.Exp, scale=-1.0), reads=[r_accd], writes=[r_accd])
                P.op("dve", I_tt(oT, accn, accd, ALU.mult), reads=[r_accn, r_accd], writes=[r_oT])
                for dch in range(KC):
                    for w in range(NW):
                        ws = slice(w * WIN, (w + 1) * WIN)
                        b = pj % 2
                        pj += 1
                        P.op("pe", I_mm(banks[b][:], wout[:, pair, dch * 128:(dch + 1) * 128], oT[:, ws]),
                             reads=[r_wout, r_oT], writes=[bregs[b]])
                        P.op("dve", I_tt(xT[:, dch, ws], banks[b][:], xT[:, dch, ws], ALU.add),
                             reads=[bregs[b], self.r_x[dch][w]], writes=[self.r_x[dch][w]])
        self.phase_barrier([r_wbuf, [r_wout], r_ebuf, r_qk, r_v, [r_accn, r_accd, r_oT], r_ebf, r_pT])
        ar.reset(m)

    def emit_mixb(self, j, norm_idx):
        P = self.P
        ar = self.arena
        xT, hT = self.xT, self.hT
        banks, bregs = self.banks, self.bank_regs
        self.emit_rmsnorm(norm_idx, range(NW))
        m = ar.mark()
        NB = S // 128
        cst = ar.alloc([13 * 128], F32)
        r_cst = Reg("bcst")
        TRI, SELL, IDN, MUI, MUS = [cst[:, i * 128:(i + 1) * 128] for i in range(5)]
        SEL = cst[:, 5 * 128:13 * 128].rearrange("p (h m) -> p h m", h=8)
        idb = ar.alloc([128], BF16)
        small = ar.alloc([3 + 256], F32)
        r_small = Reg("bsmall")
        dtb_col, alog_col, normw_col = small[:, 0:1], small[:, 1:2], small[:, 2:3]
        dtb_rep, alog_rep = small[:, 3:131], small[:, 131:259]
        nega = ar.alloc([1 + 128], F32)
        r_nega = Reg("nega")
        abT = ar.alloc([S], F32)
        r_abT = Reg("abT")
        tok = ar.alloc([4, 128], F32)
        gc_tok, beta_tok, bg_tok, kt_tok = [tok[:, i, :] for i in range(4)]
        r_tok = Reg("tok")
        wab = ar.alloc([KC, 64], BF16)
        wabt = ar.alloc([KC, 16], BF16)
        r_wab = Reg("wab")
        wh = [ar.alloc([KC, 256], BF16) for _ in range(3)]
        r_wh = [Reg("wh%d" % i) for i in range(3)]
        wo = [ar.alloc([D], BF16) for _ in range(2)]
        r_wo = [Reg("wo0"), Reg("wo1")]
        hv = [ar.alloc([12], F32) for _ in range(2)]
        r_hv = [Reg("hv0"), Reg("hv1")]
        raw = ar.alloc([S + 4], F32)
        r_raw = Reg("raw")
        acc = ar.alloc([S], F32)
        r_acc = Reg("acc")
        qT = ar.alloc([S], BF16)
        kT = ar.alloc([S], BF16)
        vT = ar.alloc([S], BF16)
        qdT = ar.alloc([S], BF16)
        zsT = ar.alloc([S], BF16)
        r_q, r_k, r_v, r_qd, r_zs = Reg("q"), Reg("k"), Reg("v"), Reg("qd"), Reg("zs")
        u = ar.alloc([NB, 128], F32)
        wT = ar.alloc([S], BF16)
        attnT = ar.alloc([NB, 128], BF16)
        ktail = ar.alloc([NB, 2, 128], BF16)
        r_u = [Reg("u%d" % b) for b in range(NB)]
        r_wT = [Reg("wT%d" % b) for b in range(NB)]
        r_at = [Reg("at%d" % b) for b in range(NB)]
        r_kt = [Reg("kt%d" % b) for b in range(NB)]
        r_qdb = [Reg("qd%d" % b) for b in range(NB)]
        cdv = ar.alloc([2 * NB], F32)
        r_cdv = Reg("cdv")
        r_W = [Reg("W%d" % i) for i in range(8)]
        NT = 3
        tmp = [ar.alloc([128], F32) for _ in range(NT)]
        r_tmp = [Reg("t%d" % i) for i in range(NT)]
        tcount = [0]

        def T():
            i = tcount[0] % NT
            tcount[0] += 1
            return tmp[i], r_tmp[i]
        Sf = ar.alloc([128], F32)
        Sb = ar.alloc([128], BF16)
        r_Sf, r_Sb = Reg("Sf"), Reg("Sb")
        vnew = ar.alloc([128], BF16)
        r_vnew = Reg("vnew")
        osb = ar.alloc([WIN], F32)
        r_osb = Reg("osb")
        ogT = ar.alloc([WIN], BF16)
        r_og = Reg("og")
        zt, r_zt = osb, r_osb
        ones = self.ones_bf
        sq, r_sq = [t[:] for t in self.n_sq], self.r_nsq
        rt, rstd, r_rt, r_rstd = self.n_rt[:], self.n_rstd[:], self.r_nrt, self.r_nrstd
        place = {(4, 0): (4, 0), (4, 1): (5, 2), (4, 2): (4, 2), (4, 3): (4, 3),
                 (5, 0): (5, 0), (5, 1): (5, 1), (5, 2): (0, 0), (5, 3): (1, 0),
                 (6, 0): (6, 0), (6, 1): (6, 1), (6, 2): (2, 0), (6, 3): (3, 0),
                 (7, 0): (7, 0), (7, 1): (6, 2)}
        qb = {}
        qr = {}
        for key, (b, q) in place.items():
            qb[key] = banks[b][:, q * 128:(q + 1) * 128]
            qr[key] = bregs[b]

        def qbf(b, q):
            b, q = place[(b, q)]
            return banks[b][:].bitcast(BF16)[:, q * 256:q * 256 + 128]

        P.dma("sp", self.s_bc[0], cst, self.d_bconst[:, :], writes=[r_cst])
        P.dma("sp", self.s_bc[1], small, self.d_bsmall[j, :, :], writes=[r_small])
        P.dma("pool", self.s_bc[2], wab, self.d_wab[j, :, :, :], writes=[r_wab])
        P.dma("pool", self.s_bc[2], wabt, self.d_wabt[j, :, :, :], writes=[r_wab])
        P.op("dve", I_copy(idb, IDN), reads=[r_cst], writes=[r_cst])
        P.op("dve", I_memset(raw[:, 0:4], 0.0), writes=[r_raw])
        P.op("dve", I_memset(ktail.rearrange("p a b c -> p (a b c)"), 0.0), writes=r_kt)
        P.op("act", I_act(nega[:, 0:1], alog_col, AF.Exp), reads=[r_small], writes=[r_nega])
        P.op("act", I_act(nega[:, 1:129], alog_rep, AF.Exp), reads=[r_small], writes=[r_nega])
        P.op("dve", I_ts(nega, nega, -1.0, None, ALU.mult), reads=[r_nega], writes=[r_nega])

        def load_head(h):
            sl = h % 2
            for half in range(2):
                i = (2 * h + half) % 3
                P.dma("pool", self.s_wh[i], wh[i], self.d_bwin[j, h, :, :, half * 256:(half + 1) * 256],
                      writes=[r_wh[i]])
            P.dma("pool", self.s_wo[sl], wo[sl], self.d_bwout[j, h, :, :], writes=[r_wo[sl]])
            P.dma("sp", self.s_hv[sl], hv[sl], self.d_bconv[j, h, :, :], writes=[r_hv[sl]])

        load_head(0)
        tA, r_tA = acc, r_acc
        for w in range(NW):
            ws = slice(w * WIN, (w + 1) * WIN)
            b = w % 2
            P.op("pe", I_mmgroup(banks[b][0:64, :], [(wab[:, k, :], hT[:, k, ws]) for k in range(KC)]),
                 reads=[r_wab] + [self.r_h[k][w] for k in range(KC)], writes=[bregs[b]])
            P.op("act", I_act(tA[0:8, ws], banks[b][0:8, :], AF.Exp, bias=dtb_col[0:8, :]),
                 reads=[bregs[b], r_small], writes=[r_tA])
            P.op("act", I_act(tA[0:8, ws], tA[0:8, ws], AF.Ln, bias=1.0), reads=[r_tA], writes=[r_tA])
            P.op("dve", I_ts(abT[0:8, ws], tA[0:8, ws], nega[0:8, 0:1], None, ALU.mult),
                 reads=[r_tA, r_nega], writes=[r_abT])
            P.op("act", I_act(abT[32:40, ws], banks[b][32:40, :], AF.Exp, scale=-1.0),
                 reads=[bregs[b]], writes=[r_abT])
            P.op("dve", I_ts(abT[32:40, ws], abT[32:40, ws], 1.0, None, ALU.add), reads=[r_abT], writes=[r_abT])
            P.op("dve", I_recip(abT[32:40, ws], abT[32:40, ws]), reads=[r_abT], writes=[r_abT])
        src_, dst_ = abT, tA
        for st_ in (1, 2, 4, 8, 16, 32):
            sv = src_[0:8, :].rearrange("p (n c) -> p n c", c=64)
            dv = dst_[0:8, :].rearrange("p (n c) -> p n c", c=64)
            P.op("dve", I_tt(dv[:, :, st_:64], sv[:, :, st_:64], sv[:, :, 0:64 - st_], ALU.add),
                 reads=[r_abT, r_tA], writes=[r_abT, r_tA])
            P.op("dve", I_copy(dv[:, :, 0:st_], sv[:, :, 0:st_]), reads=[r_abT, r_tA], writes=[r_abT, r_tA])
            src_, dst_ = dst_, src_
        assert src_ is abT
        for blk in range(NB):
            P.op("pe", I_mmgroup(banks[2][:, blk * 16:(blk + 1) * 16],
                                 [(hT[:, k, blk * 128:(blk + 1) * 128], wabt[:, k, :]) for k in range(KC)]),
                 reads=[r_wab] + [self.r_h[k][blk // 4] for k in range(KC)], writes=[bregs[2]])
        abv = banks[2][:, 0:256].rearrange("p (b c) -> p b c", c=16)
        t1, r_t1 = T()
        t2, r_t2 = T()
        v3 = lambda a: a.rearrange("p (b c) -> p b c", c=8)
        P.op("dve", I_tt(v3(t1), abv[:, :, 0:8], v3(dtb_rep), ALU.add), reads=[bregs[2], r_small], writes=[r_t1])
        P.op("act", I_act(t1, t1, AF.Exp), reads=[r_t1], writes=[r_t1])
        P.op("act", I_act(t1, t1, AF.Ln, bias=1.0), reads=[r_t1], writes=[r_t1])
        P.op("dve", I_tt(t1, t1, nega[:, 1:129], ALU.mult), reads=[r_t1, r_nega], writes=[r_t1])
        P.op("act", I_act(v3(beta_tok), abv[:, :, 8:16], AF.Exp, scale=-1.0), reads=[bregs[2]], writes=[r_tok])
        P.op("dve", I_ts(beta_tok, beta_tok, 1.0, None, ALU.add), reads=[r_tok], writes=[r_tok])
        P.op("dve", I_recip(beta_tok, beta_tok), reads=[r_tok], writes=[r_tok])
        P.op("pe", I_mm(qb[(4, 0)], TRI, t1), reads=[r_cst, r_t1], writes=[qr[(4, 0)]])
        P.op("act", I_act(gc_tok, qb[(4, 0)], AF.Copy), reads=[qr[(4, 0)]], writes=[r_tok])
        P.op("pe", I_mm(qb[(4, 1)], SELL, gc_tok), reads=[r_cst, r_tok], writes=[qr[(4, 1)]])
        P.op("dve", I_tt(t2, qb[(4, 1)], gc_tok, ALU.subtract), reads=[qr[(4, 1)], r_tok], writes=[r_t2])
        P.op("act", I_act(kt_tok, t2, AF.Exp), reads=[r_t2], writes=[r_tok])
        P.op("act", I_act(t2, gc_tok, AF.Exp), reads=[r_tok], writes=[r_t2])
        P.op("dve", I_tt(bg_tok, beta_tok, t2, ALU.mult), reads=[r_tok, r_t2], writes=[r_tok])

        pj = 0
        for h in range(8):
            if MIXB_STOP == "setup" or (MIXB_STOP is not None and h >= 1):
                break
            hs = h % 2
            wq = wh[(2 * h) % 3]
            wv = wh[(2 * h + 1) % 3]
            r_wq, r_wv = r_wh[(2 * h) % 3], r_wh[(2 * h + 1) % 3]
            taps = hv[hs]
            for X in range(3):
                wsrc, r_wsrc = (wq, r_wq) if X < 2 else (wv, r_wv)
                co = (X % 2) * 128
                for w in range(NW):
                    ws = slice(w * WIN, (w + 1) * WIN)
                    b = pj % 2
                    pj += 1
                    P.op("pe", I_mmgroup(banks[b][:], [(wsrc[:, k, co:co + 128], hT[:, k, ws]) for k in range(KC)]),
                         reads=[r_wsrc] + [self.r_h[k][w] for k in range(KC)], writes=[bregs[b]])
                    P.op("act", I_act(raw[:, 4 + w * WIN:4 + (w + 1) * WIN], banks[b][:], AF.Copy),
                         reads=[bregs[b]], writes=[r_raw])
                P.op("dve", I_ts(acc, raw[:, 4:4 + S], taps[:, X * 4 + 3:X * 4 + 4], None, ALU.mult),
                     reads=[r_raw, r_hv[hs]], writes=[r_acc])
                for tp in (2, 1, 0):
                    P.op("dve", I_stt(acc, raw[:, 1 + tp:1 + tp + S], taps[:, X * 4 + tp:X * 4 + tp + 1], acc,
                                      ALU.mult, ALU.add), reads=[r_raw, r_hv[hs], r_acc], writes=[r_acc])
                if X == 2:
                    P.op("act", I_act(vT, acc, AF.Silu), reads=[r_acc], writes=[r_v])
                else:
                    dst, r_dst = (qT, r_q) if X == 0 else (kT, r_k)
                    P.op("act", I_act(acc, acc, AF.Silu), reads=[r_acc], writes=[r_acc])
                    for w in range(NW):
                        ws = slice(w * WIN, (w + 1) * WIN)
                        bq_ = w % 2
                        P.op("act", I_act(sq[bq_], acc[:, ws], AF.Square), reads=[r_acc], writes=[r_sq[bq_]])
                        b = pj % 2
                        pj += 1
                        P.op("pe", I_mm(banks[b][:], ones[:], sq[bq_]), reads=[r_sq[bq_], self.r_const],
                             writes=[bregs[b]])
                        P.op("act", I_act(rt, banks[b][:], AF.Ln, bias=EPS, scale=1.0), reads=[bregs[b]],
                             writes=[r_rt])
                        P.op("act", I_act(rstd, rt, AF.Exp, scale=-0.5), reads=[r_rt], writes=[r_rstd])
                        sc = (128.0 ** -0.5) if X == 0 else 1.0
                        P.op("dve", I_stt(dst[:, ws], acc[:, ws], sc, rstd, ALU.mult, ALU.mult),
                             reads=[r_acc, r_rstd], writes=[r_dst])
            for w in range(NW):
                ws = slice(w * WIN, (w + 1) * WIN)
                b = pj % 2
                pj += 1
                P.op("pe", I_mmgroup(banks[b][:], [(wv[:, k, 128:256], hT[:, k, ws]) for k in range(KC)]),
                     reads=[r_wv] + [self.r_h[k][w] for k in range(KC)], writes=[bregs[b]])
                P.op("act", I_act(zt, banks[b][:], AF.Silu), reads=[bregs[b]], writes=[r_zt])
                P.op("dve", I_ts(zsT[:, ws], zt, normw_col, None, ALU.mult), reads=[r_zt, r_small], writes=[r_zs])
            if h + 1 < 8:
                load_head(h + 1)
            if MIXB_STOP == "proj":
                break
            for eng_ in ("pe", "act", "dve"):
                P.wait_all(eng_, [r_raw, r_acc])
            Wt = [raw[:, 4 + i * 512:4 + (i + 1) * 512] for i in range(4)] + \
                 [acc[:, i * 512:(i + 1) * 512] for i in range(4)]
            v4 = lambda a: a.rearrange("p (b i) -> p b i", b=4)
            bmid = lambda a: a.unsqueeze(1).broadcast_to([128, 4, 128])
            for gi in range(NB // 4):
                b0 = gi * 4
                gsl = slice(b0 * 128, (b0 + 4) * 128)
                bsl = [slice((b0 + bi) * 128, (b0 + bi + 1) * 128) for bi in range(4)]
                qs_ = [slice(bi * 128, (bi + 1) * 128) for bi in range(4)]

                def tokb(t, pp=slice(0, 128)):
                    return t[pp, b0 * 8 + h:(b0 + 3) * 8 + h + 1:8].unsqueeze(2).broadcast_to([pp.stop - pp.start, 4, 128])
                W_dd, W_eg, W_gm, W_gs, W_B, W_A, W_M, W_X = Wt
                rW = r_W
                P.op("pe", I_multi([(banks[4][:, qs_[bi]], SEL[0:8, h, :], abT[0:8, bsl[bi]]) for bi in range(4)]),
                     reads=[r_cst, r_abT], writes=[bregs[4]])
                P.op("pe", I_multi([(banks[5][:, qs_[bi]], SEL[32:40, h, :], abT[32:40, bsl[bi]]) for bi in range(4)]),
                     reads=[r_cst, r_abT], writes=[bregs[5]])
                P.op("pe", I_multi([(banks[6][:, qs_[bi]], kT[:, bsl[bi]], kT[:, bsl[bi]]) for bi in range(4)]),
                     reads=[r_k], writes=[bregs[6]])
                P.op("pe", I_multi([(banks[7][:, qs_[bi]], kT[:, bsl[bi]], qT[:, bsl[bi]]) for bi in range(4)]),
                     reads=[r_k, r_q], writes=[bregs[7]])
                P.op("dve", I_tt(v4(W_dd), v4(banks[4][:]), tokb(gc_tok), ALU.subtract), reads=[bregs[4], r_tok],
                     writes=[rW[0]])
                P.op("dve", I_ts(W_dd, W_dd, 0.0, None, ALU.min), reads=[rW[0]], writes=[rW[0]])
                P.op("act", I_act(W_dd, W_dd, AF.Exp), reads=[rW[0]], writes=[rW[0]])
                P.op("act", I_act(W_eg, banks[4][:], AF.Exp), reads=[bregs[4]], writes=[rW[1]])
                P.op("dve", I_tt(v4(W_gm), v4(W_dd), bmid(MUI), ALU.mult), reads=[rW[0], r_cst], writes=[rW[2]])
                P.op("dve", I_tt(attnT[:, b0:b0 + 4, :], v4(banks[7][:]), v4(W_gm), ALU.mult),
                     reads=[bregs[7], rW[2]], writes=[r_at[b0 + bi] for bi in range(4)])
                P.op("dve", I_tt(v4(W_gs), v4(W_dd), bmid(MUS), ALU.mult), reads=[rW[0], r_cst], writes=[rW[3]])
                P.op("dve", I_tt(W_gs, banks[5][:], W_gs, ALU.mult), reads=[bregs[5], rW[3]], writes=[rW[3]])
                P.op("dve", I_tt(W_B, banks[6][:], W_gs, ALU.mult), reads=[bregs[6], rW[3]], writes=[rW[4]])
                P.op("dve", I_tt(qdT[:, gsl], qT[:, gsl], W_eg, ALU.mult), reads=[r_q, rW[1]],
                     writes=[r_qdb[b0 + bi] for bi in range(4)])
                P.op("act", I_act(cdv[:, 2 * b0:2 * b0 + 8], W_eg[:, 63:512:64], AF.Copy), reads=[rW[1]],
                     writes=[r_cdv])
                P.op("pe", I_multi_tr([(banks[0][:, qs_[bi]], W_B[:, qs_[bi]], IDN) for bi in range(4)]),
                     reads=[rW[4], r_cst], writes=[bregs[0]])
                P.op("act", I_act(W_A, banks[0][:], AF.Copy), reads=[bregs[0]], writes=[rW[5]])
                P.op("dve", I_tt(v4(W_M), bmid(IDN), v4(W_B), ALU.subtract), reads=[r_cst, rW[4]], writes=[rW[6]])
                Bc, iB, Ac, iA = W_B, 4, W_A, 5
                free = [(W_dd, 0), (W_eg, 1)]
                for lev in range(1, 6):
                    An, iAn = free.pop(0)
                    P.op("pe", I_multi([(banks[1][:, qs_[bi]], Bc[:, qs_[bi]], Ac[:, qs_[bi]]) for bi in range(4)]),
                         reads=[rW[iB], rW[iA]], writes=[bregs[1]])
                    if lev < 5:
                        Bn, iBn = free.pop(0)
                        P.op("pe", I_multi([(banks[2][:, qs_[bi]], Ac[:, qs_[bi]], Bc[:, qs_[bi]])
                                            for bi in range(4)]),
                             reads=[rW[iB], rW[iA]], writes=[bregs[2]])
                    P.op("dve", I_copy(An, banks[1][:]), reads=[bregs[1]], writes=[rW[iAn]])
                    if lev < 5:
                        P.op("act", I_act(Bn, banks[2][:], AF.Copy), reads=[bregs[2]], writes=[rW[iBn]])
                    P.op("pe", I_multi([(banks[3][:, qs_[bi]], An[:, qs_[bi]], W_M[:, qs_[bi]]) for bi in range(4)]),
                         reads=[rW[iAn], rW[6]], writes=[bregs[3]])
                    P.op("dve", I_tt(W_M, banks[3][:], W_M, ALU.add), reads=[bregs[3], rW[6]], writes=[rW[6]])
                    free.append((Ac, iA))
                    Ac, iA = An, iAn
                    if lev < 5:
                        free.append((Bc, iB))
                        Bc, iB = Bn, iBn
                ktp = banks[0][:].bitcast(BF16)[:, 0:512]
                vtp = banks[1][:].bitcast(BF16)[:, 0:512]
                P.op("pe", I_multi_tr([(ktp[:, qs_[bi]], kT[:, bsl[bi]], idb) for bi in range(4)]),
                     reads=[r_k, r_cst], writes=[bregs[0]])
                P.op("pe", I_multi_tr([(vtp[:, qs_[bi]], vT[:, bsl[bi]], idb) for bi in range(4)]),
                     reads=[r_v, r_cst], writes=[bregs[1]])
                W_rw, W_ru = W_gm, W_gs
                P.op("dve", I_tt(v4(W_rw), v4(ktp), tokb(bg_tok), ALU.mult), reads=[bregs[0], r_tok], writes=[rW[2]])
                P.op("dve", I_tt(v4(W_ru), v4(vtp), tokb(beta_tok), ALU.mult), reads=[bregs[1], r_tok], writes=[rW[3]])
                for hf in range(2):
                    pp = slice(64 * hf, 64 * hf + 64)
                    P.op("dve", I_tt(ktail[pp, b0:b0 + 4, hf, :], v4(ktp)[pp], tokb(kt_tok, pp), ALU.mult),
                         reads=[bregs[0], r_tok], writes=[r_kt[b0 + bi] for bi in range(4)])
                P.op("pe", I_multi([(banks[2][:, qs_[bi]], W_M[:, qs_[bi]], W_ru[:, qs_[bi]]) for bi in range(4)]),
                     reads=[rW[6], rW[3]], writes=[bregs[2]])
                P.op("act", I_act(u[:, b0:b0 + 4, :], v4(banks[2][:]), AF.Copy), reads=[bregs[2]],
                     writes=[r_u[b0 + bi] for bi in range(4)])
                P.op("pe", I_multi([(banks[3][:, qs_[bi]], W_rw[:, qs_[bi]], W_M[:, qs_[bi]]) for bi in range(4)]),
                     reads=[rW[6], rW[2]], writes=[bregs[3]])
                P.op("dve", I_copy(wT[:, gsl], banks[3][:]), reads=[bregs[3]],
                     writes=[r_wT[b0 + bi] for bi in range(4)])
            for eng_ in ("pe", "act", "dve"):
                P.wait_all(eng_, r_W)
            P.count_active = False
            if MIXB_STOP in ("prep", "prep1"):
                break
            P.op("dve", I_memset(Sf, 0.0), writes=[r_Sf])
            P.op("dve", I_memset(Sb, 0.0), writes=[r_Sb])
            P.op("dve", I_memset(vnew, 0.0), writes=[r_vnew])
            for n in range(2 * NB):
                blk, half = n // 2, n % 2
                ps = slice(64 * half, 64 * half + 64)
                cs = slice(n * 64, (n + 1) * 64)
                w = n // 8
                ob, r_ob = banks[2 + (w % 2)], bregs[2 + (w % 2)]
                bs = slice(blk * 128, (blk + 1) * 128)
                P.op("pe", I_mm(banks[7][:, 0:128], wT[:, bs], Sb), reads=[r_wT[blk], r_Sb], writes=[qr[(7, 0)]])
                P.op("dve", I_tt(vnew[ps, :], u[ps, blk, :], banks[7][ps, 0:128], ALU.subtract),
                     reads=[r_u[blk], qr[(7, 0)]], writes=[r_vnew])
                oc = (n % 8) * 64
                P.op("pe", I_mm(qb[(7, 1)], ktail[:, blk, half, :], vnew[:, :]), reads=[r_kt[blk], r_vnew],
                     writes=[qr[(7, 1)]])
                P.op("pe", I_mmgroup(ob[:, oc:oc + 64], [(Sb, qdT[:, cs]),
                                                        (vnew[:, :], attnT[:, blk, 64 * half:64 * half + 64])]),
                     reads=[r_Sb, r_qdb[blk], r_vnew, r_at[blk]], writes=[r_ob])
                P.op("dve", I_stt(Sb, Sf, cdv[:, n:n + 1], qb[(7, 1)], ALU.mult, ALU.add),
                     reads=[r_Sf, r_cdv, qr[(7, 1)]], writes=[r_Sb])
                P.op("dve", I_stt(Sf, Sf, cdv[:, n:n + 1], qb[(7, 1)], ALU.mult, ALU.add),
                     reads=[r_Sf, r_cdv, qr[(7, 1)]], writes=[r_Sf])
                if n % 8 == 7:
                    ws = slice(w * WIN, (w + 1) * WIN)
                    P.op("act", I_act(osb, ob[:], AF.Copy), reads=[r_ob], writes=[r_osb])
                    P.op("act", I_act(sq[0], osb, AF.Square), reads=[r_osb], writes=[r_sq[0]])
                    P.op("pe", I_mm(banks[0][:], ones[:], sq[0]), reads=[r_sq[0], self.r_const], writes=[bregs[0]])
                    P.op("act", I_act(rt, banks[0][:], AF.Ln, bias=EPS, scale=1.0 / 128), reads=[bregs[0]],
                         writes=[r_rt])
                    P.op("act", I_act(rstd, rt, AF.Exp, scale=-0.5), reads=[r_rt], writes=[r_rstd])
                    P.op("dve", I_tt(osb, osb, rstd, ALU.mult), reads=[r_osb, r_rstd], writes=[r_osb])
                    P.op("dve", I_tt(ogT, osb, zsT[:, ws], ALU.mult), reads=[r_osb, r_zs], writes=[r_og])
                    for dch in range(KC):
                        P.op("pe", I_mm(banks[1][:], wo[hs][:, dch * 128:(dch + 1) * 128], ogT),
                             reads=[r_wo[hs], r_og], writes=[bregs[1]])
                        P.op("dve", I_tt(xT[:, dch, ws], banks[1][:], xT[:, dch, ws], ALU.add),
                             reads=[bregs[1], self.r_x[dch][w]], writes=[self.r_x[dch][w]])
        allregs = [r_cst, r_small, r_nega, r_abT, r_tok, r_wab, r_raw, r_acc, r_q, r_k, r_v, r_qd, r_zs, r_cdv,
                   r_Sf, r_Sb, r_vnew, r_osb, r_og]
        self.phase_barrier([allregs, r_wh, r_wo, r_hv, r_u, r_wT, r_at, r_kt, r_qdb, r_tmp, r_W, bregs])
        ar.reset(m)

    def phase_barrier(self, reg_lists):
        regs = [r for rl in reg_lists for r in rl]
        for n in ("pe", "act", "dve", "pool", "sp"):
            self.P.wait_all(n, regs)

    def build(self):
        nc, P, st = self.nc, self.P, self.stack
        self.s_wgu = [P.dma_src() for _ in range(4)]
        self.s_wd = [P.dma_src() for _ in range(3)]
        self.s_awb = [P.dma_src() for _ in range(3)]
        self.s_awout = P.dma_src()
        self.s_eb = [P.dma_src() for _ in range(2)]
        self.s_bc = [P.dma_src() for _ in range(3)]
        self.s_wh = [P.dma_src() for _ in range(3)]
        self.s_wo = [P.dma_src() for _ in range(2)]
        self.s_hv = [P.dma_src() for _ in range(2)]
        self.emit_consts()
        for s in range(self.n_seq):
            self.emit_load_x(s)
            for ph in self.phases:
                if ph[0] == "ffn":
                    self.emit_ffn(ph[1], ph[2])
                elif ph[0] == "mixa":
                    self.emit_mixa(ph[1], ph[2])
                elif ph[0] == "mixb":
                    self.emit_mixb(ph[1], ph[2])
                elif ph[0] == "final":
                    self.emit_rmsnorm(12, range(NW), out_f32_inplace=True)
                else:
                    raise ValueError(ph)
            self.emit_store_x(s)
        P.wait_all("sp", [r for k in range(KC) for r in self.r_x[k]])
        with nc.Block() as block:
            P.replay(block)
        st.close()
        return nc


def full_phases():
    ph = []
    for i in range(DEPTH):
        ph.append(("ffn", 2 * i, 3 * i))
        ph.append(("mixa" if i % 2 == 0 else "mixb", i // 2, 3 * i + 1))
        ph.append(("ffn", 2 * i + 1, 3 * i + 2))
    ph.append(("final",))
    return ph


def prep_weights(inp):
    f32 = np.float32
    out = {}
    ng = np.concatenate([np.asarray(inp["norm_g"], f32).reshape(12, D), np.asarray(inp["final_g"], f32).reshape(1, D)], 0)
    out["normg"] = np.ascontiguousarray(ng.reshape(13, KC, 128).transpose(2, 0, 1).reshape(128, 13 * KC))
    wg = np.asarray(inp["ffn_w_gate"], f32).reshape(2 * DEPTH, KC, 128, FC, 128)
    wu = np.asarray(inp["ffn_w_up"], f32).reshape(2 * DEPTH, KC, 128, FC, 128)
    wgu = np.stack([wg, wu], 0)
    wgu = wgu.transpose(1, 4, 3, 0, 2, 5)
    out["wgu"] = np.ascontiguousarray(wgu).reshape(2 * DEPTH, FC, 128, 2 * KC * 128)
    wd = np.asarray(inp["ffn_w_down"], f32).reshape(2 * DEPTH, FC, 128, KC, 128)
    wd = wd.transpose(0, 3, 2, 1, 4)
    out["wd"] = np.ascontiguousarray(wd).reshape(2 * DEPTH, KC, 128, FC * 128)
    awin = np.asarray(inp["a_w_in"], f32).reshape(2, KC, 128, 3, 3, 4, 128)
    awin = awin.transpose(0, 5, 2, 1, 3, 4, 6)
    out["awin"] = np.ascontiguousarray(awin).reshape(2, 4, 128, KC, 3 * 384)
    awout = np.asarray(inp["a_w_out"], f32).reshape(2, 4, 128, D).transpose(0, 2, 1, 3)
    out["awout"] = np.ascontiguousarray(awout)
    out["bt"] = bias_table(np.asarray(inp["rel_bias"], f32))
    bw = np.asarray(inp["b_w_in"], f32)
    w4 = bw[:, :, :4096].reshape(2, KC, 128, 4, 8, 128).transpose(0, 4, 2, 1, 3, 5)
    out["bwin"] = np.ascontiguousarray(w4).reshape(2, 8, 128, KC, 512)
    wa = bw[:, :, 4096:4104].reshape(2, KC, 128, 8).transpose(0, 2, 1, 3)
    wb_ = bw[:, :, 4104:4112].reshape(2, KC, 128, 8).transpose(0, 2, 1, 3)
    wab = np.zeros((2, 128, KC, 64), f32)
    wab[..., 0:8] = wa
    wab[..., 32:40] = wb_
    out["wab"] = wab
    out["wabt"] = np.ascontiguousarray(np.concatenate([wa, wb_], -1))
    cw = np.asarray(inp["b_conv_w"], f32).reshape(2, 4, 3, 8, 128).transpose(0, 3, 4, 2, 1)
    out["bconv"] = np.ascontiguousarray(cw).reshape(2, 8, 128, 12)
    sm = np.zeros((2, 128, 259), f32)
    dtb = np.asarray(inp["b_dt_bias"], f32)
    alog = np.asarray(inp["b_a_log"], f32)
    sm[:, 0:8, 0] = dtb
    sm[:, 0:8, 1] = alog
    sm[:, :, 2] = np.asarray(inp["b_norm_w"], f32)
    sm[:, :, 3:131] = np.tile(dtb, (1, 16))[:, None, :]
    sm[:, :, 131:259] = np.tile(alog, (1, 16))[:, None, :]
    out["bsmall"] = sm
    out["bwout"] = np.ascontiguousarray(np.asarray(inp["b_w_out"], f32).reshape(2, 8, 128, D))
    out["bconst"] = b_consts()
    return out


def b_consts():
    i = np.arange(128)
    same = (i[:, None] // 64) == (i[None, :] // 64)
    c = np.zeros((128, 13 * 128), np.float32)
    c[:, 0:128] = (same & (i[:, None] <= i[None, :]))
    c[:, 128:256] = (i[:, None] == (i[None, :] // 64) * 64 + 63)
    c[:, 256:384] = np.eye(128)
    c[:, 384:512] = (same & (i[None, :] >= i[:, None]))
    c[:, 512:640] = (same & (i[None, :] > i[:, None]))
    sel = np.zeros((128, 8, 128), np.float32)
    for h in range(8):
        sel[h, h, :] = 1.0
        sel[32 + h, h, :] = 1.0
    c[:, 640:] = sel.reshape(128, 1024)
    return c


def t5_bucket_np(dist):
    n = dist.astype(np.float32)
    large = np.float32(16.0) + (np.log(np.maximum(n, np.float32(1.0)) / np.float32(16.0)).astype(np.float32)
                                / np.float32(np.log(2048.0 / 16.0)) * np.float32(16.0)).astype(np.float32)
    large = np.minimum(large.astype(np.int32), 31)
    return np.where(dist < 16, dist, large)


def bias_table(rel_bias):
    kk = np.arange(128)[:, None]
    jj = np.arange(256)[None, :]
    rel = jj - kk
    valid = (rel >= 0) & (rel <= 128)
    bt = np.full((4, 128, 6, 256), -30000.0, np.float32)
    for g, dil in enumerate((1, 4, 16)):
        bk = t5_bucket_np(np.maximum(rel, 0) * dil)
        for pair in range(4):
            for hp in range(2):
                tab = rel_bias[:, g * 8 + 2 * pair + hp][bk]
                bt[pair, :, g * 2 + hp, :] = np.where(valid, tab, np.float32(-30000.0))
    return bt


def x_to_dev(x_seqs):
    n = x_seqs.shape[0]
    return np.ascontiguousarray(x_seqs.reshape(n, S, KC, 128).transpose(0, 2, 3, 1))


def x_from_dev(o):
    n = o.shape[0]
    return np.ascontiguousarray(o.transpose(0, 3, 1, 2).reshape(n, S, D))


_PROG_CACHE = {}


def get_prog(n_seq, phases):
    key = (n_seq, tuple(phases))
    if key not in _PROG_CACHE:
        _PROG_CACHE[key] = Builder(n_seq, list(phases)).build()
    return _PROG_CACHE[key]


def run(inputs, n_seq_per_core, phases, n_cores=N_CORES, trace=False, x_override=None):
    w = prep_weights(inputs)
    x = np.asarray(inputs["x"] if x_override is None else x_override, np.float32)
    nc = get_prog(n_seq_per_core, phases)
    in_maps = []
    for c in range(n_cores):
        m = dict(w)
        m["xT"] = x_to_dev(x[c * n_seq_per_core:(c + 1) * n_seq_per_core])
        in_maps.append(m)
    res = run_bass_kernel_spmd(nc, in_maps, core_ids=list(range(n_cores)), trace=trace)
    outs = [x_from_dev(r["outT"]) for r in res.results]
    return np.concatenate(outs, 0), res


def kernel(**inputs):
    out, _ = run(inputs, 4, full_phases())
    return out
```

```python
import numpy as np
import concourse.bass as bass
import concourse.mybir as mybir
from concourse.bass_utils import run_bass_kernel_spmd

F32 = mybir.dt.float32
BF16 = mybir.dt.bfloat16
AF = mybir.ActivationFunctionType
ALU = mybir.AluOpType

D = 1024
S = 2048
DFF = 2816
KC = D // 128
FC = DFF // 128
WIN = 512
NW = S // WIN
DEPTH = 4
EPS = 1e-6
N_CORES = 8
MIXB_STOP = None
OP_LIMIT = None


class Src:
    def __init__(self, sem, step):
        self.sem = sem
        self.step = step
        self.val = 0


class Reg:
    __slots__ = ("name", "w", "r")

    def __init__(self, name=""):
        self.name = name
        self.w = None
        self.r = {}


class Eng:
    def __init__(self, name, src):
        self.name = name
        self.src = src
        self.seen = {}
        self.ops = []


class Prog:
    def __init__(self, nc, stack):
        self.nc = nc
        self.stack = stack
        self.engs = {}
        for n in ("pe", "act", "dve", "pool", "sp"):
            sem = stack.enter_context(nc.semaphore("sem_" + n))
            self.engs[n] = Eng(n, Src(sem, 1))
        self.n_dma_src = 0

    def dma_src(self):
        self.n_dma_src += 1
        sem = self.stack.enter_context(self.nc.semaphore("dsem%d" % self.n_dma_src))
        return Src(sem, 16)

    def _waits(self, eng, reads, writes):
        deps = {}
        for r in reads:
            if r.w is not None:
                s, v = r.w
                if deps.get(s, 0) < v:
                    deps[s] = v
        for w in writes:
            if w.w is not None:
                s, v = w.w
                if deps.get(s, 0) < v:
                    deps[s] = v
            for s, v in w.r.items():
                if deps.get(s, 0) < v:
                    deps[s] = v
        for s, v in deps.items():
            if s is eng.src and eng.name == "pe":
                continue
            if eng.seen.get(s, 0) >= v:
                continue
            eng.seen[s] = v
            eng.ops.append(("wait", s.sem, v))

    def op(self, eng, fn, reads=(), writes=()):
        if getattr(self, "count_active", False):
            self.cnt = getattr(self, "cnt", 0) + 1
            if OP_LIMIT is not None and self.cnt > OP_LIMIT:
                return
        e = self.engs[eng]
        self._waits(e, reads, writes)
        e.src.val += 1
        v = e.src.val
        e.ops.append(("ins", fn, e.src.sem, 1))
        for r in reads:
            if r.r.get(e.src, 0) < v:
                r.r[e.src] = v
        for w in writes:
            w.w = (e.src, v)
            w.r = {}

    def dma(self, eng, src, out, in_, reads=(), writes=(), **kw):
        e = self.engs[eng]
        self._waits(e, reads, writes)
        src.val += 16
        v = src.val
        e.ops.append(("ins", lambda q: q.dma_start(out=out, in_=in_, **kw), src.sem, 16))
        for r in reads:
            if r.r.get(src, 0) < v:
                r.r[src] = v
        for w in writes:
            w.w = (src, v)
            w.r = {}

    def wait_all(self, eng, regs):
        e = self.engs[eng]
        self._waits(e, (), regs)

    def replay(self, block):
        handles = {"pe": block.tensor, "act": block.scalar, "dve": block.vector,
                   "pool": block.gpsimd, "sp": block.sync}
        for n, e in self.engs.items():
            ops = e.ops

            def body(q, ops=ops):
                for o in ops:
                    if o[0] == "wait":
                        q.wait_ge(o[1], o[2])
                    else:
                        o[1](q).then_inc(o[2], o[3])
            handles[n](body)


class Arena:
    def __init__(self, ap_bf16, nbytes):
        self.ap = ap_bf16
        self.nbytes = nbytes
        self.off = 0

    def mark(self):
        return self.off

    def reset(self, m=0):
        self.off = m

    def alloc(self, shape, dtype, parts=128):
        n = 1
        for s in shape:
            n *= s
        esz = 4 if dtype == F32 else 2
        nb = (n * esz + 31) // 32 * 32
        assert self.off + nb <= self.nbytes, ("arena overflow", self.off, nb, self.nbytes)
        a = self.ap[0:parts, self.off // 2:(self.off + n * esz) // 2]
        self.off += nb
        if dtype == F32:
            a = a.bitcast(F32)
        if len(shape) == 2:
            a = a.rearrange("p (a b) -> p a b", a=shape[0])
        elif len(shape) == 3:
            a = a.rearrange("p (a b c) -> p a b c", a=shape[0], b=shape[1])
        elif len(shape) == 4:
            a = a.rearrange("p (a b c d) -> p a b c d", a=shape[0], b=shape[1], c=shape[2])
        return a


def I_act(out, in_, func, **kw):
    return lambda q: q.activation(out=out, in_=in_, func=func, **kw)


def I_tt(out, in0, in1, op):
    return lambda q: q.tensor_tensor(out=out, in0=in0, in1=in1, op=op)


def I_stt(out, in0, scalar, in1, op0, op1):
    return lambda q: q.scalar_tensor_tensor(out=out, in0=in0, scalar=scalar, in1=in1, op0=op0, op1=op1)


def I_ts(out, in0, s1, s2, op0, op1=None):
    if op1 is None:
        return lambda q: q.tensor_scalar(out=out, in0=in0, scalar1=s1, scalar2=None, op0=op0)
    return lambda q: q.tensor_scalar(out=out, in0=in0, scalar1=s1, scalar2=s2, op0=op0, op1=op1)


def I_recip(out, in_):
    return lambda q: q.reciprocal(out=out, in_=in_)


def I_copy(out, in_):
    return lambda q: q.tensor_copy(out=out, in_=in_)


def I_memset(out, val):
    return lambda q: q.memset(out, val)


def I_mm(out, lhsT, rhs, start=True, stop=True):
    return lambda q: q.matmul(out, lhsT=lhsT, rhs=rhs, start=start, stop=stop)


def I_mmgroup(out, pairs):
    pairs = list(pairs)

    def f(q):
        ins = None
        n = len(pairs)
        for i, (l, r) in enumerate(pairs):
            ins = q.matmul(out, lhsT=l, rhs=r, start=(i == 0), stop=(i == n - 1))
        return ins
    return f


def I_multi(items):
    items = list(items)

    def f(q):
        ins = None
        for (o, l, r) in items:
            ins = q.matmul(o, lhsT=l, rhs=r, start=True, stop=True)
        return ins
    return f


def I_multi_tr(items):
    items = list(items)

    def f(q):
        ins = None
        for (o, i, ident) in items:
            ins = q.transpose(o, i, ident)
        return ins
    return f


def I_transpose(out, in_, ident):
    return lambda q: q.transpose(out, in_, ident)


class Builder:
    def __init__(self, n_seq, phases):
        from contextlib import ExitStack
        self.n_seq = n_seq
        self.phases = phases
        self.stack = ExitStack()
        nc = self.nc = bass.Bass("TRN2", target_bir_lowering=False)
        st = self.stack
        self.P = Prog(nc, st)
        dt = nc.dram_tensor
        self.d_xT = dt("xT", [n_seq, KC, 128, S], F32, kind="ExternalInput").ap()
        self.d_out = dt("outT", [n_seq, KC, 128, S], F32, kind="ExternalOutput").ap()
        self.d_normg = dt("normg", [128, 13 * KC], F32, kind="ExternalInput").ap()
        self.d_wgu = dt("wgu", [2 * DEPTH, FC, 128, 2 * KC * 128], F32, kind="ExternalInput").ap()
        self.d_wd = dt("wd", [2 * DEPTH, KC, 128, FC * 128], F32, kind="ExternalInput").ap()
        self.d_awin = dt("awin", [2, 4, 128, KC, 3 * 384], F32, kind="ExternalInput").ap()
        self.d_awout = dt("awout", [2, 128, 4, D], F32, kind="ExternalInput").ap()
        self.d_bt = dt("bt", [4, 128, 6, 256], F32, kind="ExternalInput").ap()
        self.d_bwin = dt("bwin", [2, 8, 128, KC, 512], F32, kind="ExternalInput").ap()
        self.d_wab = dt("wab", [2, 128, KC, 64], F32, kind="ExternalInput").ap()
        self.d_wabt = dt("wabt", [2, 128, KC, 16], F32, kind="ExternalInput").ap()
        self.d_bconv = dt("bconv", [2, 8, 128, 12], F32, kind="ExternalInput").ap()
        self.d_bsmall = dt("bsmall", [2, 128, 259], F32, kind="ExternalInput").ap()
        self.d_bwout = dt("bwout", [2, 8, 128, D], F32, kind="ExternalInput").ap()
        self.d_bconst = dt("bconst", [128, 13 * 128], F32, kind="ExternalInput").ap()
        sb = lambda name, shape, dtype: st.enter_context(nc.sbuf_tensor(name, shape, dtype))
        self.xT = sb("xTs", [128, KC, S], F32)
        self.hT = sb("hTs", [128, KC, S], BF16)
        self.normg = sb("normg_s", [128, 13 * KC], F32)
        self.ones_bf = sb("ones_bf", [128, 128], BF16)
        self.n_sq = [sb("n_sq%d" % i, [128, WIN], BF16) for i in range(2)]
        self.n_rt = sb("n_rt", [128, WIN], F32)
        self.n_rstd = sb("n_rstd", [128, WIN], F32)
        self.r_nsq = [Reg("sq0"), Reg("sq1")]
        self.r_nrt, self.r_nrstd = Reg("rt"), Reg("rstd")
        ARENA = (nc.sbuf_bytes_remaining - 2048) // 64 * 64
        self.arena_t = sb("arena", [128, ARENA // 2], BF16)
        self.arena = Arena(self.arena_t, ARENA)
        self.banks = [st.enter_context(nc.psum_tensor("bank%d" % i, [128, 512], F32)) for i in range(8)]
        self.bank_regs = [Reg("bank%d" % i) for i in range(8)]
        self.r_x = [[Reg("x%d_%d" % (k, w)) for w in range(NW)] for k in range(KC)]
        self.r_h = [[Reg("h%d_%d" % (k, w)) for w in range(NW)] for k in range(KC)]
        self.r_const = Reg("const")
        self.s_io = [self.P.dma_src() for _ in range(KC)]
        self.s_const = self.P.dma_src()

    def emit_consts(self):
        P = self.P
        P.dma("sp", self.s_const, self.normg[:], self.d_normg[:, :], writes=[self.r_const])
        ones = self.ones_bf
        P.op("dve", I_memset(ones[:], 1.0), writes=[self.r_const])

    def emit_load_x(self, s):
        P = self.P
        for k in range(KC):
            P.dma("sp", self.s_io[k], self.xT[:, k, :], self.d_xT[s, k, :, :], writes=self.r_x[k])

    def emit_store_x(self, s, src_is_h=False):
        P = self.P
        for k in range(KC):
            P.dma("sp", self.s_io[k], self.d_out[s, k, :, :], self.xT[:, k, :], reads=self.r_x[k])

    def emit_rmsnorm(self, norm_idx, windows, out_f32_inplace=False):
        P = self.P
        sq = [t[:] for t in self.n_sq]
        r_sq = self.r_nsq
        rt, rstd = self.n_rt[:], self.n_rstd[:]
        r_rt, r_rstd = self.r_nrt, self.r_nrstd
        xT, hT, ones, normg = self.xT, self.hT, self.ones_bf, self.normg
        bank = self.banks[7]
        r_bank = self.bank_regs[7]
        for w in windows:
            ws = slice(w * WIN, (w + 1) * WIN)
            for k in range(KC):
                b = k % 2
                P.op("act", I_act(sq[b], xT[:, k, ws], AF.Square),
                     reads=[self.r_x[k][w]], writes=[r_sq[b]])
                P.op("pe", I_mm(bank[:], ones[:], sq[b], start=(k == 0), stop=(k == KC - 1)),
                     reads=[r_sq[b], self.r_const], writes=[r_bank])
            P.op("act", I_act(rt, bank[:], AF.Ln, bias=EPS, scale=1.0 / D),
                 reads=[r_bank], writes=[r_rt])
            P.op("act", I_act(rstd, rt, AF.Exp, scale=-0.5), reads=[r_rt], writes=[r_rstd])
            for k in range(KC):
                g_ap = normg[:, norm_idx * KC + k:norm_idx * KC + k + 1]
                if out_f32_inplace:
                    P.op("dve", I_stt(xT[:, k, ws], xT[:, k, ws], g_ap, rstd, ALU.mult, ALU.mult),
                         reads=[r_rstd, self.r_const], writes=[self.r_x[k][w]])
                else:
                    P.op("dve", I_stt(hT[:, k, ws], xT[:, k, ws], g_ap, rstd, ALU.mult, ALU.mult),
                         reads=[self.r_x[k][w], r_rstd, self.r_const], writes=[self.r_h[k][w]])

    def emit_ffn(self, ffn_idx, norm_idx):
        P = self.P
        ar = self.arena
        xT, hT = self.xT, self.hT
        HW = 2
        for half in range(NW // HW):
            wins = [half * HW + i for i in range(HW)]
            self.emit_rmsnorm(norm_idx, wins)
            m = ar.mark()
            aT = ar.alloc([FC, HW * WIN], BF16)
            r_a = [[Reg("a") for _ in range(HW)] for _ in range(FC)]
            NGU = 4
            wgu = [ar.alloc([2, KC, 128], BF16) for _ in range(NGU)]
            r_wgu = [Reg("wgu%d" % i) for i in range(NGU)]
            s_wgu = self.s_wgu
            NWD = 3
            wd = [ar.alloc([FC, 128], BF16) for _ in range(NWD)]
            r_wd = [Reg("wd%d" % i) for i in range(NWD)]
            s_wd = self.s_wd
            sil = [ar.alloc([WIN], F32) for _ in range(2)]
            r_sil = [Reg("sil0"), Reg("sil1")]
            it = 0

            def load_wgu(f):
                sl = f % NGU
                P.dma("pool", s_wgu[sl], wgu[sl].rearrange("p a b c -> p (a b c)"),
                      self.d_wgu[ffn_idx, f, :, :], writes=[r_wgu[sl]])

            def load_wd(d):
                sl = d % NWD
                hf = FC // 2
                P.dma("pool", s_wd[sl], wd[sl][:, 0:hf, :].rearrange("p a b -> p (a b)"),
                      self.d_wd[ffn_idx, d, :, 0:hf * 128], writes=[r_wd[sl]])
                P.dma("pool", s_wd[sl], wd[sl][:, hf:FC, :].rearrange("p a b -> p (a b)"),
                      self.d_wd[ffn_idx, d, :, hf * 128:FC * 128], writes=[r_wd[sl]])

            for f in range(NGU):
                load_wgu(f)
            for d in range(NWD):
                load_wd(d)
            for f in range(FC):
                sl = f % NGU
                for wi, w in enumerate(wins):
                    ws = slice(w * WIN, (w + 1) * WIN)
                    pb = it % 2
                    it += 1
                    bg, bu = self.banks[pb * 2], self.banks[pb * 2 + 1]
                    rg, ru = self.bank_regs[pb * 2], self.bank_regs[pb * 2 + 1]
                    rd = [r_wgu[sl]] + [self.r_h[k][w] for k in range(KC)]
                    P.op("pe", I_mmgroup(bg[:], [(wgu[sl][:, 0, k, :], hT[:, k, ws]) for k in range(KC)]),
                         reads=rd, writes=[rg])
                    P.op("pe", I_mmgroup(bu[:], [(wgu[sl][:, 1, k, :], hT[:, k, ws]) for k in range(KC)]),
                         reads=rd, writes=[ru])
                    sb_ = it % 2
                    P.op("act", I_act(sil[sb_], bg[:], AF.Silu), reads=[rg], writes=[r_sil[sb_]])
                    P.op("dve", I_tt(aT[:, f, wi * WIN:(wi + 1) * WIN], sil[sb_], bu[:], ALU.mult),
                         reads=[r_sil[sb_], ru], writes=[r_a[f][wi]])
                if f + NGU < FC:
                    load_wgu(f + NGU)
            for d in range(KC):
                sl = d % NWD
                for wi, w in enumerate(wins):
                    ws = slice(w * WIN, (w + 1) * WIN)
                    pb = it % 2
                    it += 1
                    by, ry = self.banks[4 + pb], self.bank_regs[4 + pb]
                    P.op("pe", I_mmgroup(by[:], [(wd[sl][:, f, :], aT[:, f, wi * WIN:(wi + 1) * WIN]) for f in range(FC)]),
                         reads=[r_wd[sl]] + [r_a[f][wi] for f in range(FC)], writes=[ry])
                    P.op("dve", I_stt(xT[:, d, ws], by[:], 0.5, xT[:, d, ws], ALU.mult, ALU.add),
                         reads=[ry, self.r_x[d][w]], writes=[self.r_x[d][w]])
                if d + NWD < KC:
                    load_wd(d + NWD)
            self.phase_barrier([r_wgu, r_wd, r_sil, [r for rr in r_a for r in rr]])
            ar.reset(m)

    def emit_mixa(self, j, norm_idx):
        P = self.P
        ar = self.arena
        xT, hT = self.xT, self.hT
        banks, bregs = self.banks, self.bank_regs
        self.emit_rmsnorm(norm_idx, range(NW))
        r_hall = [self.r_h[k][w] for k in range(KC) for w in range(NW)]
        m = ar.mark()
        NWB = 3
        wbuf = [ar.alloc([KC, 384], BF16) for _ in range(NWB)]
        r_wbuf = [Reg("awb%d" % i) for i in range(NWB)]
        wout = ar.alloc([4, D], BF16)
        r_wout = Reg("awout")
        ebuf = [ar.alloc([6, 256], F32) for _ in range(2)]
        r_ebuf = [Reg("eb0"), Reg("eb1")]
        qT = [ar.alloc([S], BF16) for _ in range(2)]
        kT = [ar.alloc([S], BF16) for _ in range(2)]
        r_qk = [Reg("qk0"), Reg("qk1")]
        vbuf = [ar.alloc([16, 128], BF16) for _ in range(2)]
        r_v = [Reg("v0"), Reg("v1")]
        accn = ar.alloc([S], F32)
        accd = ar.alloc([S], F32)
        r_accn, r_accd = Reg("accn"), Reg("accd")
        oT = ar.alloc([S], BF16)
        r_oT = Reg("oT")
        NE = 4
        ebf = [ar.alloc([256], F32) for _ in range(NE)]
        r_ebf = [Reg("E%d" % i) for i in range(NE)]
        NPT = 8
        pT = [ar.alloc([256], BF16) for _ in range(NPT)]
        r_pT = [Reg("pT%d" % i) for i in range(NPT)]
        ones = self.ones_bf

        P.dma("pool", self.s_awout, wout, self.d_awout[j, :, :, :], writes=[r_wout])
        wl = [(pair, g) for pair in range(4) for g in range(3)]

        def load_w(i):
            pair, g = wl[i]
            sl = i % NWB
            P.dma("pool", self.s_awb[sl], wbuf[sl], self.d_awin[j, pair, :, :, g * 384:(g + 1) * 384],
                  writes=[r_wbuf[sl]])

        def load_eb(pair):
            sl = pair % 2
            P.dma("sp", self.s_eb[sl], ebuf[sl], self.d_bt[pair, :, :, :], writes=[r_ebuf[sl]])
            P.op("act", I_act(ebuf[sl], ebuf[sl], AF.Exp), reads=[r_ebuf[sl]], writes=[r_ebuf[sl]])

        for i in range(NWB):
            load_w(i)
        load_eb(0)
        pj = 0
        sj = 0
        ej = 0
        pj_t = 0
        nj = 0
        for i, (pair, g) in enumerate(wl):
            dil = (1, 4, 16)[g]
            L = S // dil
            wsl = i % NWB
            qs = i % 2
            if g == 0 and pair + 1 < 4:
                load_eb(pair + 1)
            eb = ebuf[pair % 2]
            r_eb = r_ebuf[pair % 2]
            for which, dst in ((0, qT[qs]), (1, kT[qs])):
                for w in range(NW):
                    ws = slice(w * WIN, (w + 1) * WIN)
                    b = pj % 2
                    pj += 1
                    P.op("pe", I_mmgroup(banks[b][:], [(wbuf[wsl][:, k, which * 128:(which + 1) * 128], hT[:, k, ws])
                                                       for k in range(KC)]),
                         reads=[r_wbuf[wsl]] + [self.r_h[k][w] for k in range(KC)], writes=[bregs[b]])
                    nl = WIN // dil
                    if dil == 1:
                        o_ap = dst[:, ws]
                        i_ap = banks[b][:]
                    else:
                        o_ap = dst.rearrange("p (c l) -> p c l", c=dil)[:, :, w * nl:(w + 1) * nl]
                        i_ap = banks[b][:].rearrange("p (l c) -> p c l", c=dil)
                    if which == 0:
                        P.op("act", I_act(o_ap, i_ap, AF.Copy, scale=0.125), reads=[bregs[b]], writes=[r_qk[qs]])
                    else:
                        P.op("dve", I_copy(o_ap, i_ap), reads=[bregs[b]], writes=[r_qk[qs]])
            nbl = L // 128
            for bq in range(4):
                b = pj % 2
                pj += 1
                for bi in range(4):
                    blk = bq * 4 + bi
                    c, n = blk // nbl, blk % nbl
                    t0 = n * 128 * dil + c
                    tsl = slice(t0, t0 + 127 * dil + 1, dil)
                    tw = set(t // WIN for t in (t0, t0 + 127 * dil))
                    P.op("pe", I_mmgroup(banks[b][:, bi * 128:(bi + 1) * 128],
                                         [(hT[:, k, tsl], wbuf[wsl][:, k, 256:384]) for k in range(KC)]),
                         reads=[r_wbuf[wsl]] + [self.r_h[k][w] for k in range(KC) for w in tw],
                         writes=[bregs[b]])
                P.op("dve", I_copy(vbuf[qs][:, bq * 4:(bq + 1) * 4, :].rearrange("p a b -> p (a b)"), banks[b][:]),
                     reads=[bregs[b]], writes=[r_v[qs]])
            if i + NWB < len(wl):
                load_w(i + NWB)
            groups = []
            for c in range(dil):
                for qg in range(max(1, nbl // 4)):
                    if nbl == 1:
                        if c % 4 != 0:
                            continue
                        groups.append(([(c + cc, 0) for cc in range(4)], c, qg))
                    else:
                        groups.append(([(c, qg * 4 + qq) for qq in range(4)], c, qg))
            seq = []
            for gi_, (items, c, qg) in enumerate(groups):
                for hp in range(2):
                    for col, (cc, qb) in enumerate(items):
                        seq.append((gi_, hp, col, cc, qb))
            tile_of = {}
            uses_left = {}
            slot_owner = [None] * NPT

            def need_of(cc, qb):
                return [kb for kb in (qb - 1, qb) if kb >= 0]

            def n_uses(cc, kb):
                return (1 if kb < nbl else 0) + (1 if kb + 1 < nbl else 0)

            def stage1(hp, cc, kb):
                nonlocal sj, ej, pj_t
                key = (hp, cc, kb)
                if key in tile_of:
                    return
                ps = slice(64 * hp, 64 * hp + 64)
                base = cc * L
                nq = min(256, L - kb * 128)
                sb_ = 2 + sj % 2
                sj += 1
                P.op("pe", I_mm(banks[sb_][:, 0:nq],
                                kT[qs][ps, base + kb * 128: base + kb * 128 + 128],
                                qT[qs][ps, base + kb * 128: base + kb * 128 + nq]),
                     reads=[r_qk[qs]], writes=[bregs[sb_]])
                e_i = ej % NE
                ej += 1
                P.op("act", I_act(ebf[e_i][:, 0:nq], banks[sb_][:, 0:nq], AF.Exp),
                     reads=[bregs[sb_]], writes=[r_ebf[e_i]])
                p_i = pj_t % NPT
                pj_t += 1
                assert slot_owner[p_i] is None or uses_left[slot_owner[p_i]] == 0, "pT ring too small"
                slot_owner[p_i] = key
                uses_left[key] = n_uses(cc, kb)
                P.op("pool", I_tt(pT[p_i][:, 0:nq], ebf[e_i][:, 0:nq], eb[:, g * 2 + hp, 0:nq], ALU.mult),
                     reads=[r_ebf[e_i], r_eb], writes=[r_pT[p_i]])
                tile_of[key] = p_i

            LA = 2
            gbank = {}
            for i_, (gi_, hp, col, cc, qb) in enumerate(seq):
                for la in range(i_, min(i_ + LA + 1, len(seq))):
                    _, hp2, _, cc2, qb2 = seq[la]
                    for kb in need_of(cc2, qb2):
                        stage1(hp2, cc2, kb)
                if gi_ not in gbank:
                    gbank[gi_] = nj % 2
                    nj += 1
                bsel = gbank[gi_]
                bn, bd = banks[4 + bsel], banks[6 + bsel]
                rbn, rbd = bregs[4 + bsel], bregs[6 + bsel]
                ps = slice(64 * hp, 64 * hp + 64)
                pairs_n, pairs_d, rd = [], [], [r_v[qs], self.r_const]
                for kb in need_of(cc, qb):
                    key = (hp, cc, kb)
                    p_i = tile_of[key]
                    assert slot_owner[p_i] == key
                    uses_left[key] -= 1
                    off = (qb - kb) * 128
                    blk = cc * nbl + kb
                    pairs_n.append((vbuf[qs][:, blk, 64 * hp:64 * hp + 64], pT[p_i][:, off:off + 128]))
                    pairs_d.append((ones[:, 0:64], pT[p_i][:, off:off + 128]))
                    rd.append(r_pT[p_i])
                P.op("pe", I_mmgroup(bn[ps, col * 128:(col + 1) * 128], pairs_n), reads=rd, writes=[rbn])
                P.op("pe", I_mmgroup(bd[ps, col * 128:(col + 1) * 128], pairs_d), reads=rd, writes=[rbd])
                last_of_group = (i_ + 1 == len(seq)) or (seq[i_ + 1][0] != gi_)
                if last_of_group:
                    items, c, qg = groups[gi_]
                    if dil == 1:
                        an = accn[:, qg * 512:(qg + 1) * 512]
                        ad = accd[:, qg * 512:(qg + 1) * 512]
                        sn, sd = bn[:], bd[:]
                    elif nbl == 1:
                        an = accn.rearrange("p (l c) -> p c l", c=dil)[:, c:c + 4, :]
                        ad = accd.rearrange("p (l c) -> p c l", c=dil)[:, c:c + 4, :]
                        sn = bn[:].rearrange("p (c l) -> p c l", c=4)
                        sd = bd[:].rearrange("p (c l) -> p c l", c=4)
                    else:
                        an = accn.rearrange("p (l c) -> p c l", c=dil)[:, c, qg * 512:(qg + 1) * 512]
                        ad = accd.rearrange("p (l c) -> p c l", c=dil)[:, c, qg * 512:(qg + 1) * 512]
                        sn, sd = bn[:], bd[:]
                    if g == 0:
                        P.op("dve", I_copy(an, sn), reads=[rbn], writes=[r_accn])
                        P.op("act", I_act(ad, sd, AF.Copy), reads=[rbd], writes=[r_accd])
                    else:
                        P.op("dve", I_tt(an, sn, an, ALU.add), reads=[rbn, r_accn], writes=[r_accn])
                        P.op("dve", I_tt(ad, sd, ad, ALU.add), reads=[rbd, r_accd], writes=[r_accd])
            if g == 2:
                P.op("act", I_act(accd, accd, AF.Ln), reads=[r_accd], writes=[r_accd])
                P.op("act", I_act(accd, accd, AF.Exp, scale=-1.0), reads=[r_accd], writes=[r_accd])
                P.op("dve", I_tt(oT, accn, accd, ALU.mult), reads=[r_accn, r_accd], writes=[r_oT])
                for dch in range(KC):
                    for w in range(NW):
                        ws = slice(w * WIN, (w + 1) * WIN)
                        b = pj % 2
                        pj += 1
                        P.op("pe", I_mm(banks[b][:], wout[:, pair, dch * 128:(dch + 1) * 128], oT[:, ws]),
                             reads=[r_wout, r_oT], writes=[bregs[b]])
                        P.op("dve", I_tt(xT[:, dch, ws], banks[b][:], xT[:, dch, ws], ALU.add),
                             reads=[bregs[b], self.r_x[dch][w]], writes=[self.r_x[dch][w]])
        self.phase_barrier([r_wbuf, [r_wout], r_ebuf, r_qk, r_v, [r_accn, r_accd, r_oT], r_ebf, r_pT])
        ar.reset(m)

    def emit_mixb(self, j, norm_idx):
        P = self.P
        ar = self.arena
        xT, hT = self.xT, self.hT
        banks, bregs = self.banks, self.bank_regs
        self.emit_rmsnorm(norm_idx, range(NW))
        m = ar.mark()
        NB = S // 128
        cst = ar.alloc([13 * 128], F32)
        r_cst = Reg("bcst")
        TRI, SELL, IDN, MUI, MUS = [cst[:, i * 128:(i + 1) * 128] for i in range(5)]
        SEL = cst[:, 5 * 128:13 * 128].rearrange("p (h m) -> p h m", h=8)
        idb = ar.alloc([128], BF16)
        small = ar.alloc([3 + 256], F32)
        r_small = Reg("bsmall")
        dtb_col, alog_col, normw_col = small[:, 0:1], small[:, 1:2], small[:, 2:3]
        dtb_rep, alog_rep = small[:, 3:131], small[:, 131:259]
        nega = ar.alloc([1 + 128], F32)
        r_nega = Reg("nega")
        abT = ar.alloc([S], F32)
        r_abT = Reg("abT")
        tok = ar.alloc([4, 128], F32)
        gc_tok, beta_tok, bg_tok, kt_tok = [tok[:, i, :] for i in range(4)]
        r_tok = Reg("tok")
        wab = ar.alloc([KC, 64], BF16)
        wabt = ar.alloc([KC, 16], BF16)
        r_wab = Reg("wab")
        wh = [ar.alloc([KC, 256], BF16) for _ in range(3)]
        r_wh = [Reg("wh%d" % i) for i in range(3)]
        wo = [ar.alloc([D], BF16) for _ in range(2)]
        r_wo = [Reg("wo0"), Reg("wo1")]
        hv = [ar.alloc([12], F32) for _ in range(2)]
        r_hv = [Reg("hv0"), Reg("hv1")]
        raw = ar.alloc([S + 4], F32)
        r_raw_w = [Reg("raw%d" % i) for i in range(NW + 1)]
        r_raw = r_raw_w[0]
        acc = ar.alloc([S], F32)
        r_acc_w = [Reg("acc%d" % i) for i in range(NW)]
        r_acc = r_acc_w[0]
        qT = ar.alloc([S], BF16)
        kT = ar.alloc([S], BF16)
        vT = ar.alloc([S], BF16)
        qdT = ar.alloc([S], BF16)
        zsT = ar.alloc([S], BF16)
        r_q, r_k, r_v, r_qd, r_zs = Reg("q"), Reg("k"), Reg("v"), Reg("qd"), Reg("zs")
        u = ar.alloc([NB, 128], BF16)
        wT = ar.alloc([S], BF16)
        attnT = ar.alloc([NB, 128], BF16)
        ktail = ar.alloc([NB, 2, 128], BF16)
        r_u = [Reg("u%d" % b) for b in range(NB)]
        r_wT = [Reg("wT%d" % b) for b in range(NB)]
        r_at = [Reg("at%d" % b) for b in range(NB)]
        r_kt = [Reg("kt%d" % b) for b in range(NB)]
        r_qdb = [Reg("qd%d" % b) for b in range(NB)]
        cdv = ar.alloc([2 * NB], F32)
        r_cdv = Reg("cdv")
        r_W = [Reg("W%d" % i) for i in range(8)]
        NT = 3
        tmp = [ar.alloc([128], F32) for _ in range(NT)]
        r_tmp = [Reg("t%d" % i) for i in range(NT)]
        tcount = [0]

        def T():
            i = tcount[0] % NT
            tcount[0] += 1
            return tmp[i], r_tmp[i]
        Sf = ar.alloc([128], F32)
        Sb = ar.alloc([128], BF16)
        r_Sf, r_Sb = Reg("Sf"), Reg("Sb")
        vnew = ar.alloc([128], BF16)
        r_vnew = Reg("vnew")
        osb = ar.alloc([WIN], F32)
        r_osb = Reg("osb")
        rt2 = ar.alloc([WIN], F32)
        rstd2 = ar.alloc([WIN], F32)
        r_rt2, r_rstd2 = Reg("rt2"), Reg("rstd2")
        ogT = ar.alloc([WIN], BF16)
        r_og = Reg("og")
        zt, r_zt = osb, r_osb
        ones = self.ones_bf
        sq, r_sq = [t[:] for t in self.n_sq], self.r_nsq
        rt, rstd, r_rt, r_rstd = self.n_rt[:], self.n_rstd[:], self.r_nrt, self.r_nrstd
        place = {(4, 0): (4, 0), (4, 1): (5, 2), (4, 2): (4, 2), (4, 3): (4, 3),
                 (5, 0): (5, 0), (5, 1): (5, 1), (5, 2): (0, 0), (5, 3): (1, 0),
                 (6, 0): (6, 0), (6, 1): (6, 1), (6, 2): (2, 0), (6, 3): (3, 0),
                 (7, 0): (7, 0), (7, 1): (6, 2)}
        qb = {}
        qr = {}
        for key, (b, q) in place.items():
            qb[key] = banks[b][:, q * 128:(q + 1) * 128]
            qr[key] = bregs[b]

        def qbf(b, q):
            b, q = place[(b, q)]
            return banks[b][:].bitcast(BF16)[:, q * 256:q * 256 + 128]

        P.dma("sp", self.s_bc[0], cst, self.d_bconst[:, :], writes=[r_cst])
        P.dma("sp", self.s_bc[1], small, self.d_bsmall[j, :, :], writes=[r_small])
        P.dma("pool", self.s_bc[2], wab, self.d_wab[j, :, :, :], writes=[r_wab])
        P.dma("pool", self.s_bc[2], wabt, self.d_wabt[j, :, :, :], writes=[r_wab])
        P.op("dve", I_copy(idb, IDN), reads=[r_cst], writes=[r_cst])
        P.op("dve", I_memset(raw[:, 0:4], 0.0), writes=[r_raw])
        P.op("dve", I_memset(ktail.rearrange("p a b c -> p (a b c)"), 0.0), writes=r_kt)
        P.op("act", I_act(nega[:, 0:1], alog_col, AF.Exp), reads=[r_small], writes=[r_nega])
        P.op("act", I_act(nega[:, 1:129], alog_rep, AF.Exp), reads=[r_small], writes=[r_nega])
        P.op("dve", I_ts(nega, nega, -1.0, None, ALU.mult), reads=[r_nega], writes=[r_nega])

        def load_head(h):
            sl = h % 2
            for half in range(2):
                i = (2 * h + half) % 3
                P.dma("pool", self.s_wh[i], wh[i], self.d_bwin[j, h, :, :, half * 256:(half + 1) * 256],
                      writes=[r_wh[i]])
            P.dma("pool", self.s_wo[sl], wo[sl], self.d_bwout[j, h, :, :], writes=[r_wo[sl]])
            P.dma("sp", self.s_hv[sl], hv[sl], self.d_bconv[j, h, :, :], writes=[r_hv[sl]])

        load_head(0)
        tA, r_tA = acc, r_acc
        r_tA_all = r_acc_w
        for w in range(NW):
            ws = slice(w * WIN, (w + 1) * WIN)
            b = w % 2
            P.op("pe", I_mmgroup(banks[b][0:64, :], [(wab[:, k, :], hT[:, k, ws]) for k in range(KC)]),
                 reads=[r_wab] + [self.r_h[k][w] for k in range(KC)], writes=[bregs[b]])
            P.op("act", I_act(tA[0:8, ws], banks[b][0:8, :], AF.Exp, bias=dtb_col[0:8, :]),
                 reads=[bregs[b], r_small], writes=r_tA_all)
            P.op("act", I_act(tA[0:8, ws], tA[0:8, ws], AF.Ln, bias=1.0), reads=r_tA_all, writes=r_tA_all)
            P.op("dve", I_ts(abT[0:8, ws], tA[0:8, ws], nega[0:8, 0:1], None, ALU.mult),
                 reads=r_tA_all + [r_nega], writes=[r_abT])
            P.op("act", I_act(abT[32:40, ws], banks[b][32:40, :], AF.Exp, scale=-1.0),
                 reads=[bregs[b]], writes=[r_abT])
            P.op("dve", I_ts(abT[32:40, ws], abT[32:40, ws], 1.0, None, ALU.add), reads=[r_abT], writes=[r_abT])
            P.op("dve", I_recip(abT[32:40, ws], abT[32:40, ws]), reads=[r_abT], writes=[r_abT])
        src_, dst_ = abT, tA
        for st_ in (1, 2, 4, 8, 16, 32):
            sv = src_[0:8, :].rearrange("p (n c) -> p n c", c=64)
            dv = dst_[0:8, :].rearrange("p (n c) -> p n c", c=64)
            P.op("dve", I_tt(dv[:, :, st_:64], sv[:, :, st_:64], sv[:, :, 0:64 - st_], ALU.add),
                 reads=[r_abT] + r_tA_all, writes=[r_abT] + r_tA_all)
            P.op("dve", I_copy(dv[:, :, 0:st_], sv[:, :, 0:st_]), reads=[r_abT] + r_tA_all, writes=[r_abT] + r_tA_all)
            src_, dst_ = dst_, src_
        assert src_ is abT
        for blk in range(NB):
            P.op("pe", I_mmgroup(banks[2][:, blk * 16:(blk + 1) * 16],
                                 [(hT[:, k, blk * 128:(blk + 1) * 128], wabt[:, k, :]) for k in range(KC)]),
                 reads=[r_wab] + [self.r_h[k][blk // 4] for k in range(KC)], writes=[bregs[2]])
        abv = banks[2][:, 0:256].rearrange("p (b c) -> p b c", c=16)
        t1, r_t1 = T()
        t2, r_t2 = T()
        v3 = lambda a: a.rearrange("p (b c) -> p b c", c=8)
        P.op("dve", I_tt(v3(t1), abv[:, :, 0:8], v3(dtb_rep), ALU.add), reads=[bregs[2], r_small], writes=[r_t1])
        P.op("act", I_act(t1, t1, AF.Exp), reads=[r_t1], writes=[r_t1])
        P.op("act", I_act(t1, t1, AF.Ln, bias=1.0), reads=[r_t1], writes=[r_t1])
        P.op("dve", I_tt(t1, t1, nega[:, 1:129], ALU.mult), reads=[r_t1, r_nega], writes=[r_t1])
        P.op("act", I_act(v3(beta_tok), abv[:, :, 8:16], AF.Exp, scale=-1.0), reads=[bregs[2]], writes=[r_tok])
        P.op("dve", I_ts(beta_tok, beta_tok, 1.0, None, ALU.add), reads=[r_tok], writes=[r_tok])
        P.op("dve", I_recip(beta_tok, beta_tok), reads=[r_tok], writes=[r_tok])
        P.op("pe", I_mm(qb[(4, 0)], TRI, t1), reads=[r_cst, r_t1], writes=[qr[(4, 0)]])
        P.op("act", I_act(gc_tok, qb[(4, 0)], AF.Copy), reads=[qr[(4, 0)]], writes=[r_tok])
        P.op("pe", I_mm(qb[(4, 1)], SELL, gc_tok), reads=[r_cst, r_tok], writes=[qr[(4, 1)]])
        P.op("dve", I_tt(t2, qb[(4, 1)], gc_tok, ALU.subtract), reads=[qr[(4, 1)], r_tok], writes=[r_t2])
        P.op("act", I_act(kt_tok, t2, AF.Exp), reads=[r_t2], writes=[r_tok])
        P.op("act", I_act(t2, gc_tok, AF.Exp), reads=[r_tok], writes=[r_t2])
        P.op("dve", I_tt(bg_tok, beta_tok, t2, ALU.mult), reads=[r_tok, r_t2], writes=[r_tok])

        pjc = [0]

        def gen_h2(h):
            hs = h % 2
            wq = wh[(2 * h) % 3]
            wv = wh[(2 * h + 1) % 3]
            r_wq, r_wv = r_wh[(2 * h) % 3], r_wh[(2 * h + 1) % 3]
            taps = hv[hs]
            items = [(X, w) for X in range(3) for w in range(NW)]
            pb = {}

            def stA(i):
                X, w = items[i]
                wsrc, r_wsrc = (wq, r_wq) if X < 2 else (wv, r_wv)
                co = (X % 2) * 128
                ws = slice(w * WIN, (w + 1) * WIN)
                b = 4 + i % 2
                pb[i] = b
                P.op("pe", I_mmgroup(banks[b][:], [(wsrc[:, k, co:co + 128], hT[:, k, ws]) for k in range(KC)]),
                     reads=[r_wsrc] + [self.r_h[k][w] for k in range(KC)], writes=[bregs[b]])

            def stB(i):
                X, w = items[i]
                b = pb[i]
                P.op("act", I_act(raw[:, 4 + w * WIN:4 + (w + 1) * WIN], banks[b][:], AF.Copy),
                     reads=[bregs[b]], writes=[r_raw_w[1 + w]])

            def stC(i):
                X, w = items[i]
                ws = slice(w * WIN, (w + 1) * WIN)
                P.op("dve", I_ts(acc[:, ws], raw[:, 4 + w * WIN:4 + (w + 1) * WIN],
                                 taps[:, X * 4 + 3:X * 4 + 4], None, ALU.mult),
                     reads=[r_raw_w[1 + w], r_hv[hs]], writes=[r_acc_w[w]])
                for tp in (2, 1, 0):
                    P.op("dve", I_stt(acc[:, ws], raw[:, 1 + tp + w * WIN:1 + tp + (w + 1) * WIN],
                                      taps[:, X * 4 + tp:X * 4 + tp + 1], acc[:, ws], ALU.mult, ALU.add),
                         reads=[r_raw_w[w], r_raw_w[1 + w], r_hv[hs], r_acc_w[w]], writes=[r_acc_w[w]])

            def stD1(i):
                X, w = items[i]
                ws = slice(w * WIN, (w + 1) * WIN)
                if X == 2:
                    P.op("act", I_act(vT[:, ws], acc[:, ws], AF.Silu), reads=[r_acc_w[w]], writes=[r_v])
                else:
                    P.op("act", I_act(acc[:, ws], acc[:, ws], AF.Silu), reads=[r_acc_w[w]], writes=[r_acc_w[w]])
                    P.op("act", I_act(sq[1], acc[:, ws], AF.Square), reads=[r_acc_w[w]], writes=[r_sq[1]])

            def stD2(i):
                X, w = items[i]
                if X < 2:
                    P.op("pe", I_mm(banks[3][:], ones[:], sq[1]), reads=[r_sq[1], self.r_const], writes=[bregs[3]])

            def stD3(i):
                X, w = items[i]
                if X < 2:
                    P.op("act", I_act(rt, banks[3][:], AF.Ln, bias=EPS, scale=1.0), reads=[bregs[3]], writes=[r_rt])
                    P.op("act", I_act(rstd, rt, AF.Exp, scale=-0.5), reads=[r_rt], writes=[r_rstd])

            def stD4(i):
                X, w = items[i]
                if X < 2:
                    ws = slice(w * WIN, (w + 1) * WIN)
                    dst, r_dst = (qT, r_q) if X == 0 else (kT, r_k)
                    sc = (128.0 ** -0.5) if X == 0 else 1.0
                    P.op("dve", I_stt(dst[:, ws], acc[:, ws], sc, rstd, ALU.mult, ALU.mult),
                         reads=[r_acc_w[w], r_rstd], writes=[r_dst])

            stages = [stA, stB, stC, stD1, stD2, stD3, stD4]
            n = len(items)
            for t in range(n + len(stages) - 1):
                for si in range(len(stages) - 1, -1, -1):
                    i = t - si
                    if 0 <= i < n:
                        stages[si](i)
                yield

        def emit_z(h):
            wv = wh[(2 * h + 1) % 3]
            r_wv = r_wh[(2 * h + 1) % 3]
            for w in range(NW):
                ws = slice(w * WIN, (w + 1) * WIN)
                b = pjc[0] % 2
                pjc[0] += 1
                P.op("pe", I_mmgroup(banks[b][:], [(wv[:, k, 128:256], hT[:, k, ws]) for k in range(KC)]),
                     reads=[r_wv] + [self.r_h[k][w] for k in range(KC)], writes=[bregs[b]])
                P.op("act", I_act(zt, banks[b][:], AF.Silu), reads=[bregs[b]], writes=[r_zt])
                P.op("dve", I_ts(zsT[:, ws], zt, normw_col, None, ALU.mult), reads=[r_zt, r_small], writes=[r_zs])

        def gen_scan(h):
            hs = h % 2
            P.op("dve", I_memset(Sf, 0.0), writes=[r_Sf])
            P.op("dve", I_memset(Sb, 0.0), writes=[r_Sb])
            P.op("dve", I_memset(vnew, 0.0), writes=[r_vnew])
            for n in range(2 * NB):
                if n > 0:
                    yield
                blk, half = n // 2, n % 2
                ps = slice(64 * half, 64 * half + 64)
                cs = slice(n * 64, (n + 1) * 64)
                w = n // 8
                ob, r_ob = banks[2], bregs[2]
                bs = slice(blk * 128, (blk + 1) * 128)
                P.op("pe", I_mm(banks[7][:, 0:128], wT[:, bs], Sb), reads=[r_wT[blk], r_Sb], writes=[qr[(7, 0)]])
                P.op("dve", I_tt(vnew[ps, :], u[ps, blk, :], banks[7][ps, 0:128], ALU.subtract),
                     reads=[r_u[blk], qr[(7, 0)]], writes=[r_vnew])
                oc = (n % 8) * 64
                P.op("pe", I_mm(qb[(7, 1)], ktail[:, blk, half, :], vnew[:, :]), reads=[r_kt[blk], r_vnew],
                     writes=[qr[(7, 1)]])
                P.op("pe", I_mmgroup(ob[:, oc:oc + 64], [(Sb, qdT[:, cs]),
                                                        (vnew[:, :], attnT[:, blk, 64 * half:64 * half + 64])]),
                     reads=[r_Sb, r_qdb[blk], r_vnew, r_at[blk]], writes=[r_ob])
                P.op("dve", I_stt(Sb, Sf, cdv[:, n:n + 1], qb[(7, 1)], ALU.mult, ALU.add),
                     reads=[r_Sf, r_cdv, qr[(7, 1)]], writes=[r_Sb])
                P.op("dve", I_stt(Sf, Sf, cdv[:, n:n + 1], qb[(7, 1)], ALU.mult, ALU.add),
                     reads=[r_Sf, r_cdv, qr[(7, 1)]], writes=[r_Sf])
                if n % 8 == 7:
                    ws = slice(w * WIN, (w + 1) * WIN)
                    P.op("act", I_act(osb, ob[:], AF.Copy), reads=[r_ob], writes=[r_osb])
                    P.op("act", I_act(sq[0], osb, AF.Square), reads=[r_osb], writes=[r_sq[0]])
                    P.op("pe", I_mm(banks[0][:], ones[:], sq[0]), reads=[r_sq[0], self.r_const], writes=[bregs[0]])
                    P.op("act", I_act(rt2, banks[0][:], AF.Ln, bias=EPS, scale=1.0 / 128), reads=[bregs[0]],
                         writes=[r_rt2])
                    P.op("act", I_act(rstd2, rt2, AF.Exp, scale=-0.5), reads=[r_rt2], writes=[r_rstd2])
                    P.op("dve", I_tt(osb, osb, rstd2, ALU.mult), reads=[r_osb, r_rstd2], writes=[r_osb])
                    P.op("dve", I_tt(ogT, osb, zsT[:, ws], ALU.mult), reads=[r_osb, r_zs], writes=[r_og])
                    for dch in range(KC):
                        P.op("pe", I_mm(banks[1][:], wo[hs][:, dch * 128:(dch + 1) * 128], ogT),
                             reads=[r_wo[hs], r_og], writes=[bregs[1]])
                        P.op("dve", I_tt(xT[:, dch, ws], banks[1][:], xT[:, dch, ws], ALU.add),
                             reads=[bregs[1], self.r_x[dch][w]], writes=[self.r_x[dch][w]])

        prev_scan = None
        for h in range(8):
            g2 = gen_h2(h)
            if prev_scan is not None:
                a_live, b_live = True, True
                while a_live or b_live:
                    if a_live:
                        try:
                            next(prev_scan)
                        except StopIteration:
                            a_live = False
                    if b_live:
                        try:
                            next(g2)
                        except StopIteration:
                            b_live = False
            else:
                for _ in g2:
                    pass
            emit_z(h)
            if h + 1 < 8:
                load_head(h + 1)
            for eng_ in ("pe", "act", "dve"):
                P.wait_all(eng_, r_raw_w + r_acc_w)
            Wt = [raw[:, 4 + i * 512:4 + (i + 1) * 512] for i in range(4)] + \
                 [acc[:, i * 512:(i + 1) * 512] for i in range(4)]
            v4 = lambda a: a.rearrange("p (b i) -> p b i", b=4)
            bmid = lambda a: a.unsqueeze(1).broadcast_to([128, 4, 128])
            for gi in range(NB // 4):
                b0 = gi * 4
                gsl = slice(b0 * 128, (b0 + 4) * 128)
                bsl = [slice((b0 + bi) * 128, (b0 + bi + 1) * 128) for bi in range(4)]
                qs_ = [slice(bi * 128, (bi + 1) * 128) for bi in range(4)]

                def tokb(t, pp=slice(0, 128)):
                    return t[pp, b0 * 8 + h:(b0 + 3) * 8 + h + 1:8].unsqueeze(2).broadcast_to([pp.stop - pp.start, 4, 128])
                W_dd, W_eg, W_gm, W_gs, W_B, W_A, W_M, W_X = Wt
                rW = r_W
                P.op("pe", I_multi([(banks[4][:, qs_[bi]], SEL[0:8, h, :], abT[0:8, bsl[bi]]) for bi in range(4)]),
                     reads=[r_cst, r_abT], writes=[bregs[4]])
                P.op("pe", I_multi([(banks[5][:, qs_[bi]], SEL[32:40, h, :], abT[32:40, bsl[bi]]) for bi in range(4)]),
                     reads=[r_cst, r_abT], writes=[bregs[5]])
                P.op("pe", I_multi([(banks[6][:, qs_[bi]], kT[:, bsl[bi]], kT[:, bsl[bi]]) for bi in range(4)]),
                     reads=[r_k], writes=[bregs[6]])
                P.op("pe", I_multi([(banks[7][:, qs_[bi]], kT[:, bsl[bi]], qT[:, bsl[bi]]) for bi in range(4)]),
                     reads=[r_k, r_q], writes=[bregs[7]])
                P.op("dve", I_tt(v4(W_dd), v4(banks[4][:]), tokb(gc_tok), ALU.subtract), reads=[bregs[4], r_tok],
                     writes=[rW[0]])
                P.op("dve", I_ts(W_dd, W_dd, 0.0, None, ALU.min), reads=[rW[0]], writes=[rW[0]])
                P.op("act", I_act(W_dd, W_dd, AF.Exp), reads=[rW[0]], writes=[rW[0]])
                P.op("act", I_act(W_eg, banks[4][:], AF.Exp), reads=[bregs[4]], writes=[rW[1]])
                P.op("dve", I_tt(v4(W_gm), v4(W_dd), bmid(MUI), ALU.mult), reads=[rW[0], r_cst], writes=[rW[2]])
                P.op("dve", I_tt(attnT[:, b0:b0 + 4, :], v4(banks[7][:]), v4(W_gm), ALU.mult),
                     reads=[bregs[7], rW[2]], writes=[r_at[b0 + bi] for bi in range(4)])
                P.op("dve", I_tt(v4(W_gs), v4(W_dd), bmid(MUS), ALU.mult), reads=[rW[0], r_cst], writes=[rW[3]])
                P.op("dve", I_tt(W_gs, banks[5][:], W_gs, ALU.mult), reads=[bregs[5], rW[3]], writes=[rW[3]])
                P.op("dve", I_tt(W_B, banks[6][:], W_gs, ALU.mult), reads=[bregs[6], rW[3]], writes=[rW[4]])
                P.op("dve", I_tt(qdT[:, gsl], qT[:, gsl], W_eg, ALU.mult), reads=[r_q, rW[1]],
                     writes=[r_qdb[b0 + bi] for bi in range(4)])
                P.op("act", I_act(cdv[:, 2 * b0:2 * b0 + 8], W_eg[:, 63:512:64], AF.Copy), reads=[rW[1]],
                     writes=[r_cdv])
                P.op("pe", I_multi_tr([(banks[0][:, qs_[bi]], W_B[:, qs_[bi]], IDN) for bi in range(4)]),
                     reads=[rW[4], r_cst], writes=[bregs[0]])
                P.op("act", I_act(W_A, banks[0][:], AF.Copy), reads=[bregs[0]], writes=[rW[5]])
                P.op("dve", I_tt(v4(W_M), bmid(IDN), v4(W_B), ALU.subtract), reads=[r_cst, rW[4]], writes=[rW[6]])
                Bc, iB, Ac, iA = W_B, 4, W_A, 5
                free = [(W_dd, 0), (W_eg, 1)]
                for lev in range(1, 6):
                    An, iAn = free.pop(0)
                    P.op("pe", I_multi([(banks[1][:, qs_[bi]], Bc[:, qs_[bi]], Ac[:, qs_[bi]]) for bi in range(4)]),
                         reads=[rW[iB], rW[iA]], writes=[bregs[1]])
                    if lev < 5:
                        Bn, iBn = free.pop(0)
                        P.op("pe", I_multi([(banks[2][:, qs_[bi]], Ac[:, qs_[bi]], Bc[:, qs_[bi]])
                                            for bi in range(4)]),
                             reads=[rW[iB], rW[iA]], writes=[bregs[2]])
                    P.op("dve", I_copy(An, banks[1][:]), reads=[bregs[1]], writes=[rW[iAn]])
                    if lev < 5:
                        P.op("act", I_act(Bn, banks[2][:], AF.Copy), reads=[bregs[2]], writes=[rW[iBn]])
                    P.op("pe", I_multi([(banks[3][:, qs_[bi]], An[:, qs_[bi]], W_M[:, qs_[bi]]) for bi in range(4)]),
                         reads=[rW[iAn], rW[6]], writes=[bregs[3]])
                    P.op("dve", I_tt(W_M, banks[3][:], W_M, ALU.add), reads=[bregs[3], rW[6]], writes=[rW[6]])
                    free.append((Ac, iA))
                    Ac, iA = An, iAn
                    if lev < 5:
                        free.append((Bc, iB))
                        Bc, iB = Bn, iBn
                ktp = banks[0][:].bitcast(BF16)[:, 0:512]
                vtp = banks[1][:].bitcast(BF16)[:, 0:512]
                P.op("pe", I_multi_tr([(ktp[:, qs_[bi]], kT[:, bsl[bi]], idb) for bi in range(4)]),
                     reads=[r_k, r_cst], writes=[bregs[0]])
                P.op("pe", I_multi_tr([(vtp[:, qs_[bi]], vT[:, bsl[bi]], idb) for bi in range(4)]),
                     reads=[r_v, r_cst], writes=[bregs[1]])
                W_rw, W_ru = W_gm, W_gs
                P.op("dve", I_tt(v4(W_rw), v4(ktp), tokb(bg_tok), ALU.mult), reads=[bregs[0], r_tok], writes=[rW[2]])
                P.op("dve", I_tt(v4(W_ru), v4(vtp), tokb(beta_tok), ALU.mult), reads=[bregs[1], r_tok], writes=[rW[3]])
                for hf in range(2):
                    pp = slice(64 * hf, 64 * hf + 64)
                    P.op("dve", I_tt(ktail[pp, b0:b0 + 4, hf, :], v4(ktp)[pp], tokb(kt_tok, pp), ALU.mult),
                         reads=[bregs[0], r_tok], writes=[r_kt[b0 + bi] for bi in range(4)])
                P.op("pe", I_multi([(banks[2][:, qs_[bi]], W_M[:, qs_[bi]], W_ru[:, qs_[bi]]) for bi in range(4)]),
                     reads=[rW[6], rW[3]], writes=[bregs[2]])
                P.op("act", I_act(u[:, b0:b0 + 4, :], v4(banks[2][:]), AF.Copy), reads=[bregs[2]],
                     writes=[r_u[b0 + bi] for bi in range(4)])
                P.op("pe", I_multi([(banks[3][:, qs_[bi]], W_rw[:, qs_[bi]], W_M[:, qs_[bi]]) for bi in range(4)]),
                     reads=[rW[6], rW[2]], writes=[bregs[3]])
                P.op("dve", I_copy(wT[:, gsl], banks[3][:]), reads=[bregs[3]],
                     writes=[r_wT[b0 + bi] for bi in range(4)])
            for eng_ in ("pe", "act", "dve"):
                P.wait_all(eng_, r_W)
            prev_scan = gen_scan(h)
        for _ in prev_scan:
            pass
        allregs = r_raw_w + r_acc_w + [r_cst, r_small, r_nega, r_abT, r_tok, r_wab, r_q, r_k, r_v, r_qd, r_zs, r_cdv,
                   r_Sf, r_Sb, r_vnew, r_osb, r_og, r_rt2, r_rstd2]
        self.phase_barrier([allregs, r_wh, r_wo, r_hv, r_u, r_wT, r_at, r_kt, r_qdb, r_tmp, r_W, bregs])
        ar.reset(m)

    def phase_barrier(self, reg_lists):
        regs = [r for rl in reg_lists for r in rl]
        for n in ("pe", "act", "dve", "pool", "sp"):
            self.P.wait_all(n, regs)

    def build(self):
        nc, P, st = self.nc, self.P, self.stack
        self.s_wgu = [P.dma_src() for _ in range(4)]
        self.s_wd = [P.dma_src() for _ in range(3)]
        self.s_awb = [P.dma_src() for _ in range(3)]
        self.s_awout = P.dma_src()
        self.s_eb = [P.dma_src() for _ in range(2)]
        self.s_bc = [P.dma_src() for _ in range(3)]
        self.s_wh = [P.dma_src() for _ in range(3)]
        self.s_wo = [P.dma_src() for _ in range(2)]
        self.s_hv = [P.dma_src() for _ in range(2)]
        self.emit_consts()
        for s in range(self.n_seq):
            self.emit_load_x(s)
            for ph in self.phases:
                if ph[0] == "ffn":
                    self.emit_ffn(ph[1], ph[2])
                elif ph[0] == "mixa":
                    self.emit_mixa(ph[1], ph[2])
                elif ph[0] == "mixb":
                    self.emit_mixb(ph[1], ph[2])
                elif ph[0] == "final":
                    self.emit_rmsnorm(12, range(NW), out_f32_inplace=True)
                else:
                    raise ValueError(ph)
            self.emit_store_x(s)
        P.wait_all("sp", [r for k in range(KC) for r in self.r_x[k]])
        with nc.Block() as block:
            P.replay(block)
        st.close()
        return nc


def full_phases():
    ph = []
    for i in range(DEPTH):
        ph.append(("ffn", 2 * i, 3 * i))
        ph.append(("mixa" if i % 2 == 0 else "mixb", i // 2, 3 * i + 1))
        ph.append(("ffn", 2 * i + 1, 3 * i + 2))
    ph.append(("final",))
    return ph


def prep_weights(inp):
    f32 = np.float32
    out = {}
    ng = np.concatenate([np.asarray(inp["norm_g"], f32).reshape(12, D), np.asarray(inp["final_g"], f32).reshape(1, D)], 0)
    out["normg"] = np.ascontiguousarray(ng.reshape(13, KC, 128).transpose(2, 0, 1).reshape(128, 13 * KC))
    wg = np.asarray(inp["ffn_w_gate"], f32).reshape(2 * DEPTH, KC, 128, FC, 128)
    wu = np.asarray(inp["ffn_w_up"], f32).reshape(2 * DEPTH, KC, 128, FC, 128)
    wgu = np.stack([wg, wu], 0)
    wgu = wgu.transpose(1, 4, 3, 0, 2, 5)
    out["wgu"] = np.ascontiguousarray(wgu).reshape(2 * DEPTH, FC, 128, 2 * KC * 128)
    wd = np.asarray(inp["ffn_w_down"], f32).reshape(2 * DEPTH, FC, 128, KC, 128)
    wd = wd.transpose(0, 3, 2, 1, 4)
    out["wd"] = np.ascontiguousarray(wd).reshape(2 * DEPTH, KC, 128, FC * 128)
    awin = np.asarray(inp["a_w_in"], f32).reshape(2, KC, 128, 3, 3, 4, 128)
    awin = awin.transpose(0, 5, 2, 1, 3, 4, 6)
    out["awin"] = np.ascontiguousarray(awin).reshape(2, 4, 128, KC, 3 * 384)
    awout = np.asarray(inp["a_w_out"], f32).reshape(2, 4, 128, D).transpose(0, 2, 1, 3)
    out["awout"] = np.ascontiguousarray(awout)
    out["bt"] = bias_table(np.asarray(inp["rel_bias"], f32))
    bw = np.asarray(inp["b_w_in"], f32)
    w4 = bw[:, :, :4096].reshape(2, KC, 128, 4, 8, 128).transpose(0, 4, 2, 1, 3, 5)
    out["bwin"] = np.ascontiguousarray(w4).reshape(2, 8, 128, KC, 512)
    wa = bw[:, :, 4096:4104].reshape(2, KC, 128, 8).transpose(0, 2, 1, 3)
    wb_ = bw[:, :, 4104:4112].reshape(2, KC, 128, 8).transpose(0, 2, 1, 3)
    wab = np.zeros((2, 128, KC, 64), f32)
    wab[..., 0:8] = wa
    wab[..., 32:40] = wb_
    out["wab"] = wab
    out["wabt"] = np.ascontiguousarray(np.concatenate([wa, wb_], -1))
    cw = np.asarray(inp["b_conv_w"], f32).reshape(2, 4, 3, 8, 128).transpose(0, 3, 4, 2, 1)
    out["bconv"] = np.ascontiguousarray(cw).reshape(2, 8, 128, 12)
    sm = np.zeros((2, 128, 259), f32)
    dtb = np.asarray(inp["b_dt_bias"], f32)
    alog = np.asarray(inp["b_a_log"], f32)
    sm[:, 0:8, 0] = dtb
    sm[:, 0:8, 1] = alog
    sm[:, :, 2] = np.asarray(inp["b_norm_w"], f32)
    sm[:, :, 3:131] = np.tile(dtb, (1, 16))[:, None, :]
    sm[:, :, 131:259] = np.tile(alog, (1, 16))[:, None, :]
    out["bsmall"] = sm
    out["bwout"] = np.ascontiguousarray(np.asarray(inp["b_w_out"], f32).reshape(2, 8, 128, D))
    out["bconst"] = b_consts()
    return out


def b_consts():
    i = np.arange(128)
    same = (i[:, None] // 64) == (i[None, :] // 64)
    c = np.zeros((128, 13 * 128), np.float32)
    c[:, 0:128] = (same & (i[:, None] <= i[None, :]))
    c[:, 128:256] = (i[:, None] == (i[None, :] // 64) * 64 + 63)
    c[:, 256:384] = np.eye(128)
    c[:, 384:512] = (same & (i[None, :] >= i[:, None]))
    c[:, 512:640] = (same & (i[None, :] > i[:, None]))
    sel = np.zeros((128, 8, 128), np.float32)
    for h in range(8):
        sel[h, h, :] = 1.0
        sel[32 + h, h, :] = 1.0
    c[:, 640:] = sel.reshape(128, 1024)
    return c


def t5_bucket_np(dist):
    n = dist.astype(np.float32)
    large = np.float32(16.0) + (np.log(np.maximum(n, np.float32(1.0)) / np.float32(16.0)).astype(np.float32)
                                / np.float32(np.log(2048.0 / 16.0)) * np.float32(16.0)).astype(np.float32)
    large = np.minimum(large.astype(np.int32), 31)
    return np.where(dist < 16, dist, large)


def bias_table(rel_bias):
    kk = np.arange(128)[:, None]
    jj = np.arange(256)[None, :]
    rel = jj - kk
    valid = (rel >= 0) & (rel <= 128)
    bt = np.full((4, 128, 6, 256), -30000.0, np.float32)
    for g, dil in enumerate((1, 4, 16)):
        bk = t5_bucket_np(np.maximum(rel, 0) * dil)
        for pair in range(4):
            for hp in range(2):
                tab = rel_bias[:, g * 8 + 2 * pair + hp][bk]
                bt[pair, :, g * 2 + hp, :] = np.where(valid, tab, np.float32(-30000.0))
    return bt


def x_to_dev(x_seqs):
    n = x_seqs.shape[0]
    return np.ascontiguousarray(x_seqs.reshape(n, S, KC, 128).transpose(0, 2, 3, 1))


def x_from_dev(o):
    n = o.shape[0]
    return np.ascontiguousarray(o.transpose(0, 3, 1, 2).reshape(n, S, D))


_PROG_CACHE = {}


def get_prog(n_seq, phases):
    key = (n_seq, tuple(phases))
    if key not in _PROG_CACHE:
        _PROG_CACHE[key] = Builder(n_seq, list(phases)).build()
    return _PROG_CACHE[key]


def run(inputs, n_seq_per_core, phases, n_cores=N_CORES, trace=False, x_override=None):
    w = prep_weights(inputs)
    x = np.asarray(inputs["x"] if x_override is None else x_override, np.float32)
    nc = get_prog(n_seq_per_core, phases)
    in_maps = []
    for c in range(n_cores):
        m = dict(w)
        m["xT"] = x_to_dev(x[c * n_seq_per_core:(c + 1) * n_seq_per_core])
        in_maps.append(m)
    res = run_bass_kernel_spmd(nc, in_maps, core_ids=list(range(n_cores)), trace=trace)
    outs = [x_from_dev(r["outT"]) for r in res.results]
    return np.concatenate(outs, 0), res


def kernel(**inputs):
    out, _ = run(inputs, 4, full_phases())
    return out
```

```python
import numpy as np
import concourse.bass as bass
import concourse.mybir as mybir
from concourse.bass_utils import run_bass_kernel_spmd

F32 = mybir.dt.float32
BF16 = mybir.dt.bfloat16
FP32R = mybir.dt.float32r
DBL_R = False
AF = mybir.ActivationFunctionType
ALU = mybir.AluOpType

D = 1024
S = 2048
DFF = 2816
KC = D // 128
FC = DFF // 128
WIN = 512
NW = S // WIN
DEPTH = 4
EPS = 1e-6
N_CORES = 8
MIXB_STOP = None
OP_LIMIT = None


class Src:
    def __init__(self, sem, step):
        self.sem = sem
        self.step = step
        self.val = 0


class Reg:
    __slots__ = ("name", "w", "r")

    def __init__(self, name=""):
        self.name = name
        self.w = None
        self.r = {}


class Eng:
    def __init__(self, name, src):
        self.name = name
        self.src = src
        self.seen = {}
        self.ops = []


class Prog:
    def __init__(self, nc, stack):
        self.nc = nc
        self.stack = stack
        self.engs = {}
        for n in ("pe", "act", "dve", "pool", "sp"):
            sem = stack.enter_context(nc.semaphore("sem_" + n))
            self.engs[n] = Eng(n, Src(sem, 1))
        self.n_dma_src = 0

    def dma_src(self):
        self.n_dma_src += 1
        sem = self.stack.enter_context(self.nc.semaphore("dsem%d" % self.n_dma_src))
        return Src(sem, 16)

    def _waits(self, eng, reads, writes):
        deps = {}
        for r in reads:
            if r.w is not None:
                s, v = r.w
                if deps.get(s, 0) < v:
                    deps[s] = v
        for w in writes:
            if w.w is not None:
                s, v = w.w
                if deps.get(s, 0) < v:
                    deps[s] = v
            for s, v in w.r.items():
                if deps.get(s, 0) < v:
                    deps[s] = v
        for s, v in deps.items():
            if s is eng.src and eng.name == "pe":
                continue
            if eng.seen.get(s, 0) >= v:
                continue
            eng.seen[s] = v
            eng.ops.append(("wait", s.sem, v))

    def op(self, eng, fn, reads=(), writes=()):
        if getattr(self, "count_active", False):
            self.cnt = getattr(self, "cnt", 0) + 1
            if OP_LIMIT is not None and self.cnt > OP_LIMIT:
                return
        e = self.engs[eng]
        self._waits(e, reads, writes)
        e.src.val += 1
        v = e.src.val
        e.ops.append(("ins", fn, e.src.sem, 1))
        for r in reads:
            if r.r.get(e.src, 0) < v:
                r.r[e.src] = v
        for w in writes:
            w.w = (e.src, v)
            w.r = {}

    def dma(self, eng, src, out, in_, reads=(), writes=(), **kw):
        e = self.engs[eng]
        self._waits(e, reads, writes)
        src.val += 16
        v = src.val
        e.ops.append(("ins", lambda q: q.dma_start(out=out, in_=in_, **kw), src.sem, 16))
        for r in reads:
            if r.r.get(src, 0) < v:
                r.r[src] = v
        for w in writes:
            w.w = (src, v)
            w.r = {}

    def wait_all(self, eng, regs):
        e = self.engs[eng]
        self._waits(e, (), regs)

    def replay(self, block):
        handles = {"pe": block.tensor, "act": block.scalar, "dve": block.vector,
                   "pool": block.gpsimd, "sp": block.sync}
        for n, e in self.engs.items():
            ops = e.ops

            def body(q, ops=ops):
                for o in ops:
                    if o[0] == "wait":
                        q.wait_ge(o[1], o[2])
                    else:
                        o[1](q).then_inc(o[2], o[3])
            handles[n](body)


class Arena:
    def __init__(self, ap_bf16, nbytes):
        self.ap = ap_bf16
        self.nbytes = nbytes
        self.off = 0

    def mark(self):
        return self.off

    def reset(self, m=0):
        self.off = m

    def alloc(self, shape, dtype, parts=128):
        n = 1
        for s in shape:
            n *= s
        esz = 4 if dtype == F32 else 2
        nb = (n * esz + 31) // 32 * 32
        assert self.off + nb <= self.nbytes, ("arena overflow", self.off, nb, self.nbytes)
        a = self.ap[0:parts, self.off // 2:(self.off + n * esz) // 2]
        self.off += nb
        if dtype == F32:
            a = a.bitcast(F32)
        if len(shape) == 2:
            a = a.rearrange("p (a b) -> p a b", a=shape[0])
        elif len(shape) == 3:
            a = a.rearrange("p (a b c) -> p a b c", a=shape[0], b=shape[1])
        elif len(shape) == 4:
            a = a.rearrange("p (a b c d) -> p a b c d", a=shape[0], b=shape[1], c=shape[2])
        return a


def I_act(out, in_, func, **kw):
    return lambda q: q.activation(out=out, in_=in_, func=func, **kw)


def I_tt(out, in0, in1, op):
    return lambda q: q.tensor_tensor(out=out, in0=in0, in1=in1, op=op)


def I_stt(out, in0, scalar, in1, op0, op1):
    return lambda q: q.scalar_tensor_tensor(out=out, in0=in0, scalar=scalar, in1=in1, op0=op0, op1=op1)


def I_ts(out, in0, s1, s2, op0, op1=None):
    if op1 is None:
        return lambda q: q.tensor_scalar(out=out, in0=in0, scalar1=s1, scalar2=None, op0=op0)
    return lambda q: q.tensor_scalar(out=out, in0=in0, scalar1=s1, scalar2=s2, op0=op0, op1=op1)


def I_recip(out, in_):
    return lambda q: q.reciprocal(out=out, in_=in_)


def I_copy(out, in_):
    return lambda q: q.tensor_copy(out=out, in_=in_)


def I_memset(out, val):
    return lambda q: q.memset(out, val)


def I_mm(out, lhsT, rhs, start=True, stop=True):
    return lambda q: q.matmul(out, lhsT=lhsT, rhs=rhs, start=start, stop=stop)


def I_mmgroup(out, pairs):
    pairs = list(pairs)

    def f(q):
        ins = None
        n = len(pairs)
        for i, (l, r) in enumerate(pairs):
            ins = q.matmul(out, lhsT=l, rhs=r, start=(i == 0), stop=(i == n - 1))
        return ins
    return f


def I_multi(items, r=False):
    items = list(items)
    if r and DBL_R:
        items = [(o, l.bitcast(FP32R), rr.bitcast(FP32R)) for (o, l, rr) in items]

    def f(q):
        ins = None
        for (o, l, r) in items:
            ins = q.matmul(o, lhsT=l, rhs=r, start=True, stop=True)
        return ins
    return f


def I_multi_tr(items):
    items = list(items)

    def f(q):
        ins = None
        for (o, i, ident) in items:
            ins = q.transpose(o, i, ident)
        return ins
    return f


def I_transpose(out, in_, ident):
    return lambda q: q.transpose(out, in_, ident)


class Builder:
    def __init__(self, n_seq, phases):
        from contextlib import ExitStack
        self.n_seq = n_seq
        self.phases = phases
        self.stack = ExitStack()
        nc = self.nc = bass.Bass("TRN2", target_bir_lowering=False)
        st = self.stack
        self.P = Prog(nc, st)
        dt = nc.dram_tensor
        self.d_xT = dt("xT", [n_seq, KC, 128, S], F32, kind="ExternalInput").ap()
        self.d_out = dt("outT", [n_seq, KC, 128, S], F32, kind="ExternalOutput").ap()
        self.d_normg = dt("normg", [128, 13 * KC], F32, kind="ExternalInput").ap()
        self.d_wgu = dt("wgu", [2 * DEPTH, FC, 128, 2 * KC * 128], F32, kind="ExternalInput").ap()
        self.d_wd = dt("wd", [2 * DEPTH, KC, 128, FC * 128], F32, kind="ExternalInput").ap()
        self.d_awin = dt("awin", [2, 4, 128, KC, 3 * 384], F32, kind="ExternalInput").ap()
        self.d_awout = dt("awout", [2, 128, 4, D], F32, kind="ExternalInput").ap()
        self.d_bt = dt("bt", [4, 128, 6, 256], F32, kind="ExternalInput").ap()
        self.d_bwin = dt("bwin", [2, 8, 128, KC, 512], F32, kind="ExternalInput").ap()
        self.d_wab = dt("wab", [2, 128, KC, 64], F32, kind="ExternalInput").ap()
        self.d_wabt = dt("wabt", [2, 128, KC, 16], F32, kind="ExternalInput").ap()
        self.d_bconv = dt("bconv", [2, 8, 128, 12], F32, kind="ExternalInput").ap()
        self.d_bsmall = dt("bsmall", [2, 128, 259], F32, kind="ExternalInput").ap()
        self.d_bwout = dt("bwout", [2, 8, 128, D], F32, kind="ExternalInput").ap()
        self.d_bconst = dt("bconst", [128, 13 * 128], F32, kind="ExternalInput").ap()
        sb = lambda name, shape, dtype: st.enter_context(nc.sbuf_tensor(name, shape, dtype))
        self.xT = sb("xTs", [128, KC, S], F32)
        self.hT = sb("hTs", [128, KC, S], BF16)
        self.normg = sb("normg_s", [128, 13 * KC], F32)
        self.ones_bf = sb("ones_bf", [128, 128], BF16)
        self.n_sq = [sb("n_sq%d" % i, [128, WIN], BF16) for i in range(2)]
        self.n_rt = sb("n_rt", [128, WIN], F32)
        self.n_rstd = sb("n_rstd", [128, WIN], F32)
        self.r_nsq = [Reg("sq0"), Reg("sq1")]
        self.r_nrt, self.r_nrstd = Reg("rt"), Reg("rstd")
        ARENA = (nc.sbuf_bytes_remaining - 2048) // 64 * 64
        self.arena_t = sb("arena", [128, ARENA // 2], BF16)
        self.arena = Arena(self.arena_t, ARENA)
        self.banks = [st.enter_context(nc.psum_tensor("bank%d" % i, [128, 512], F32)) for i in range(8)]
        self.bank_regs = [Reg("bank%d" % i) for i in range(8)]
        self.r_x = [[Reg("x%d_%d" % (k, w)) for w in range(NW)] for k in range(KC)]
        self.r_h = [[Reg("h%d_%d" % (k, w)) for w in range(NW)] for k in range(KC)]
        self.r_const = Reg("const")
        self.s_io = [self.P.dma_src() for _ in range(KC)]
        self.s_const = self.P.dma_src()

    def emit_consts(self):
        P = self.P
        P.dma("sp", self.s_const, self.normg[:], self.d_normg[:, :], writes=[self.r_const])
        ones = self.ones_bf
        P.op("dve", I_memset(ones[:], 1.0), writes=[self.r_const])

    def emit_load_x(self, s):
        P = self.P
        for k in range(KC):
            P.dma("sp", self.s_io[k], self.xT[:, k, :], self.d_xT[s, k, :, :], writes=self.r_x[k])

    def emit_store_x(self, s, src_is_h=False):
        P = self.P
        for k in range(KC):
            P.dma("sp", self.s_io[k], self.d_out[s, k, :, :], self.xT[:, k, :], reads=self.r_x[k])

    def gen_rmsnorm(self, norm_idx, windows, out_f32_inplace=False):
        P = self.P
        sq = [t[:] for t in self.n_sq]
        r_sq = self.r_nsq
        rt, rstd = self.n_rt[:], self.n_rstd[:]
        r_rt, r_rstd = self.r_nrt, self.r_nrstd
        xT, hT, ones, normg = self.xT, self.hT, self.ones_bf, self.normg
        bank = self.banks[7]
        r_bank = self.bank_regs[7]
        for w in windows:
            ws = slice(w * WIN, (w + 1) * WIN)
            for k in range(KC):
                b = k % 2
                P.op("act", I_act(sq[b], xT[:, k, ws], AF.Square),
                     reads=[self.r_x[k][w]], writes=[r_sq[b]])
                P.op("pe", I_mm(bank[:], ones[:], sq[b], start=(k == 0), stop=(k == KC - 1)),
                     reads=[r_sq[b], self.r_const], writes=[r_bank])
                yield
            P.op("act", I_act(rt, bank[:], AF.Ln, bias=EPS, scale=1.0 / D),
                 reads=[r_bank], writes=[r_rt])
            P.op("act", I_act(rstd, rt, AF.Exp, scale=-0.5), reads=[r_rt], writes=[r_rstd])
            yield
            for k in range(KC):
                g_ap = normg[:, norm_idx * KC + k:norm_idx * KC + k + 1]
                if out_f32_inplace:
                    P.op("dve", I_stt(xT[:, k, ws], xT[:, k, ws], g_ap, rstd, ALU.mult, ALU.mult),
                         reads=[r_rstd, self.r_const], writes=[self.r_x[k][w]])
                else:
                    P.op("dve", I_stt(hT[:, k, ws], xT[:, k, ws], g_ap, rstd, ALU.mult, ALU.mult),
                         reads=[self.r_x[k][w], r_rstd, self.r_const], writes=[self.r_h[k][w]])
                yield

    def emit_rmsnorm(self, norm_idx, windows, out_f32_inplace=False):
        for _ in self.gen_rmsnorm(norm_idx, windows, out_f32_inplace):
            pass

    def emit_ffn(self, ffn_idx, norm_idx, pre_normed=False, next_norm_idx=None):
        P = self.P
        ar = self.arena
        xT, hT = self.xT, self.hT
        HW = 2
        NH = NW // HW
        m = ar.mark()
        aT = ar.alloc([FC, HW * WIN], BF16)
        r_a = [[Reg("a") for _ in range(HW)] for _ in range(FC)]
        NGU = 4
        wgu = [ar.alloc([2, KC, 128], BF16) for _ in range(NGU)]
        r_wgu = [Reg("wgu%d" % i) for i in range(NGU)]
        s_wgu = self.s_wgu
        NWD = 3
        wd = [ar.alloc([FC, 128], BF16) for _ in range(NWD)]
        r_wd = [Reg("wd%d" % i) for i in range(NWD)]
        s_wd = self.s_wd
        sil = [ar.alloc([WIN], F32) for _ in range(2)]
        r_sil = [Reg("sil0"), Reg("sil1")]
        it = 0

        def load_wgu(f):
            sl = f % NGU
            P.dma("pool", s_wgu[sl], wgu[sl].rearrange("p a b c -> p (a b c)"),
                  self.d_wgu[ffn_idx, f, :, :], writes=[r_wgu[sl]])

        def load_wd(gd):
            sl = gd % NWD
            d = gd % KC
            hf = FC // 2
            P.dma("pool", s_wd[sl], wd[sl][:, 0:hf, :].rearrange("p a b -> p (a b)"),
                  self.d_wd[ffn_idx, d, :, 0:hf * 128], writes=[r_wd[sl]])
            P.dma("pool", s_wd[sl], wd[sl][:, hf:FC, :].rearrange("p a b -> p (a b)"),
                  self.d_wd[ffn_idx, d, :, hf * 128:FC * 128], writes=[r_wd[sl]])

        for f in range(NGU):
            load_wgu(f)
        for d in range(NWD):
            load_wd(d)
        if not pre_normed:
            self.emit_rmsnorm(norm_idx, [0, 1])
        for half in range(NH):
            wins = [half * HW + i for i in range(HW)]
            for f in range(FC):
                sl = f % NGU
                for wi, w in enumerate(wins):
                    ws = slice(w * WIN, (w + 1) * WIN)
                    pb = it % 2
                    it += 1
                    bg, bu = self.banks[pb * 2], self.banks[pb * 2 + 1]
                    rg, ru = self.bank_regs[pb * 2], self.bank_regs[pb * 2 + 1]
                    rd = [r_wgu[sl]] + [self.r_h[k][w] for k in range(KC)]
                    P.op("pe", I_mmgroup(bg[:], [(wgu[sl][:, 0, k, :], hT[:, k, ws]) for k in range(KC)]),
                         reads=rd, writes=[rg])
                    P.op("pe", I_mmgroup(bu[:], [(wgu[sl][:, 1, k, :], hT[:, k, ws]) for k in range(KC)]),
                         reads=rd, writes=[ru])
                    sb_ = it % 2
                    P.op("act", I_act(sil[sb_], bg[:], AF.Silu), reads=[rg], writes=[r_sil[sb_]])
                    P.op("dve", I_tt(aT[:, f, wi * WIN:(wi + 1) * WIN], sil[sb_], bu[:], ALU.mult),
                         reads=[r_sil[sb_], ru], writes=[r_a[f][wi]])
                if f + NGU < FC:
                    load_wgu(f + NGU)
            if half + 1 < NH:
                for f in range(NGU):
                    load_wgu(f)
                side = self.gen_rmsnorm(norm_idx, [(half + 1) * HW + i for i in range(HW)])
            elif next_norm_idx is not None:
                side = self.gen_rmsnorm(next_norm_idx, [0, 1])
            else:
                side = iter(())
            for d in range(KC):
                gd = half * KC + d
                sl = gd % NWD
                for wi, w in enumerate(wins):
                    ws = slice(w * WIN, (w + 1) * WIN)
                    pb = it % 2
                    it += 1
                    by, ry = self.banks[4 + pb], self.bank_regs[4 + pb]
                    P.op("pe", I_mmgroup(by[:], [(wd[sl][:, f, :], aT[:, f, wi * WIN:(wi + 1) * WIN]) for f in range(FC)]),
                         reads=[r_wd[sl]] + [r_a[f][wi] for f in range(FC)], writes=[ry])
                    P.op("dve", I_stt(xT[:, d, ws], by[:], 0.5, xT[:, d, ws], ALU.mult, ALU.add),
                         reads=[ry, self.r_x[d][w]], writes=[self.r_x[d][w]])
                    for _ in range(3):
                        next(side, None)
                if gd + NWD < NH * KC:
                    load_wd(gd + NWD)
            for _ in side:
                pass
        self.phase_barrier([r_wgu, r_wd, r_sil, [r for rr in r_a for r in rr]])
        ar.reset(m)

    def emit_mixa(self, j, norm_idx):
        P = self.P
        ar = self.arena
        xT, hT = self.xT, self.hT
        banks, bregs = self.banks, self.bank_regs
        self.emit_rmsnorm(norm_idx, range(NW))
        r_hall = [self.r_h[k][w] for k in range(KC) for w in range(NW)]
        m = ar.mark()
        NWB = 3
        wbuf = [ar.alloc([KC, 384], BF16) for _ in range(NWB)]
        r_wbuf = [Reg("awb%d" % i) for i in range(NWB)]
        wout = ar.alloc([4, D], BF16)
        r_wout = Reg("awout")
        ebuf = [ar.alloc([6, 256], F32) for _ in range(2)]
        r_ebuf = [Reg("eb0"), Reg("eb1")]
        qT = [ar.alloc([S], BF16) for _ in range(2)]
        kT = [ar.alloc([S], BF16) for _ in range(2)]
        r_qk = [Reg("qk0"), Reg("qk1")]
        vbuf = [ar.alloc([16, 128], BF16) for _ in range(2)]
        r_v = [Reg("v0"), Reg("v1")]
        accn = ar.alloc([S], F32)
        accd = ar.alloc([S], F32)
        r_accn, r_accd = Reg("accn"), Reg("accd")
        oT = ar.alloc([S], BF16)
        r_oT = Reg("oT")
        NE = 4
        ebf = [ar.alloc([256], F32) for _ in range(NE)]
        r_ebf = [Reg("E%d" % i) for i in range(NE)]
        NPT = 8
        pT = [ar.alloc([256], BF16) for _ in range(NPT)]
        r_pT = [Reg("pT%d" % i) for i in range(NPT)]
        ones = self.ones_bf

        P.dma("pool", self.s_awout, wout, self.d_awout[j, :, :, :], writes=[r_wout])
        wl = [(pair, g) for pair in range(4) for g in range(3)]

        def load_w(i):
            pair, g = wl[i]
            sl = i % NWB
            P.dma("pool", self.s_awb[sl], wbuf[sl], self.d_awin[j, pair, :, :, g * 384:(g + 1) * 384],
                  writes=[r_wbuf[sl]])

        def load_eb(pair):
            sl = pair % 2
            P.dma("sp", self.s_eb[sl], ebuf[sl], self.d_bt[pair, :, :, :], writes=[r_ebuf[sl]])
            P.op("act", I_act(ebuf[sl], ebuf[sl], AF.Exp), reads=[r_ebuf[sl]], writes=[r_ebuf[sl]])

        for i in range(NWB):
            load_w(i)
        load_eb(0)
        pj = 0
        sj = 0
        ej = 0
        pj_t = 0
        nj = 0
        for i, (pair, g) in enumerate(wl):
            dil = (1, 4, 16)[g]
            L = S // dil
            wsl = i % NWB
            qs = i % 2
            if g == 0 and pair + 1 < 4:
                load_eb(pair + 1)
            eb = ebuf[pair % 2]
            r_eb = r_ebuf[pair % 2]
            for which, dst in ((0, qT[qs]), (1, kT[qs])):
                for w in range(NW):
                    ws = slice(w * WIN, (w + 1) * WIN)
                    b = pj % 2
                    pj += 1
                    P.op("pe", I_mmgroup(banks[b][:], [(wbuf[wsl][:, k, which * 128:(which + 1) * 128], hT[:, k, ws])
                                                       for k in range(KC)]),
                         reads=[r_wbuf[wsl]] + [self.r_h[k][w] for k in range(KC)], writes=[bregs[b]])
                    nl = WIN // dil
                    if dil == 1:
                        o_ap = dst[:, ws]
                        i_ap = banks[b][:]
                    else:
                        o_ap = dst.rearrange("p (c l) -> p c l", c=dil)[:, :, w * nl:(w + 1) * nl]
                        i_ap = banks[b][:].rearrange("p (l c) -> p c l", c=dil)
                    if which == 0:
                        P.op("act", I_act(o_ap, i_ap, AF.Copy, scale=0.125), reads=[bregs[b]], writes=[r_qk[qs]])
                    else:
                        P.op("dve", I_copy(o_ap, i_ap), reads=[bregs[b]], writes=[r_qk[qs]])
            nbl = L // 128
            for bq in range(4):
                b = pj % 2
                pj += 1
                for bi in range(4):
                    blk = bq * 4 + bi
                    c, n = blk // nbl, blk % nbl
                    t0 = n * 128 * dil + c
                    tsl = slice(t0, t0 + 127 * dil + 1, dil)
                    tw = set(t // WIN for t in (t0, t0 + 127 * dil))
                    P.op("pe", I_mmgroup(banks[b][:, bi * 128:(bi + 1) * 128],
                                         [(hT[:, k, tsl], wbuf[wsl][:, k, 256:384]) for k in range(KC)]),
                         reads=[r_wbuf[wsl]] + [self.r_h[k][w] for k in range(KC) for w in tw],
                         writes=[bregs[b]])
                P.op("dve", I_copy(vbuf[qs][:, bq * 4:(bq + 1) * 4, :].rearrange("p a b -> p (a b)"), banks[b][:]),
                     reads=[bregs[b]], writes=[r_v[qs]])
            if i + NWB < len(wl):
                load_w(i + NWB)
            groups = []
            for c in range(dil):
                for qg in range(max(1, nbl // 4)):
                    if nbl == 1:
                        if c % 4 != 0:
                            continue
                        groups.append(([(c + cc, 0) for cc in range(4)], c, qg))
                    else:
                        groups.append(([(c, qg * 4 + qq) for qq in range(4)], c, qg))
            seq = []
            for gi_, (items, c, qg) in enumerate(groups):
                for hp in range(2):
                    for col, (cc, qb) in enumerate(items):
                        seq.append((gi_, hp, col, cc, qb))
            tile_of = {}
            uses_left = {}
            slot_owner = [None] * NPT

            def need_of(cc, qb):
                return [kb for kb in (qb - 1, qb) if kb >= 0]

            def n_uses(cc, kb):
                return (1 if kb < nbl else 0) + (1 if kb + 1 < nbl else 0)

            def stage1(hp, cc, kb):
                nonlocal sj, ej, pj_t
                key = (hp, cc, kb)
                if key in tile_of:
                    return
                ps = slice(64 * hp, 64 * hp + 64)
                base = cc * L
                nq = min(256, L - kb * 128)
                sb_ = 2 + sj % 2
                sj += 1
                P.op("pe", I_mm(banks[sb_][:, 0:nq],
                                kT[qs][ps, base + kb * 128: base + kb * 128 + 128],
                                qT[qs][ps, base + kb * 128: base + kb * 128 + nq]),
                     reads=[r_qk[qs]], writes=[bregs[sb_]])
                e_i = ej % NE
                ej += 1
                P.op("act", I_act(ebf[e_i][:, 0:nq], banks[sb_][:, 0:nq], AF.Exp),
                     reads=[bregs[sb_]], writes=[r_ebf[e_i]])
                p_i = pj_t % NPT
                pj_t += 1
                assert slot_owner[p_i] is None or uses_left[slot_owner[p_i]] == 0, "pT ring too small"
                slot_owner[p_i] = key
                uses_left[key] = n_uses(cc, kb)
                P.op("pool", I_tt(pT[p_i][:, 0:nq], ebf[e_i][:, 0:nq], eb[:, g * 2 + hp, 0:nq], ALU.mult),
                     reads=[r_ebf[e_i], r_eb], writes=[r_pT[p_i]])
                tile_of[key] = p_i

            LA = 2
            gbank = {}
            for i_, (gi_, hp, col, cc, qb) in enumerate(seq):
                for la in range(i_, min(i_ + LA + 1, len(seq))):
                    _, hp2, _, cc2, qb2 = seq[la]
                    for kb in need_of(cc2, qb2):
                        stage1(hp2, cc2, kb)
                if gi_ not in gbank:
                    gbank[gi_] = nj % 2
                    nj += 1
                bsel = gbank[gi_]
                bn, bd = banks[4 + bsel], banks[6 + bsel]
                rbn, rbd = bregs[4 + bsel], bregs[6 + bsel]
                ps = slice(64 * hp, 64 * hp + 64)
                pairs_n, pairs_d, rd = [], [], [r_v[qs], self.r_const]
                for kb in need_of(cc, qb):
                    key = (hp, cc, kb)
                    p_i = tile_of[key]
                    assert slot_owner[p_i] == key
                    uses_left[key] -= 1
                    off = (qb - kb) * 128
                    blk = cc * nbl + kb
                    pairs_n.append((vbuf[qs][:, blk, 64 * hp:64 * hp + 64], pT[p_i][:, off:off + 128]))
                    pairs_d.append((ones[:, 0:64], pT[p_i][:, off:off + 128]))
                    rd.append(r_pT[p_i])
                P.op("pe", I_mmgroup(bn[ps, col * 128:(col + 1) * 128], pairs_n), reads=rd, writes=[rbn])
                P.op("pe", I_mmgroup(bd[ps, col * 128:(col + 1) * 128], pairs_d), reads=rd, writes=[rbd])
                last_of_group = (i_ + 1 == len(seq)) or (seq[i_ + 1][0] != gi_)
                if last_of_group:
                    items, c, qg = groups[gi_]
                    if dil == 1:
                        an = accn[:, qg * 512:(qg + 1) * 512]
                        ad = accd[:, qg * 512:(qg + 1) * 512]
                        sn, sd = bn[:], bd[:]
                    elif nbl == 1:
                        an = accn.rearrange("p (l c) -> p c l", c=dil)[:, c:c + 4, :]
                        ad = accd.rearrange("p (l c) -> p c l", c=dil)[:, c:c + 4, :]
                        sn = bn[:].rearrange("p (c l) -> p c l", c=4)
                        sd = bd[:].rearrange("p (c l) -> p c l", c=4)
                    else:
                        an = accn.rearrange("p (l c) -> p c l", c=dil)[:, c, qg * 512:(qg + 1) * 512]
                        ad = accd.rearrange("p (l c) -> p c l", c=dil)[:, c, qg * 512:(qg + 1) * 512]
                        sn, sd = bn[:], bd[:]
                    if g == 0:
                        P.op("dve", I_copy(an, sn), reads=[rbn], writes=[r_accn])
                        P.op("act", I_act(ad, sd, AF.Copy), reads=[rbd], writes=[r_accd])
                    else:
                        P.op("dve", I_tt(an, sn, an, ALU.add), reads=[rbn, r_accn], writes=[r_accn])
                        P.op("dve", I_tt(ad, sd, ad, ALU.add), reads=[rbd, r_accd], writes=[r_accd])
            if g == 2:
                P.op("act", I_act(accd, accd, AF.Ln), reads=[r_accd], writes=[r_accd])
                P.op("act", I_act(accd, accd, AF.Exp, scale=-1.0), reads=[r_accd], writes=[r_accd])
                P.op("dve", I_tt(oT, accn, accd, ALU.mult), reads=[r_accn, r_accd], writes=[r_oT])
                for dch in range(KC):
                    for w in range(NW):
                        ws = slice(w * WIN, (w + 1) * WIN)
                        b = pj % 2
                        pj += 1
                        P.op("pe", I_mm(banks[b][:], wout[:, pair, dch * 128:(dch + 1) * 128], oT[:, ws]),
                             reads=[r_wout, r_oT], writes=[bregs[b]])
                        P.op("dve", I_tt(xT[:, dch, ws], banks[b][:], xT[:, dch, ws], ALU.add),
                             reads=[bregs[b], self.r_x[dch][w]], writes=[self.r_x[dch][w]])
        self.phase_barrier([r_wbuf, [r_wout], r_ebuf, r_qk, r_v, [r_accn, r_accd, r_oT], r_ebf, r_pT])
        ar.reset(m)

    def emit_mixb(self, j, norm_idx):
        P = self.P
        ar = self.arena
        xT, hT = self.xT, self.hT
        banks, bregs = self.banks, self.bank_regs
        self.emit_rmsnorm(norm_idx, range(NW))
        m = ar.mark()
        NB = S // 128
        cst = ar.alloc([13 * 128], F32)
        r_cst = Reg("bcst")
        TRI, SELL, IDN, MUI, MUS = [cst[:, i * 128:(i + 1) * 128] for i in range(5)]
        SEL = cst[:, 5 * 128:13 * 128].rearrange("p (h m) -> p h m", h=8)
        idb = ar.alloc([128], BF16)
        small = ar.alloc([3 + 256], F32)
        r_small = Reg("bsmall")
        dtb_col, alog_col, normw_col = small[:, 0:1], small[:, 1:2], small[:, 2:3]
        dtb_rep, alog_rep = small[:, 3:131], small[:, 131:259]
        nega = ar.alloc([1 + 128], F32)
        r_nega = Reg("nega")
        abT = ar.alloc([S], F32)
        r_abT = Reg("abT")
        tok = ar.alloc([4, 128], F32)
        gc_tok, beta_tok, bg_tok, kt_tok = [tok[:, i, :] for i in range(4)]
        r_tok = Reg("tok")
        wab = ar.alloc([KC, 64], BF16)
        wabt = ar.alloc([KC, 16], BF16)
        r_wab = Reg("wab")
        wh = [ar.alloc([KC, 256], BF16) for _ in range(3)]
        r_wh = [Reg("wh%d" % i) for i in range(3)]
        wo = [ar.alloc([D], BF16) for _ in range(2)]
        r_wo = [Reg("wo0"), Reg("wo1")]
        hv = [ar.alloc([12], F32) for _ in range(2)]
        r_hv = [Reg("hv0"), Reg("hv1")]
        raw = ar.alloc([S + 4], F32)
        r_raw_w = [Reg("raw%d" % i) for i in range(NW + 1)]
        r_raw = r_raw_w[0]
        acc = ar.alloc([S], F32)
        r_acc_w = [Reg("acc%d" % i) for i in range(NW)]
        r_acc = r_acc_w[0]
        qT = ar.alloc([S], BF16)
        kT = ar.alloc([S], BF16)
        vT = ar.alloc([S], BF16)
        qdT = ar.alloc([S], BF16)
        zsT = ar.alloc([S], BF16)
        r_q, r_k, r_v, r_qd, r_zs = Reg("q"), Reg("k"), Reg("v"), Reg("qd"), Reg("zs")
        u = ar.alloc([NB, 128], BF16)
        wT = ar.alloc([S], BF16)
        attnT = ar.alloc([NB, 128], BF16)
        ktail = ar.alloc([NB, 2, 128], BF16)
        r_u = [Reg("u%d" % b) for b in range(NB)]
        r_wT = [Reg("wT%d" % b) for b in range(NB)]
        r_at = [Reg("at%d" % b) for b in range(NB)]
        r_kt = [Reg("kt%d" % b) for b in range(NB)]
        r_qdb = [Reg("qd%d" % b) for b in range(NB)]
        cdv = ar.alloc([2 * NB], F32)
        r_cdv = Reg("cdv")
        r_W = [Reg("W%d" % i) for i in range(8)]
        NT = 3
        tmp = [ar.alloc([128], F32) for _ in range(NT)]
        r_tmp = [Reg("t%d" % i) for i in range(NT)]
        tcount = [0]

        def T():
            i = tcount[0] % NT
            tcount[0] += 1
            return tmp[i], r_tmp[i]
        Sf = ar.alloc([128], F32)
        Sb = ar.alloc([128], BF16)
        r_Sf, r_Sb = Reg("Sf"), Reg("Sb")
        vnew = ar.alloc([128], BF16)
        r_vnew = Reg("vnew")
        osb = ar.alloc([WIN], F32)
        r_osb = Reg("osb")
        rt2 = ar.alloc([WIN], F32)
        rstd2 = ar.alloc([WIN], F32)
        r_rt2, r_rstd2 = Reg("rt2"), Reg("rstd2")
        ogT = ar.alloc([WIN], BF16)
        r_og = Reg("og")
        zt, r_zt = osb, r_osb
        ones = self.ones_bf
        sq, r_sq = [t[:] for t in self.n_sq], self.r_nsq
        rt, rstd, r_rt, r_rstd = self.n_rt[:], self.n_rstd[:], self.r_nrt, self.r_nrstd
        place = {(4, 0): (4, 0), (4, 1): (5, 2), (4, 2): (4, 2), (4, 3): (4, 3),
                 (5, 0): (5, 0), (5, 1): (5, 1), (5, 2): (0, 0), (5, 3): (1, 0),
                 (6, 0): (6, 0), (6, 1): (6, 1), (6, 2): (2, 0), (6, 3): (3, 0),
                 (7, 0): (7, 0), (7, 1): (6, 2)}
        qb = {}
        qr = {}
        for key, (b, q) in place.items():
            qb[key] = banks[b][:, q * 128:(q + 1) * 128]
            qr[key] = bregs[b]

        def qbf(b, q):
            b, q = place[(b, q)]
            return banks[b][:].bitcast(BF16)[:, q * 256:q * 256 + 128]

        P.dma("sp", self.s_bc[0], cst, self.d_bconst[:, :], writes=[r_cst])
        P.dma("sp", self.s_bc[1], small, self.d_bsmall[j, :, :], writes=[r_small])
        P.dma("pool", self.s_bc[2], wab, self.d_wab[j, :, :, :], writes=[r_wab])
        P.dma("pool", self.s_bc[2], wabt, self.d_wabt[j, :, :, :], writes=[r_wab])
        P.op("dve", I_copy(idb, IDN), reads=[r_cst], writes=[r_cst])
        P.op("dve", I_memset(raw[:, 0:4], 0.0), writes=[r_raw])
        P.op("dve", I_memset(ktail.rearrange("p a b c -> p (a b c)"), 0.0), writes=r_kt)
        P.op("act", I_act(nega[:, 0:1], alog_col, AF.Exp), reads=[r_small], writes=[r_nega])
        P.op("act", I_act(nega[:, 1:129], alog_rep, AF.Exp), reads=[r_small], writes=[r_nega])
        P.op("dve", I_ts(nega, nega, -1.0, None, ALU.mult), reads=[r_nega], writes=[r_nega])

        def load_head(h):
            sl = h % 2
            for half in range(2):
                i = (2 * h + half) % 3
                P.dma("pool", self.s_wh[i], wh[i], self.d_bwin[j, h, :, :, half * 256:(half + 1) * 256],
                      writes=[r_wh[i]])
            P.dma("pool", self.s_wo[sl], wo[sl], self.d_bwout[j, h, :, :], writes=[r_wo[sl]])
            P.dma("sp", self.s_hv[sl], hv[sl], self.d_bconv[j, h, :, :], writes=[r_hv[sl]])

        load_head(0)
        tA, r_tA = acc, r_acc
        r_tA_all = r_acc_w
        for w in range(NW):
            ws = slice(w * WIN, (w + 1) * WIN)
            b = w % 2
            P.op("pe", I_mmgroup(banks[b][0:64, :], [(wab[:, k, :], hT[:, k, ws]) for k in range(KC)]),
                 reads=[r_wab] + [self.r_h[k][w] for k in range(KC)], writes=[bregs[b]])
            P.op("act", I_act(tA[0:8, ws], banks[b][0:8, :], AF.Exp, bias=dtb_col[0:8, :]),
                 reads=[bregs[b], r_small], writes=r_tA_all)
            P.op("act", I_act(tA[0:8, ws], tA[0:8, ws], AF.Ln, bias=1.0), reads=r_tA_all, writes=r_tA_all)
            P.op("dve", I_ts(abT[0:8, ws], tA[0:8, ws], nega[0:8, 0:1], None, ALU.mult),
                 reads=r_tA_all + [r_nega], writes=[r_abT])
            P.op("act", I_act(abT[32:40, ws], banks[b][32:40, :], AF.Exp, scale=-1.0),
                 reads=[bregs[b]], writes=[r_abT])
            P.op("dve", I_ts(abT[32:40, ws], abT[32:40, ws], 1.0, None, ALU.add), reads=[r_abT], writes=[r_abT])
            P.op("dve", I_recip(abT[32:40, ws], abT[32:40, ws]), reads=[r_abT], writes=[r_abT])
        src_, dst_ = abT, tA
        for st_ in (1, 2, 4, 8, 16, 32):
            sv = src_[0:8, :].rearrange("p (n c) -> p n c", c=64)
            dv = dst_[0:8, :].rearrange("p (n c) -> p n c", c=64)
            P.op("dve", I_tt(dv[:, :, st_:64], sv[:, :, st_:64], sv[:, :, 0:64 - st_], ALU.add),
                 reads=[r_abT] + r_tA_all, writes=[r_abT] + r_tA_all)
            P.op("dve", I_copy(dv[:, :, 0:st_], sv[:, :, 0:st_]), reads=[r_abT] + r_tA_all, writes=[r_abT] + r_tA_all)
            src_, dst_ = dst_, src_
        assert src_ is abT
        for blk in range(NB):
            P.op("pe", I_mmgroup(banks[2][:, blk * 16:(blk + 1) * 16],
                                 [(hT[:, k, blk * 128:(blk + 1) * 128], wabt[:, k, :]) for k in range(KC)]),
                 reads=[r_wab] + [self.r_h[k][blk // 4] for k in range(KC)], writes=[bregs[2]])
        abv = banks[2][:, 0:256].rearrange("p (b c) -> p b c", c=16)
        t1, r_t1 = T()
        t2, r_t2 = T()
        v3 = lambda a: a.rearrange("p (b c) -> p b c", c=8)
        P.op("dve", I_tt(v3(t1), abv[:, :, 0:8], v3(dtb_rep), ALU.add), reads=[bregs[2], r_small], writes=[r_t1])
        P.op("act", I_act(t1, t1, AF.Exp), reads=[r_t1], writes=[r_t1])
        P.op("act", I_act(t1, t1, AF.Ln, bias=1.0), reads=[r_t1], writes=[r_t1])
        P.op("dve", I_tt(t1, t1, nega[:, 1:129], ALU.mult), reads=[r_t1, r_nega], writes=[r_t1])
        P.op("act", I_act(v3(beta_tok), abv[:, :, 8:16], AF.Exp, scale=-1.0), reads=[bregs[2]], writes=[r_tok])
        P.op("dve", I_ts(beta_tok, beta_tok, 1.0, None, ALU.add), reads=[r_tok], writes=[r_tok])
        P.op("dve", I_recip(beta_tok, beta_tok), reads=[r_tok], writes=[r_tok])
        P.op("pe", I_mm(qb[(4, 0)], TRI, t1), reads=[r_cst, r_t1], writes=[qr[(4, 0)]])
        P.op("act", I_act(gc_tok, qb[(4, 0)], AF.Copy), reads=[qr[(4, 0)]], writes=[r_tok])
        P.op("pe", I_mm(qb[(4, 1)], SELL, gc_tok), reads=[r_cst, r_tok], writes=[qr[(4, 1)]])
        P.op("dve", I_tt(t2, qb[(4, 1)], gc_tok, ALU.subtract), reads=[qr[(4, 1)], r_tok], writes=[r_t2])
        P.op("act", I_act(kt_tok, t2, AF.Exp), reads=[r_t2], writes=[r_tok])
        P.op("act", I_act(t2, gc_tok, AF.Exp), reads=[r_tok], writes=[r_t2])
        P.op("dve", I_tt(bg_tok, beta_tok, t2, ALU.mult), reads=[r_tok, r_t2], writes=[r_tok])

        pjc = [0]

        def gen_h2(h):
            hs = h % 2
            wq = wh[(2 * h) % 3]
            wv = wh[(2 * h + 1) % 3]
            r_wq, r_wv = r_wh[(2 * h) % 3], r_wh[(2 * h + 1) % 3]
            taps = hv[hs]
            items = [(X, w) for X in range(3) for w in range(NW)]
            pb = {}

            def stA(i):
                X, w = items[i]
                wsrc, r_wsrc = (wq, r_wq) if X < 2 else (wv, r_wv)
                co = (X % 2) * 128
                ws = slice(w * WIN, (w + 1) * WIN)
                b = 4 + i % 2
                pb[i] = b
                P.op("pe", I_mmgroup(banks[b][:], [(wsrc[:, k, co:co + 128], hT[:, k, ws]) for k in range(KC)]),
                     reads=[r_wsrc] + [self.r_h[k][w] for k in range(KC)], writes=[bregs[b]])

            def stB(i):
                X, w = items[i]
                b = pb[i]
                P.op("act", I_act(raw[:, 4 + w * WIN:4 + (w + 1) * WIN], banks[b][:], AF.Copy),
                     reads=[bregs[b]], writes=[r_raw_w[1 + w]])

            def stC(i):
                X, w = items[i]
                ws = slice(w * WIN, (w + 1) * WIN)
                P.op("dve", I_ts(acc[:, ws], raw[:, 4 + w * WIN:4 + (w + 1) * WIN],
                                 taps[:, X * 4 + 3:X * 4 + 4], None, ALU.mult),
                     reads=[r_raw_w[1 + w], r_hv[hs]], writes=[r_acc_w[w]])
                for tp in (2, 1, 0):
                    P.op("dve", I_stt(acc[:, ws], raw[:, 1 + tp + w * WIN:1 + tp + (w + 1) * WIN],
                                      taps[:, X * 4 + tp:X * 4 + tp + 1], acc[:, ws], ALU.mult, ALU.add),
                         reads=[r_raw_w[w], r_raw_w[1 + w], r_hv[hs], r_acc_w[w]], writes=[r_acc_w[w]])

            def stD1(i):
                X, w = items[i]
                ws = slice(w * WIN, (w + 1) * WIN)
                if X == 2:
                    P.op("act", I_act(vT[:, ws], acc[:, ws], AF.Silu), reads=[r_acc_w[w]], writes=[r_v])
                else:
                    P.op("act", I_act(acc[:, ws], acc[:, ws], AF.Silu), reads=[r_acc_w[w]], writes=[r_acc_w[w]])
                    P.op("act", I_act(sq[1], acc[:, ws], AF.Square), reads=[r_acc_w[w]], writes=[r_sq[1]])

            def stD2(i):
                X, w = items[i]
                if X < 2:
                    P.op("pe", I_mm(banks[3][:], ones[:], sq[1]), reads=[r_sq[1], self.r_const], writes=[bregs[3]])

            def stD3(i):
                X, w = items[i]
                if X < 2:
                    P.op("act", I_act(rt, banks[3][:], AF.Ln, bias=EPS, scale=1.0), reads=[bregs[3]], writes=[r_rt])
                    P.op("act", I_act(rstd, rt, AF.Exp, scale=-0.5), reads=[r_rt], writes=[r_rstd])

            def stD4(i):
                X, w = items[i]
                if X < 2:
                    ws = slice(w * WIN, (w + 1) * WIN)
                    dst, r_dst = (qT, r_q) if X == 0 else (kT, r_k)
                    sc = (128.0 ** -0.5) if X == 0 else 1.0
                    P.op("dve", I_stt(dst[:, ws], acc[:, ws], sc, rstd, ALU.mult, ALU.mult),
                         reads=[r_acc_w[w], r_rstd], writes=[r_dst])

            stages = [stA, stB, stC, stD1, stD2, stD3, stD4]
            n = len(items)
            for t in range(n + len(stages) - 1):
                for si in range(len(stages) - 1, -1, -1):
                    i = t - si
                    if 0 <= i < n:
                        stages[si](i)
                yield

        def emit_z(h):
            wv = wh[(2 * h + 1) % 3]
            r_wv = r_wh[(2 * h + 1) % 3]
            for w in range(NW):
                ws = slice(w * WIN, (w + 1) * WIN)
                b = pjc[0] % 2
                pjc[0] += 1
                P.op("pe", I_mmgroup(banks[b][:], [(wv[:, k, 128:256], hT[:, k, ws]) for k in range(KC)]),
                     reads=[r_wv] + [self.r_h[k][w] for k in range(KC)], writes=[bregs[b]])
                P.op("act", I_act(zt, banks[b][:], AF.Silu), reads=[bregs[b]], writes=[r_zt])
                P.op("dve", I_ts(zsT[:, ws], zt, normw_col, None, ALU.mult), reads=[r_zt, r_small], writes=[r_zs])

        def gen_scan(h):
            hs = h % 2
            P.op("dve", I_memset(Sf, 0.0), writes=[r_Sf])
            P.op("dve", I_memset(Sb, 0.0), writes=[r_Sb])
            P.op("dve", I_memset(vnew, 0.0), writes=[r_vnew])
            for n in range(2 * NB):
                if n > 0:
                    yield
                blk, half = n // 2, n % 2
                ps = slice(64 * half, 64 * half + 64)
                cs = slice(n * 64, (n + 1) * 64)
                w = n // 8
                ob, r_ob = banks[2], bregs[2]
                bs = slice(blk * 128, (blk + 1) * 128)
                P.op("pe", I_mm(banks[7][:, 0:128], wT[:, bs], Sb), reads=[r_wT[blk], r_Sb], writes=[qr[(7, 0)]])
                P.op("dve", I_tt(vnew[ps, :], u[ps, blk, :], banks[7][ps, 0:128], ALU.subtract),
                     reads=[r_u[blk], qr[(7, 0)]], writes=[r_vnew])
                oc = (n % 8) * 64
                P.op("pe", I_mm(qb[(7, 1)], ktail[:, blk, half, :], vnew[:, :]), reads=[r_kt[blk], r_vnew],
                     writes=[qr[(7, 1)]])
                P.op("pe", I_mmgroup(ob[:, oc:oc + 64], [(Sb, qdT[:, cs]),
                                                        (vnew[:, :], attnT[:, blk, 64 * half:64 * half + 64])]),
                     reads=[r_Sb, r_qdb[blk], r_vnew, r_at[blk]], writes=[r_ob])
                P.op("dve", I_stt(Sb, Sf, cdv[:, n:n + 1], qb[(7, 1)], ALU.mult, ALU.add),
                     reads=[r_Sf, r_cdv, qr[(7, 1)]], writes=[r_Sb])
                P.op("dve", I_stt(Sf, Sf, cdv[:, n:n + 1], qb[(7, 1)], ALU.mult, ALU.add),
                     reads=[r_Sf, r_cdv, qr[(7, 1)]], writes=[r_Sf])
                if n % 8 == 7:
                    ws = slice(w * WIN, (w + 1) * WIN)
                    P.op("act", I_act(osb, ob[:], AF.Copy), reads=[r_ob], writes=[r_osb])
                    P.op("act", I_act(sq[0], osb, AF.Square), reads=[r_osb], writes=[r_sq[0]])
                    P.op("pe", I_mm(banks[0][:], ones[:], sq[0]), reads=[r_sq[0], self.r_const], writes=[bregs[0]])
                    P.op("act", I_act(rt2, banks[0][:], AF.Ln, bias=EPS, scale=1.0 / 128), reads=[bregs[0]],
                         writes=[r_rt2])
                    P.op("act", I_act(rstd2, rt2, AF.Exp, scale=-0.5), reads=[r_rt2], writes=[r_rstd2])
                    P.op("dve", I_tt(osb, osb, rstd2, ALU.mult), reads=[r_osb, r_rstd2], writes=[r_osb])
                    P.op("dve", I_tt(ogT, osb, zsT[:, ws], ALU.mult), reads=[r_osb, r_zs], writes=[r_og])
                    for dch in range(KC):
                        P.op("pe", I_mm(banks[1][:], wo[hs][:, dch * 128:(dch + 1) * 128], ogT),
                             reads=[r_wo[hs], r_og], writes=[bregs[1]])
                        P.op("dve", I_tt(xT[:, dch, ws], banks[1][:], xT[:, dch, ws], ALU.add),
                             reads=[bregs[1], self.r_x[dch][w]], writes=[self.r_x[dch][w]])

        prev_scan = None
        for h in range(8):
            g2 = gen_h2(h)
            if prev_scan is not None:
                a_live, b_live = True, True
                while a_live or b_live:
                    if a_live:
                        try:
                            next(prev_scan)
                        except StopIteration:
                            a_live = False
                    if b_live:
                        try:
                            next(g2)
                        except StopIteration:
                            b_live = False
            else:
                for _ in g2:
                    pass
            emit_z(h)
            if h + 1 < 8:
                load_head(h + 1)
            for eng_ in ("pe", "act", "dve"):
                P.wait_all(eng_, r_raw_w + r_acc_w)
            Wt = [raw[:, 4 + i * 512:4 + (i + 1) * 512] for i in range(4)] + \
                 [acc[:, i * 512:(i + 1) * 512] for i in range(4)]
            v4 = lambda a: a.rearrange("p (b i) -> p b i", b=4)
            RR = (lambda a: a.bitcast(FP32R)) if DBL_R else (lambda a: a)
            bmid = lambda a: a.unsqueeze(1).broadcast_to([128, 4, 128])
            for gi in range(NB // 4):
                b0 = gi * 4
                gsl = slice(b0 * 128, (b0 + 4) * 128)
                bsl = [slice((b0 + bi) * 128, (b0 + bi + 1) * 128) for bi in range(4)]
                qs_ = [slice(bi * 128, (bi + 1) * 128) for bi in range(4)]

                def tokb(t, pp=slice(0, 128)):
                    return t[pp, b0 * 8 + h:(b0 + 3) * 8 + h + 1:8].unsqueeze(2).broadcast_to([pp.stop - pp.start, 4, 128])
                W_dd, W_eg, W_gm, W_gs, W_B, W_A, W_M, W_X = Wt
                rW = r_W
                P.op("pe", I_multi([(banks[4][:, qs_[bi]], SEL[0:8, h, :], abT[0:8, bsl[bi]]) for bi in range(4)]),
                     reads=[r_cst, r_abT], writes=[bregs[4]])
                P.op("pe", I_multi([(banks[5][:, qs_[bi]], SEL[32:40, h, :], abT[32:40, bsl[bi]]) for bi in range(4)]),
                     reads=[r_cst, r_abT], writes=[bregs[5]])
                P.op("pe", I_multi([(banks[6][:, qs_[bi]], kT[:, bsl[bi]], kT[:, bsl[bi]]) for bi in range(4)]),
                     reads=[r_k], writes=[bregs[6]])
                P.op("pe", I_multi([(banks[7][:, qs_[bi]], kT[:, bsl[bi]], qT[:, bsl[bi]]) for bi in range(4)]),
                     reads=[r_k, r_q], writes=[bregs[7]])
                P.op("dve", I_tt(v4(W_dd), v4(banks[4][:]), tokb(gc_tok), ALU.subtract), reads=[bregs[4], r_tok],
                     writes=[rW[0]])
                P.op("dve", I_ts(W_dd, W_dd, 0.0, None, ALU.min), reads=[rW[0]], writes=[rW[0]])
                P.op("act", I_act(W_dd, W_dd, AF.Exp), reads=[rW[0]], writes=[rW[0]])
                P.op("act", I_act(W_eg, banks[4][:], AF.Exp), reads=[bregs[4]], writes=[rW[1]])
                P.op("dve", I_tt(v4(W_gm), v4(W_dd), bmid(MUI), ALU.mult), reads=[rW[0], r_cst], writes=[rW[2]])
                P.op("dve", I_tt(attnT[:, b0:b0 + 4, :], v4(banks[7][:]), v4(W_gm), ALU.mult),
                     reads=[bregs[7], rW[2]], writes=[r_at[b0 + bi] for bi in range(4)])
                P.op("dve", I_tt(v4(W_gs), v4(W_dd), bmid(MUS), ALU.mult), reads=[rW[0], r_cst], writes=[rW[3]])
                P.op("dve", I_tt(W_gs, banks[5][:], W_gs, ALU.mult), reads=[bregs[5], rW[3]], writes=[rW[3]])
                P.op("dve", I_tt(RR(W_B), banks[6][:], W_gs, ALU.mult), reads=[bregs[6], rW[3]], writes=[rW[4]])
                P.op("dve", I_tt(qdT[:, gsl], qT[:, gsl], W_eg, ALU.mult), reads=[r_q, rW[1]],
                     writes=[r_qdb[b0 + bi] for bi in range(4)])
                P.op("act", I_act(cdv[:, 2 * b0:2 * b0 + 8], W_eg[:, 63:512:64], AF.Copy), reads=[rW[1]],
                     writes=[r_cdv])
                P.op("pe", I_multi_tr([(banks[0][:, qs_[bi]], W_B[:, qs_[bi]], IDN) for bi in range(4)]),
                     reads=[rW[4], r_cst], writes=[bregs[0]])
                P.op("act", I_act(RR(W_A), banks[0][:], AF.Copy), reads=[bregs[0]], writes=[rW[5]])
                P.op("dve", I_tt(v4(RR(W_M)), bmid(IDN), v4(W_B), ALU.subtract), reads=[r_cst, rW[4]], writes=[rW[6]])
                Bc, iB, Ac, iA = W_B, 4, W_A, 5
                free = [(W_dd, 0), (W_eg, 1)]
                for lev in range(1, 6):
                    An, iAn = free.pop(0)
                    P.op("pe", I_multi([(banks[1][:, qs_[bi]], Bc[:, qs_[bi]], Ac[:, qs_[bi]]) for bi in range(4)], r=True),
                         reads=[rW[iB], rW[iA]], writes=[bregs[1]])
                    if lev < 5:
                        Bn, iBn = free.pop(0)
                        P.op("pe", I_multi([(banks[2][:, qs_[bi]], Ac[:, qs_[bi]], Bc[:, qs_[bi]])
                                            for bi in range(4)], r=True),
                             reads=[rW[iB], rW[iA]], writes=[bregs[2]])
                    P.op("dve", I_copy(RR(An), banks[1][:]), reads=[bregs[1]], writes=[rW[iAn]])
                    if lev < 5:
                        P.op("act", I_act(RR(Bn), banks[2][:], AF.Copy), reads=[bregs[2]], writes=[rW[iBn]])
                    P.op("pe", I_multi([(banks[3][:, qs_[bi]], An[:, qs_[bi]], W_M[:, qs_[bi]]) for bi in range(4)], r=True),
                         reads=[rW[iAn], rW[6]], writes=[bregs[3]])
                    P.op("dve", I_tt(RR(W_M), banks[3][:], W_M, ALU.add), reads=[bregs[3], rW[6]], writes=[rW[6]])
                    free.append((Ac, iA))
                    Ac, iA = An, iAn
                    if lev < 5:
                        free.append((Bc, iB))
                        Bc, iB = Bn, iBn
                ktp = banks[0][:].bitcast(BF16)[:, 0:512]
                vtp = banks[1][:].bitcast(BF16)[:, 0:512]
                P.op("pe", I_multi_tr([(ktp[:, qs_[bi]], kT[:, bsl[bi]], idb) for bi in range(4)]),
                     reads=[r_k, r_cst], writes=[bregs[0]])
                P.op("pe", I_multi_tr([(vtp[:, qs_[bi]], vT[:, bsl[bi]], idb) for bi in range(4)]),
                     reads=[r_v, r_cst], writes=[bregs[1]])
                W_rw, W_ru = W_gm, W_gs
                P.op("dve", I_tt(v4(RR(W_rw)), v4(ktp), tokb(bg_tok), ALU.mult), reads=[bregs[0], r_tok], writes=[rW[2]])
                P.op("dve", I_tt(v4(RR(W_ru)), v4(vtp), tokb(beta_tok), ALU.mult), reads=[bregs[1], r_tok], writes=[rW[3]])
                for hf in range(2):
                    pp = slice(64 * hf, 64 * hf + 64)
                    P.op("dve", I_tt(ktail[pp, b0:b0 + 4, hf, :], v4(ktp)[pp], tokb(kt_tok, pp), ALU.mult),
                         reads=[bregs[0], r_tok], writes=[r_kt[b0 + bi] for bi in range(4)])
                P.op("pe", I_multi([(banks[2][:, qs_[bi]], W_M[:, qs_[bi]], W_ru[:, qs_[bi]]) for bi in range(4)], r=True),
                     reads=[rW[6], rW[3]], writes=[bregs[2]])
                P.op("act", I_act(u[:, b0:b0 + 4, :], v4(banks[2][:]), AF.Copy), reads=[bregs[2]],
                     writes=[r_u[b0 + bi] for bi in range(4)])
                P.op("pe", I_multi([(banks[3][:, qs_[bi]], W_rw[:, qs_[bi]], W_M[:, qs_[bi]]) for bi in range(4)], r=True),
                     reads=[rW[6], rW[2]], writes=[bregs[3]])
                P.op("dve", I_copy(wT[:, gsl], banks[3][:]), reads=[bregs[3]],
                     writes=[r_wT[b0 + bi] for bi in range(4)])
            for eng_ in ("pe", "act", "dve"):
                P.wait_all(eng_, r_W)
            prev_scan = gen_scan(h)
        for _ in prev_scan:
            pass
        allregs = r_raw_w + r_acc_w + [r_cst, r_small, r_nega, r_abT, r_tok, r_wab, r_q, r_k, r_v, r_qd, r_zs, r_cdv,
                   r_Sf, r_Sb, r_vnew, r_osb, r_og, r_rt2, r_rstd2]
        self.phase_barrier([allregs, r_wh, r_wo, r_hv, r_u, r_wT, r_at, r_kt, r_qdb, r_tmp, r_W, bregs])
        ar.reset(m)

    def phase_barrier(self, reg_lists):
        regs = [r for rl in reg_lists for r in rl]
        for n in ("pe", "act", "dve", "pool", "sp"):
            self.P.wait_all(n, regs)

    def build(self):
        nc, P, st = self.nc, self.P, self.stack
        self.s_wgu = [P.dma_src() for _ in range(4)]
        self.s_wd = [P.dma_src() for _ in range(3)]
        self.s_awb = [P.dma_src() for _ in range(3)]
        self.s_awout = P.dma_src()
        self.s_eb = [P.dma_src() for _ in range(2)]
        self.s_bc = [P.dma_src() for _ in range(3)]
        self.s_wh = [P.dma_src() for _ in range(3)]
        self.s_wo = [P.dma_src() for _ in range(2)]
        self.s_hv = [P.dma_src() for _ in range(2)]
        self.emit_consts()
        for s in range(self.n_seq):
            self.emit_load_x(s)
            for pi, ph in enumerate(self.phases):
                if ph[0] == "ffn":
                    nxt = self.phases[pi + 1] if pi + 1 < len(self.phases) else None
                    prv = self.phases[pi - 1] if pi > 0 else None
                    self.emit_ffn(ph[1], ph[2],
                                  pre_normed=(prv is not None and prv[0] == "ffn"),
                                  next_norm_idx=(nxt[2] if (nxt is not None and nxt[0] == "ffn") else None))
                elif ph[0] == "mixa":
                    self.emit_mixa(ph[1], ph[2])
                elif ph[0] == "mixb":
                    self.emit_mixb(ph[1], ph[2])
                elif ph[0] == "final":
                    self.emit_rmsnorm(12, range(NW), out_f32_inplace=True)
                else:
                    raise ValueError(ph)
            self.emit_store_x(s)
        P.wait_all("sp", [r for k in range(KC) for r in self.r_x[k]])
        with nc.Block() as block:
            P.replay(block)
        st.close()
        return nc


def full_phases():
    ph = []
    for i in range(DEPTH):
        ph.append(("ffn", 2 * i, 3 * i))
        ph.append(("mixa" if i % 2 == 0 else "mixb", i // 2, 3 * i + 1))
        ph.append(("ffn", 2 * i + 1, 3 * i + 2))
    ph.append(("final",))
    return ph


def prep_weights(inp):
    f32 = np.float32
    out = {}
    ng = np.concatenate([np.asarray(inp["norm_g"], f32).reshape(12, D), np.asarray(inp["final_g"], f32).reshape(1, D)], 0)
    out["normg"] = np.ascontiguousarray(ng.reshape(13, KC, 128).transpose(2, 0, 1).reshape(128, 13 * KC))
    wg = np.asarray(inp["ffn_w_gate"], f32).reshape(2 * DEPTH, KC, 128, FC, 128)
    wu = np.asarray(inp["ffn_w_up"], f32).reshape(2 * DEPTH, KC, 128, FC, 128)
    wgu = np.stack([wg, wu], 0)
    wgu = wgu.transpose(1, 4, 3, 0, 2, 5)
    out["wgu"] = np.ascontiguousarray(wgu).reshape(2 * DEPTH, FC, 128, 2 * KC * 128)
    wd = np.asarray(inp["ffn_w_down"], f32).reshape(2 * DEPTH, FC, 128, KC, 128)
    wd = wd.transpose(0, 3, 2, 1, 4)
    out["wd"] = np.ascontiguousarray(wd).reshape(2 * DEPTH, KC, 128, FC * 128)
    awin = np.asarray(inp["a_w_in"], f32).reshape(2, KC, 128, 3, 3, 4, 128)
    awin = awin.transpose(0, 5, 2, 1, 3, 4, 6)
    out["awin"] = np.ascontiguousarray(awin).reshape(2, 4, 128, KC, 3 * 384)
    awout = np.asarray(inp["a_w_out"], f32).reshape(2, 4, 128, D).transpose(0, 2, 1, 3)
    out["awout"] = np.ascontiguousarray(awout)
    out["bt"] = bias_table(np.asarray(inp["rel_bias"], f32))
    bw = np.asarray(inp["b_w_in"], f32)
    w4 = bw[:, :, :4096].reshape(2, KC, 128, 4, 8, 128).transpose(0, 4, 2, 1, 3, 5)
    out["bwin"] = np.ascontiguousarray(w4).reshape(2, 8, 128, KC, 512)
    wa = bw[:, :, 4096:4104].reshape(2, KC, 128, 8).transpose(0, 2, 1, 3)
    wb_ = bw[:, :, 4104:4112].reshape(2, KC, 128, 8).transpose(0, 2, 1, 3)
    wab = np.zeros((2, 128, KC, 64), f32)
    wab[..., 0:8] = wa
    wab[..., 32:40] = wb_
    out["wab"] = wab
    out["wabt"] = np.ascontiguousarray(np.concatenate([wa, wb_], -1))
    cw = np.asarray(inp["b_conv_w"], f32).reshape(2, 4, 3, 8, 128).transpose(0, 3, 4, 2, 1)
    out["bconv"] = np.ascontiguousarray(cw).reshape(2, 8, 128, 12)
    sm = np.zeros((2, 128, 259), f32)
    dtb = np.asarray(inp["b_dt_bias"], f32)
    alog = np.asarray(inp["b_a_log"], f32)
    sm[:, 0:8, 0] = dtb
    sm[:, 0:8, 1] = alog
    sm[:, :, 2] = np.asarray(inp["b_norm_w"], f32)
    sm[:, :, 3:131] = np.tile(dtb, (1, 16))[:, None, :]
    sm[:, :, 131:259] = np.tile(alog, (1, 16))[:, None, :]
    out["bsmall"] = sm
    out["bwout"] = np.ascontiguousarray(np.asarray(inp["b_w_out"], f32).reshape(2, 8, 128, D))
    out["bconst"] = b_consts()
    return out


def b_consts():
    i = np.arange(128)
    same = (i[:, None] // 64) == (i[None, :] // 64)
    c = np.zeros((128, 13 * 128), np.float32)
    c[:, 0:128] = (same & (i[:, None] <= i[None, :]))
    c[:, 128:256] = (i[:, None] == (i[None, :] // 64) * 64 + 63)
    c[:, 256:384] = np.eye(128)
    c[:, 384:512] = (same & (i[None, :] >= i[:, None]))
    c[:, 512:640] = (same & (i[None, :] > i[:, None]))
    sel = np.zeros((128, 8, 128), np.float32)
    for h in range(8):
        sel[h, h, :] = 1.0
        sel[32 + h, h, :] = 1.0
    c[:, 640:] = sel.reshape(128, 1024)
    return c


def t5_bucket_np(dist):
    n = dist.astype(np.float32)
    large = np.float32(16.0) + (np.log(np.maximum(n, np.float32(1.0)) / np.float32(16.0)).astype(np.float32)
                                / np.float32(np.log(2048.0 / 16.0)) * np.float32(16.0)).astype(np.float32)
    large = np.minimum(large.astype(np.int32), 31)
    return np.where(dist < 16, dist, large)


def bias_table(rel_bias):
    kk = np.arange(128)[:, None]
    jj = np.arange(256)[None, :]
    rel = jj - kk
    valid = (rel >= 0) & (rel <= 128)
    bt = np.full((4, 128, 6, 256), -30000.0, np.float32)
    for g, dil in enumerate((1, 4, 16)):
        bk = t5_bucket_np(np.maximum(rel, 0) * dil)
        for pair in range(4):
            for hp in range(2):
                tab = rel_bias[:, g * 8 + 2 * pair + hp][bk]
                bt[pair, :, g * 2 + hp, :] = np.where(valid, tab, np.float32(-30000.0))
    return bt


def x_to_dev(x_seqs):
    n = x_seqs.shape[0]
    return np.ascontiguousarray(x_seqs.reshape(n, S, KC, 128).transpose(0, 2, 3, 1))


def x_from_dev(o):
    n = o.shape[0]
    return np.ascontiguousarray(o.transpose(0, 3, 1, 2).reshape(n, S, D))


_PROG_CACHE = {}


def get_prog(n_seq, phases):
    key = (n_seq, tuple(phases))
    if key not in _PROG_CACHE:
        _PROG_CACHE[key] = Builder(n_seq, list(phases)).build()
    return _PROG_CACHE[key]


def run(inputs, n_seq_per_core, phases, n_cores=N_CORES, trace=False, x_override=None):
    w = prep_weights(inputs)
    x = np.asarray(inputs["x"] if x_override is None else x_override, np.float32)
    nc = get_prog(n_seq_per_core, phases)
    in_maps = []
    for c in range(n_cores):
        m = dict(w)
        m["xT"] = x_to_dev(x[c * n_seq_per_core:(c + 1) * n_seq_per_core])
        in_maps.append(m)
    res = run_bass_kernel_spmd(nc, in_maps, core_ids=list(range(n_cores)), trace=trace)
    outs = [x_from_dev(r["outT"]) for r in res.results]
    return np.concatenate(outs, 0), res


def kernel(**inputs):
    out, _ = run(inputs, 4, full_phases())
    return out
```

```python
import numpy as np
import concourse.bass as bass
import concourse.mybir as mybir
from concourse.bass_utils import run_bass_kernel_spmd

F32 = mybir.dt.float32
BF16 = mybir.dt.bfloat16
FP32R = mybir.dt.float32r
H2_PER_SCAN = 2
DBL_R = False
AF = mybir.ActivationFunctionType
ALU = mybir.AluOpType

D = 1024
S = 2048
DFF = 2816
KC = D // 128
FC = DFF // 128
WIN = 512
NW = S // WIN
DEPTH = 4
EPS = 1e-6
N_CORES = 8
MIXB_STOP = None
OP_LIMIT = None


class Src:
    def __init__(self, sem, step):
        self.sem = sem
        self.step = step
        self.val = 0


class Reg:
    __slots__ = ("name", "w", "r")

    def __init__(self, name=""):
        self.name = name
        self.w = None
        self.r = {}


class Eng:
    def __init__(self, name, src):
        self.name = name
        self.src = src
        self.seen = {}
        self.ops = []


class Prog:
    def __init__(self, nc, stack):
        self.nc = nc
        self.stack = stack
        self.engs = {}
        for n in ("pe", "act", "dve", "pool", "sp"):
            sem = stack.enter_context(nc.semaphore("sem_" + n))
            self.engs[n] = Eng(n, Src(sem, 1))
        self.n_dma_src = 0

    def dma_src(self):
        self.n_dma_src += 1
        sem = self.stack.enter_context(self.nc.semaphore("dsem%d" % self.n_dma_src))
        return Src(sem, 16)

    def _waits(self, eng, reads, writes):
        deps = {}
        for r in reads:
            if r.w is not None:
                s, v = r.w
                if deps.get(s, 0) < v:
                    deps[s] = v
        for w in writes:
            if w.w is not None:
                s, v = w.w
                if deps.get(s, 0) < v:
                    deps[s] = v
            for s, v in w.r.items():
                if deps.get(s, 0) < v:
                    deps[s] = v
        for s, v in deps.items():
            if s is eng.src and eng.name == "pe":
                continue
            if eng.seen.get(s, 0) >= v:
                continue
            eng.seen[s] = v
            eng.ops.append(("wait", s.sem, v))

    def op(self, eng, fn, reads=(), writes=()):
        if getattr(self, "count_active", False):
            self.cnt = getattr(self, "cnt", 0) + 1
            if OP_LIMIT is not None and self.cnt > OP_LIMIT:
                return
        e = self.engs[eng]
        self._waits(e, reads, writes)
        e.src.val += 1
        v = e.src.val
        e.ops.append(("ins", fn, e.src.sem, 1))
        for r in reads:
            if r.r.get(e.src, 0) < v:
                r.r[e.src] = v
        for w in writes:
            w.w = (e.src, v)
            w.r = {}

    def dma(self, eng, src, out, in_, reads=(), writes=(), **kw):
        e = self.engs[eng]
        self._waits(e, reads, writes)
        src.val += 16
        v = src.val
        e.ops.append(("ins", lambda q: q.dma_start(out=out, in_=in_, **kw), src.sem, 16))
        for r in reads:
            if r.r.get(src, 0) < v:
                r.r[src] = v
        for w in writes:
            w.w = (src, v)
            w.r = {}

    def wait_all(self, eng, regs):
        e = self.engs[eng]
        self._waits(e, (), regs)

    def replay(self, block):
        handles = {"pe": block.tensor, "act": block.scalar, "dve": block.vector,
                   "pool": block.gpsimd, "sp": block.sync}
        for n, e in self.engs.items():
            ops = e.ops

            def body(q, ops=ops):
                for o in ops:
                    if o[0] == "wait":
                        q.wait_ge(o[1], o[2])
                    else:
                        o[1](q).then_inc(o[2], o[3])
            handles[n](body)


class Arena:
    def __init__(self, ap_bf16, nbytes):
        self.ap = ap_bf16
        self.nbytes = nbytes
        self.off = 0

    def mark(self):
        return self.off

    def reset(self, m=0):
        self.off = m

    def alloc(self, shape, dtype, parts=128):
        n = 1
        for s in shape:
            n *= s
        esz = 4 if dtype == F32 else 2
        nb = (n * esz + 31) // 32 * 32
        assert self.off + nb <= self.nbytes, ("arena overflow", self.off, nb, self.nbytes)
        a = self.ap[0:parts, self.off // 2:(self.off + n * esz) // 2]
        self.off += nb
        if dtype == F32:
            a = a.bitcast(F32)
        if len(shape) == 2:
            a = a.rearrange("p (a b) -> p a b", a=shape[0])
        elif len(shape) == 3:
            a = a.rearrange("p (a b c) -> p a b c", a=shape[0], b=shape[1])
        elif len(shape) == 4:
            a = a.rearrange("p (a b c d) -> p a b c d", a=shape[0], b=shape[1], c=shape[2])
        return a


def I_act(out, in_, func, **kw):
    return lambda q: q.activation(out=out, in_=in_, func=func, **kw)


def I_tt(out, in0, in1, op):
    return lambda q: q.tensor_tensor(out=out, in0=in0, in1=in1, op=op)


def I_stt(out, in0, scalar, in1, op0, op1):
    return lambda q: q.scalar_tensor_tensor(out=out, in0=in0, scalar=scalar, in1=in1, op0=op0, op1=op1)


def I_ts(out, in0, s1, s2, op0, op1=None):
    if op1 is None:
        return lambda q: q.tensor_scalar(out=out, in0=in0, scalar1=s1, scalar2=None, op0=op0)
    return lambda q: q.tensor_scalar(out=out, in0=in0, scalar1=s1, scalar2=s2, op0=op0, op1=op1)


def I_recip(out, in_):
    return lambda q: q.reciprocal(out=out, in_=in_)


def I_copy(out, in_):
    return lambda q: q.tensor_copy(out=out, in_=in_)


def I_memset(out, val):
    return lambda q: q.memset(out, val)


def I_mm(out, lhsT, rhs, start=True, stop=True):
    return lambda q: q.matmul(out, lhsT=lhsT, rhs=rhs, start=start, stop=stop)


def I_mmgroup(out, pairs):
    pairs = list(pairs)

    def f(q):
        ins = None
        n = len(pairs)
        for i, (l, r) in enumerate(pairs):
            ins = q.matmul(out, lhsT=l, rhs=r, start=(i == 0), stop=(i == n - 1))
        return ins
    return f


def I_multi(items, r=False):
    items = list(items)
    if r and DBL_R:
        items = [(o, l.bitcast(FP32R), rr.bitcast(FP32R)) for (o, l, rr) in items]

    def f(q):
        ins = None
        for (o, l, r) in items:
            ins = q.matmul(o, lhsT=l, rhs=r, start=True, stop=True)
        return ins
    return f


def I_multi_tr(items):
    items = list(items)

    def f(q):
        ins = None
        for (o, i, ident) in items:
            ins = q.transpose(o, i, ident)
        return ins
    return f


def I_transpose(out, in_, ident):
    return lambda q: q.transpose(out, in_, ident)


class Builder:
    def __init__(self, n_seq, phases):
        from contextlib import ExitStack
        self.n_seq = n_seq
        self.phases = phases
        self.stack = ExitStack()
        nc = self.nc = bass.Bass("TRN2", target_bir_lowering=False)
        st = self.stack
        self.P = Prog(nc, st)
        dt = nc.dram_tensor
        self.d_xT = dt("xT", [n_seq, KC, 128, S], F32, kind="ExternalInput").ap()
        self.d_out = dt("outT", [n_seq, KC, 128, S], F32, kind="ExternalOutput").ap()
        self.d_normg = dt("normg", [128, 13 * KC], F32, kind="ExternalInput").ap()
        self.d_wgu = dt("wgu", [2 * DEPTH, FC, 128, 2 * KC * 128], F32, kind="ExternalInput").ap()
        self.d_wd = dt("wd", [2 * DEPTH, KC, 128, FC * 128], F32, kind="ExternalInput").ap()
        self.d_awin = dt("awin", [2, 4, 128, KC, 3 * 384], F32, kind="ExternalInput").ap()
        self.d_awout = dt("awout", [2, 128, 4, D], F32, kind="ExternalInput").ap()
        self.d_bt = dt("bt", [4, 128, 6, 256], F32, kind="ExternalInput").ap()
        self.d_bwin = dt("bwin", [2, 8, 128, KC, 512], F32, kind="ExternalInput").ap()
        self.d_wab = dt("wab", [2, 128, KC, 64], F32, kind="ExternalInput").ap()
        self.d_wabt = dt("wabt", [2, 128, KC, 16], F32, kind="ExternalInput").ap()
        self.d_bconv = dt("bconv", [2, 8, 128, 12], F32, kind="ExternalInput").ap()
        self.d_bsmall = dt("bsmall", [2, 128, 259], F32, kind="ExternalInput").ap()
        self.d_bwout = dt("bwout", [2, 8, 128, D], F32, kind="ExternalInput").ap()
        self.d_bconst = dt("bconst", [128, 13 * 128], F32, kind="ExternalInput").ap()
        sb = lambda name, shape, dtype: st.enter_context(nc.sbuf_tensor(name, shape, dtype))
        self.xT = sb("xTs", [128, KC, S], F32)
        self.hT = sb("hTs", [128, KC, S], BF16)
        self.normg = sb("normg_s", [128, 13 * KC], F32)
        self.ones_bf = sb("ones_bf", [128, 128], BF16)
        self.n_sq = [sb("n_sq%d" % i, [128, WIN], BF16) for i in range(2)]
        self.n_rt = sb("n_rt", [128, WIN], F32)
        self.n_rstd = sb("n_rstd", [128, WIN], F32)
        self.r_nsq = [Reg("sq0"), Reg("sq1")]
        self.r_nrt, self.r_nrstd = Reg("rt"), Reg("rstd")
        ARENA = (nc.sbuf_bytes_remaining - 2048) // 64 * 64
        self.arena_t = sb("arena", [128, ARENA // 2], BF16)
        self.arena = Arena(self.arena_t, ARENA)
        self.banks = [st.enter_context(nc.psum_tensor("bank%d" % i, [128, 512], F32)) for i in range(8)]
        self.bank_regs = [Reg("bank%d" % i) for i in range(8)]
        self.r_x = [[Reg("x%d_%d" % (k, w)) for w in range(NW)] for k in range(KC)]
        self.r_h = [[Reg("h%d_%d" % (k, w)) for w in range(NW)] for k in range(KC)]
        self.r_const = Reg("const")
        self.s_io = [self.P.dma_src() for _ in range(KC)]
        self.s_const = self.P.dma_src()

    def emit_consts(self):
        P = self.P
        P.dma("sp", self.s_const, self.normg[:], self.d_normg[:, :], writes=[self.r_const])
        ones = self.ones_bf
        P.op("dve", I_memset(ones[:], 1.0), writes=[self.r_const])

    def emit_load_x(self, s):
        P = self.P
        for k in range(KC):
            P.dma("sp", self.s_io[k], self.xT[:, k, :], self.d_xT[s, k, :, :], writes=self.r_x[k])

    def emit_store_x(self, s, src_is_h=False):
        P = self.P
        for k in range(KC):
            P.dma("sp", self.s_io[k], self.d_out[s, k, :, :], self.xT[:, k, :], reads=self.r_x[k])

    def gen_rmsnorm(self, norm_idx, windows, out_f32_inplace=False):
        P = self.P
        sq = [t[:] for t in self.n_sq]
        r_sq = self.r_nsq
        rt, rstd = self.n_rt[:], self.n_rstd[:]
        r_rt, r_rstd = self.r_nrt, self.r_nrstd
        xT, hT, ones, normg = self.xT, self.hT, self.ones_bf, self.normg
        bank = self.banks[7]
        r_bank = self.bank_regs[7]
        for w in windows:
            ws = slice(w * WIN, (w + 1) * WIN)
            for k in range(KC):
                b = k % 2
                P.op("act", I_act(sq[b], xT[:, k, ws], AF.Square),
                     reads=[self.r_x[k][w]], writes=[r_sq[b]])
                P.op("pe", I_mm(bank[:], ones[:], sq[b], start=(k == 0), stop=(k == KC - 1)),
                     reads=[r_sq[b], self.r_const], writes=[r_bank])
                yield
            P.op("act", I_act(rt, bank[:], AF.Ln, bias=EPS, scale=1.0 / D),
                 reads=[r_bank], writes=[r_rt])
            P.op("act", I_act(rstd, rt, AF.Exp, scale=-0.5), reads=[r_rt], writes=[r_rstd])
            yield
            for k in range(KC):
                g_ap = normg[:, norm_idx * KC + k:norm_idx * KC + k + 1]
                if out_f32_inplace:
                    P.op("dve", I_stt(xT[:, k, ws], xT[:, k, ws], g_ap, rstd, ALU.mult, ALU.mult),
                         reads=[r_rstd, self.r_const], writes=[self.r_x[k][w]])
                else:
                    P.op("dve", I_stt(hT[:, k, ws], xT[:, k, ws], g_ap, rstd, ALU.mult, ALU.mult),
                         reads=[self.r_x[k][w], r_rstd, self.r_const], writes=[self.r_h[k][w]])
                yield

    def emit_rmsnorm(self, norm_idx, windows, out_f32_inplace=False):
        for _ in self.gen_rmsnorm(norm_idx, windows, out_f32_inplace):
            pass

    def emit_ffn(self, ffn_idx, norm_idx, pre_normed=False, next_norm_idx=None):
        P = self.P
        ar = self.arena
        xT, hT = self.xT, self.hT
        HW = 2
        NH = NW // HW
        m = ar.mark()
        aT = ar.alloc([FC, HW * WIN], BF16)
        r_a = [[Reg("a") for _ in range(HW)] for _ in range(FC)]
        NGU = 4
        wgu = [ar.alloc([2, KC, 128], BF16) for _ in range(NGU)]
        r_wgu = [Reg("wgu%d" % i) for i in range(NGU)]
        s_wgu = self.s_wgu
        NWD = 3
        wd = [ar.alloc([FC, 128], BF16) for _ in range(NWD)]
        r_wd = [Reg("wd%d" % i) for i in range(NWD)]
        s_wd = self.s_wd
        sil = [ar.alloc([WIN], F32) for _ in range(2)]
        r_sil = [Reg("sil0"), Reg("sil1")]
        it = 0

        def load_wgu(f):
            sl = f % NGU
            P.dma("pool", s_wgu[sl], wgu[sl].rearrange("p a b c -> p (a b c)"),
                  self.d_wgu[ffn_idx, f, :, :], writes=[r_wgu[sl]])

        def load_wd(gd):
            sl = gd % NWD
            d = gd % KC
            hf = FC // 2
            P.dma("pool", s_wd[sl], wd[sl][:, 0:hf, :].rearrange("p a b -> p (a b)"),
                  self.d_wd[ffn_idx, d, :, 0:hf * 128], writes=[r_wd[sl]])
            P.dma("pool", s_wd[sl], wd[sl][:, hf:FC, :].rearrange("p a b -> p (a b)"),
                  self.d_wd[ffn_idx, d, :, hf * 128:FC * 128], writes=[r_wd[sl]])

        for f in range(NGU):
            load_wgu(f)
        for d in range(NWD):
            load_wd(d)
        if not pre_normed:
            self.emit_rmsnorm(norm_idx, [0, 1])
        for half in range(NH):
            wins = [half * HW + i for i in range(HW)]
            for f in range(FC):
                sl = f % NGU
                for wi, w in enumerate(wins):
                    ws = slice(w * WIN, (w + 1) * WIN)
                    pb = it % 2
                    it += 1
                    bg, bu = self.banks[pb * 2], self.banks[pb * 2 + 1]
                    rg, ru = self.bank_regs[pb * 2], self.bank_regs[pb * 2 + 1]
                    rd = [r_wgu[sl]] + [self.r_h[k][w] for k in range(KC)]
                    P.op("pe", I_mmgroup(bg[:], [(wgu[sl][:, 0, k, :], hT[:, k, ws]) for k in range(KC)]),
                         reads=rd, writes=[rg])
                    P.op("pe", I_mmgroup(bu[:], [(wgu[sl][:, 1, k, :], hT[:, k, ws]) for k in range(KC)]),
                         reads=rd, writes=[ru])
                    sb_ = it % 2
                    P.op("act", I_act(sil[sb_], bg[:], AF.Silu), reads=[rg], writes=[r_sil[sb_]])
                    P.op("dve", I_tt(aT[:, f, wi * WIN:(wi + 1) * WIN], sil[sb_], bu[:], ALU.mult),
                         reads=[r_sil[sb_], ru], writes=[r_a[f][wi]])
                if f + NGU < FC:
                    load_wgu(f + NGU)
            if half + 1 < NH:
                for f in range(NGU):
                    load_wgu(f)
                side = self.gen_rmsnorm(norm_idx, [(half + 1) * HW + i for i in range(HW)])
            elif next_norm_idx is not None:
                side = self.gen_rmsnorm(next_norm_idx, [0, 1])
            else:
                side = iter(())
            for d in range(KC):
                gd = half * KC + d
                sl = gd % NWD
                for wi, w in enumerate(wins):
                    ws = slice(w * WIN, (w + 1) * WIN)
                    pb = it % 2
                    it += 1
                    by, ry = self.banks[4 + pb], self.bank_regs[4 + pb]
                    P.op("pe", I_mmgroup(by[:], [(wd[sl][:, f, :], aT[:, f, wi * WIN:(wi + 1) * WIN]) for f in range(FC)]),
                         reads=[r_wd[sl]] + [r_a[f][wi] for f in range(FC)], writes=[ry])
                    P.op("dve", I_stt(xT[:, d, ws], by[:], 0.5, xT[:, d, ws], ALU.mult, ALU.add),
                         reads=[ry, self.r_x[d][w]], writes=[self.r_x[d][w]])
                    for _ in range(3):
                        next(side, None)
                if gd + NWD < NH * KC:
                    load_wd(gd + NWD)
            for _ in side:
                pass
        self.phase_barrier([r_wgu, r_wd, r_sil, [r for rr in r_a for r in rr]])
        ar.reset(m)

    def emit_mixa(self, j, norm_idx):
        P = self.P
        ar = self.arena
        xT, hT = self.xT, self.hT
        banks, bregs = self.banks, self.bank_regs
        self.emit_rmsnorm(norm_idx, range(NW))
        r_hall = [self.r_h[k][w] for k in range(KC) for w in range(NW)]
        m = ar.mark()
        NWB = 3
        wbuf = [ar.alloc([KC, 384], BF16) for _ in range(NWB)]
        r_wbuf = [Reg("awb%d" % i) for i in range(NWB)]
        wout = ar.alloc([4, D], BF16)
        r_wout = Reg("awout")
        ebuf = [ar.alloc([6, 256], F32) for _ in range(2)]
        r_ebuf = [Reg("eb0"), Reg("eb1")]
        qT = [ar.alloc([S], BF16) for _ in range(2)]
        kT = [ar.alloc([S], BF16) for _ in range(2)]
        r_qk = [Reg("qk0"), Reg("qk1")]
        vbuf = [ar.alloc([16, 128], BF16) for _ in range(2)]
        r_v = [Reg("v0"), Reg("v1")]
        accn = ar.alloc([S], F32)
        accd = ar.alloc([S], F32)
        r_accn, r_accd = Reg("accn"), Reg("accd")
        oT = ar.alloc([S], BF16)
        r_oT = Reg("oT")
        NE = 4
        ebf = [ar.alloc([256], F32) for _ in range(NE)]
        r_ebf = [Reg("E%d" % i) for i in range(NE)]
        NPT = 8
        pT = [ar.alloc([256], BF16) for _ in range(NPT)]
        r_pT = [Reg("pT%d" % i) for i in range(NPT)]
        ones = self.ones_bf

        P.dma("pool", self.s_awout, wout, self.d_awout[j, :, :, :], writes=[r_wout])
        wl = [(pair, g) for pair in range(4) for g in range(3)]

        def load_w(i):
            pair, g = wl[i]
            sl = i % NWB
            P.dma("pool", self.s_awb[sl], wbuf[sl], self.d_awin[j, pair, :, :, g * 384:(g + 1) * 384],
                  writes=[r_wbuf[sl]])

        def load_eb(pair):
            sl = pair % 2
            P.dma("sp", self.s_eb[sl], ebuf[sl], self.d_bt[pair, :, :, :], writes=[r_ebuf[sl]])
            P.op("act", I_act(ebuf[sl], ebuf[sl], AF.Exp), reads=[r_ebuf[sl]], writes=[r_ebuf[sl]])

        for i in range(NWB):
            load_w(i)
        load_eb(0)
        pj = 0
        sj = 0
        ej = 0
        pj_t = 0
        nj = 0
        for i, (pair, g) in enumerate(wl):
            dil = (1, 4, 16)[g]
            L = S // dil
            wsl = i % NWB
            qs = i % 2
            if g == 0 and pair + 1 < 4:
                load_eb(pair + 1)
            eb = ebuf[pair % 2]
            r_eb = r_ebuf[pair % 2]
            for which, dst in ((0, qT[qs]), (1, kT[qs])):
                for w in range(NW):
                    ws = slice(w * WIN, (w + 1) * WIN)
                    b = pj % 2
                    pj += 1
                    P.op("pe", I_mmgroup(banks[b][:], [(wbuf[wsl][:, k, which * 128:(which + 1) * 128], hT[:, k, ws])
                                                       for k in range(KC)]),
                         reads=[r_wbuf[wsl]] + [self.r_h[k][w] for k in range(KC)], writes=[bregs[b]])
                    nl = WIN // dil
                    if dil == 1:
                        o_ap = dst[:, ws]
                        i_ap = banks[b][:]
                    else:
                        o_ap = dst.rearrange("p (c l) -> p c l", c=dil)[:, :, w * nl:(w + 1) * nl]
                        i_ap = banks[b][:].rearrange("p (l c) -> p c l", c=dil)
                    if which == 0:
                        P.op("act", I_act(o_ap, i_ap, AF.Copy, scale=0.125), reads=[bregs[b]], writes=[r_qk[qs]])
                    else:
                        P.op("dve", I_copy(o_ap, i_ap), reads=[bregs[b]], writes=[r_qk[qs]])
            nbl = L // 128
            for bq in range(4):
                b = pj % 2
                pj += 1
                for bi in range(4):
                    blk = bq * 4 + bi
                    c, n = blk // nbl, blk % nbl
                    t0 = n * 128 * dil + c
                    tsl = slice(t0, t0 + 127 * dil + 1, dil)
                    tw = set(t // WIN for t in (t0, t0 + 127 * dil))
                    P.op("pe", I_mmgroup(banks[b][:, bi * 128:(bi + 1) * 128],
                                         [(hT[:, k, tsl], wbuf[wsl][:, k, 256:384]) for k in range(KC)]),
                         reads=[r_wbuf[wsl]] + [self.r_h[k][w] for k in range(KC) for w in tw],
                         writes=[bregs[b]])
                P.op("dve", I_copy(vbuf[qs][:, bq * 4:(bq + 1) * 4, :].rearrange("p a b -> p (a b)"), banks[b][:]),
                     reads=[bregs[b]], writes=[r_v[qs]])
            if i + NWB < len(wl):
                load_w(i + NWB)
            groups = []
            for c in range(dil):
                for qg in range(max(1, nbl // 4)):
                    if nbl == 1:
                        if c % 4 != 0:
                            continue
                        groups.append(([(c + cc, 0) for cc in range(4)], c, qg))
                    else:
                        groups.append(([(c, qg * 4 + qq) for qq in range(4)], c, qg))
            seq = []
            for gi_, (items, c, qg) in enumerate(groups):
                for hp in range(2):
                    for col, (cc, qb) in enumerate(items):
                        seq.append((gi_, hp, col, cc, qb))
            tile_of = {}
            uses_left = {}
            slot_owner = [None] * NPT

            def need_of(cc, qb):
                return [kb for kb in (qb - 1, qb) if kb >= 0]

            def n_uses(cc, kb):
                return (1 if kb < nbl else 0) + (1 if kb + 1 < nbl else 0)

            def stage1(hp, cc, kb):
                nonlocal sj, ej, pj_t
                key = (hp, cc, kb)
                if key in tile_of:
                    return
                ps = slice(64 * hp, 64 * hp + 64)
                base = cc * L
                nq = min(256, L - kb * 128)
                sb_ = 2 + sj % 2
                sj += 1
                P.op("pe", I_mm(banks[sb_][:, 0:nq],
                                kT[qs][ps, base + kb * 128: base + kb * 128 + 128],
                                qT[qs][ps, base + kb * 128: base + kb * 128 + nq]),
                     reads=[r_qk[qs]], writes=[bregs[sb_]])
                e_i = ej % NE
                ej += 1
                P.op("act", I_act(ebf[e_i][:, 0:nq], banks[sb_][:, 0:nq], AF.Exp),
                     reads=[bregs[sb_]], writes=[r_ebf[e_i]])
                p_i = pj_t % NPT
                pj_t += 1
                assert slot_owner[p_i] is None or uses_left[slot_owner[p_i]] == 0, "pT ring too small"
                slot_owner[p_i] = key
                uses_left[key] = n_uses(cc, kb)
                P.op("pool", I_tt(pT[p_i][:, 0:nq], ebf[e_i][:, 0:nq], eb[:, g * 2 + hp, 0:nq], ALU.mult),
                     reads=[r_ebf[e_i], r_eb], writes=[r_pT[p_i]])
                tile_of[key] = p_i

            LA = 2
            gbank = {}
            for i_, (gi_, hp, col, cc, qb) in enumerate(seq):
                for la in range(i_, min(i_ + LA + 1, len(seq))):
                    _, hp2, _, cc2, qb2 = seq[la]
                    for kb in need_of(cc2, qb2):
                        stage1(hp2, cc2, kb)
                if gi_ not in gbank:
                    gbank[gi_] = nj % 2
                    nj += 1
                bsel = gbank[gi_]
                bn, bd = banks[4 + bsel], banks[6 + bsel]
                rbn, rbd = bregs[4 + bsel], bregs[6 + bsel]
                ps = slice(64 * hp, 64 * hp + 64)
                pairs_n, pairs_d, rd = [], [], [r_v[qs], self.r_const]
                for kb in need_of(cc, qb):
                    key = (hp, cc, kb)
                    p_i = tile_of[key]
                    assert slot_owner[p_i] == key
                    uses_left[key] -= 1
                    off = (qb - kb) * 128
                    blk = cc * nbl + kb
                    pairs_n.append((vbuf[qs][:, blk, 64 * hp:64 * hp + 64], pT[p_i][:, off:off + 128]))
                    pairs_d.append((ones[:, 0:64], pT[p_i][:, off:off + 128]))
                    rd.append(r_pT[p_i])
                P.op("pe", I_mmgroup(bn[ps, col * 128:(col + 1) * 128], pairs_n), reads=rd, writes=[rbn])
                P.op("pe", I_mmgroup(bd[ps, col * 128:(col + 1) * 128], pairs_d), reads=rd, writes=[rbd])
                last_of_group = (i_ + 1 == len(seq)) or (seq[i_ + 1][0] != gi_)
                if last_of_group:
                    items, c, qg = groups[gi_]
                    if dil == 1:
                        an = accn[:, qg * 512:(qg + 1) * 512]
                        ad = accd[:, qg * 512:(qg + 1) * 512]
                        sn, sd = bn[:], bd[:]
                    elif nbl == 1:
                        an = accn.rearrange("p (l c) -> p c l", c=dil)[:, c:c + 4, :]
                        ad = accd.rearrange("p (l c) -> p c l", c=dil)[:, c:c + 4, :]
                        sn = bn[:].rearrange("p (c l) -> p c l", c=4)
                        sd = bd[:].rearrange("p (c l) -> p c l", c=4)
                    else:
                        an = accn.rearrange("p (l c) -> p c l", c=dil)[:, c, qg * 512:(qg + 1) * 512]
                        ad = accd.rearrange("p (l c) -> p c l", c=dil)[:, c, qg * 512:(qg + 1) * 512]
                        sn, sd = bn[:], bd[:]
                    if g == 0:
                        P.op("dve", I_copy(an, sn), reads=[rbn], writes=[r_accn])
                        P.op("act", I_act(ad, sd, AF.Copy), reads=[rbd], writes=[r_accd])
                    else:
                        P.op("dve", I_tt(an, sn, an, ALU.add), reads=[rbn, r_accn], writes=[r_accn])
                        P.op("dve", I_tt(ad, sd, ad, ALU.add), reads=[rbd, r_accd], writes=[r_accd])
            if g == 2:
                P.op("act", I_act(accd, accd, AF.Ln), reads=[r_accd], writes=[r_accd])
                P.op("act", I_act(accd, accd, AF.Exp, scale=-1.0), reads=[r_accd], writes=[r_accd])
                P.op("dve", I_tt(oT, accn, accd, ALU.mult), reads=[r_accn, r_accd], writes=[r_oT])
                for dch in range(KC):
                    for w in range(NW):
                        ws = slice(w * WIN, (w + 1) * WIN)
                        b = pj % 2
                        pj += 1
                        P.op("pe", I_mm(banks[b][:], wout[:, pair, dch * 128:(dch + 1) * 128], oT[:, ws]),
                             reads=[r_wout, r_oT], writes=[bregs[b]])
                        P.op("dve", I_tt(xT[:, dch, ws], banks[b][:], xT[:, dch, ws], ALU.add),
                             reads=[bregs[b], self.r_x[dch][w]], writes=[self.r_x[dch][w]])
        self.phase_barrier([r_wbuf, [r_wout], r_ebuf, r_qk, r_v, [r_accn, r_accd, r_oT], r_ebf, r_pT])
        ar.reset(m)

    def emit_mixb(self, j, norm_idx):
        P = self.P
        ar = self.arena
        xT, hT = self.xT, self.hT
        banks, bregs = self.banks, self.bank_regs
        self.emit_rmsnorm(norm_idx, range(NW))
        m = ar.mark()
        NB = S // 128
        cst = ar.alloc([13 * 128], F32)
        r_cst = Reg("bcst")
        TRI, SELL, IDN, MUI, MUS = [cst[:, i * 128:(i + 1) * 128] for i in range(5)]
        SEL = cst[:, 5 * 128:13 * 128].rearrange("p (h m) -> p h m", h=8)
        idb = ar.alloc([128], BF16)
        small = ar.alloc([3 + 256], F32)
        r_small = Reg("bsmall")
        dtb_col, alog_col, normw_col = small[:, 0:1], small[:, 1:2], small[:, 2:3]
        dtb_rep, alog_rep = small[:, 3:131], small[:, 131:259]
        nega = ar.alloc([1 + 128], F32)
        r_nega = Reg("nega")
        abT = ar.alloc([S], F32)
        r_abT = Reg("abT")
        tok = ar.alloc([4, 128], F32)
        gc_tok, beta_tok, bg_tok, kt_tok = [tok[:, i, :] for i in range(4)]
        r_tok = Reg("tok")
        wab = ar.alloc([KC, 64], BF16)
        wabt = ar.alloc([KC, 16], BF16)
        r_wab = Reg("wab")
        wh = [ar.alloc([KC, 256], BF16) for _ in range(3)]
        r_wh = [Reg("wh%d" % i) for i in range(3)]
        wo = [ar.alloc([D], BF16) for _ in range(2)]
        r_wo = [Reg("wo0"), Reg("wo1")]
        hv = [ar.alloc([12], F32) for _ in range(2)]
        r_hv = [Reg("hv0"), Reg("hv1")]
        raw = ar.alloc([S + 4], F32)
        r_raw_w = [Reg("raw%d" % i) for i in range(NW + 1)]
        r_raw = r_raw_w[0]
        acc = ar.alloc([S], F32)
        r_acc_w = [Reg("acc%d" % i) for i in range(NW)]
        r_acc = r_acc_w[0]
        qT = ar.alloc([S], BF16)
        kT = ar.alloc([S], BF16)
        vT = ar.alloc([S], BF16)
        qdT = ar.alloc([S], BF16)
        zsT = ar.alloc([S], BF16)
        r_q, r_k, r_v, r_qd, r_zs = Reg("q"), Reg("k"), Reg("v"), Reg("qd"), Reg("zs")
        u = ar.alloc([NB, 128], BF16)
        wT = ar.alloc([S], BF16)
        attnT = ar.alloc([NB, 128], BF16)
        ktail = ar.alloc([NB, 2, 128], BF16)
        r_u = [Reg("u%d" % b) for b in range(NB)]
        r_wT = [Reg("wT%d" % b) for b in range(NB)]
        r_at = [Reg("at%d" % b) for b in range(NB)]
        r_kt = [Reg("kt%d" % b) for b in range(NB)]
        r_qdb = [Reg("qd%d" % b) for b in range(NB)]
        cdv = ar.alloc([2 * NB], F32)
        r_cdv = Reg("cdv")
        r_W = [Reg("W%d" % i) for i in range(8)]
        NT = 3
        tmp = [ar.alloc([128], F32) for _ in range(NT)]
        r_tmp = [Reg("t%d" % i) for i in range(NT)]
        tcount = [0]

        def T():
            i = tcount[0] % NT
            tcount[0] += 1
            return tmp[i], r_tmp[i]
        Sf = ar.alloc([128], F32)
        Sb = ar.alloc([128], BF16)
        r_Sf, r_Sb = Reg("Sf"), Reg("Sb")
        vnew = ar.alloc([128], BF16)
        r_vnew = Reg("vnew")
        osb = ar.alloc([WIN], F32)
        r_osb = Reg("osb")
        rt2 = ar.alloc([WIN], F32)
        rstd2 = ar.alloc([WIN], F32)
        r_rt2, r_rstd2 = Reg("rt2"), Reg("rstd2")
        ogT = ar.alloc([WIN], BF16)
        r_og = Reg("og")
        zt, r_zt = osb, r_osb
        ones = self.ones_bf
        sq, r_sq = [t[:] for t in self.n_sq], self.r_nsq
        rt, rstd, r_rt, r_rstd = self.n_rt[:], self.n_rstd[:], self.r_nrt, self.r_nrstd
        place = {(4, 0): (4, 0), (4, 1): (5, 2), (4, 2): (4, 2), (4, 3): (4, 3),
                 (5, 0): (5, 0), (5, 1): (5, 1), (5, 2): (0, 0), (5, 3): (1, 0),
                 (6, 0): (6, 0), (6, 1): (6, 1), (6, 2): (2, 0), (6, 3): (3, 0),
                 (7, 0): (7, 0), (7, 1): (6, 2)}
        qb = {}
        qr = {}
        for key, (b, q) in place.items():
            qb[key] = banks[b][:, q * 128:(q + 1) * 128]
            qr[key] = bregs[b]

        def qbf(b, q):
            b, q = place[(b, q)]
            return banks[b][:].bitcast(BF16)[:, q * 256:q * 256 + 128]

        P.dma("sp", self.s_bc[0], cst, self.d_bconst[:, :], writes=[r_cst])
        P.dma("sp", self.s_bc[1], small, self.d_bsmall[j, :, :], writes=[r_small])
        P.dma("pool", self.s_bc[2], wab, self.d_wab[j, :, :, :], writes=[r_wab])
        P.dma("pool", self.s_bc[2], wabt, self.d_wabt[j, :, :, :], writes=[r_wab])
        P.op("dve", I_copy(idb, IDN), reads=[r_cst], writes=[r_cst])
        P.op("dve", I_memset(raw[:, 0:4], 0.0), writes=[r_raw])
        P.op("dve", I_memset(ktail.rearrange("p a b c -> p (a b c)"), 0.0), writes=r_kt)
        P.op("act", I_act(nega[:, 0:1], alog_col, AF.Exp), reads=[r_small], writes=[r_nega])
        P.op("act", I_act(nega[:, 1:129], alog_rep, AF.Exp), reads=[r_small], writes=[r_nega])
        P.op("dve", I_ts(nega, nega, -1.0, None, ALU.mult), reads=[r_nega], writes=[r_nega])

        def load_head(h):
            sl = h % 2
            for half in range(2):
                i = (2 * h + half) % 3
                P.dma("pool", self.s_wh[i], wh[i], self.d_bwin[j, h, :, :, half * 256:(half + 1) * 256],
                      writes=[r_wh[i]])
            P.dma("pool", self.s_wo[sl], wo[sl], self.d_bwout[j, h, :, :], writes=[r_wo[sl]])
            P.dma("sp", self.s_hv[sl], hv[sl], self.d_bconv[j, h, :, :], writes=[r_hv[sl]])

        load_head(0)
        tA, r_tA = acc, r_acc
        r_tA_all = r_acc_w
        for w in range(NW):
            ws = slice(w * WIN, (w + 1) * WIN)
            b = w % 2
            P.op("pe", I_mmgroup(banks[b][0:64, :], [(wab[:, k, :], hT[:, k, ws]) for k in range(KC)]),
                 reads=[r_wab] + [self.r_h[k][w] for k in range(KC)], writes=[bregs[b]])
            P.op("act", I_act(tA[0:8, ws], banks[b][0:8, :], AF.Exp, bias=dtb_col[0:8, :]),
                 reads=[bregs[b], r_small], writes=r_tA_all)
            P.op("act", I_act(tA[0:8, ws], tA[0:8, ws], AF.Ln, bias=1.0), reads=r_tA_all, writes=r_tA_all)
            P.op("dve", I_ts(abT[0:8, ws], tA[0:8, ws], nega[0:8, 0:1], None, ALU.mult),
                 reads=r_tA_all + [r_nega], writes=[r_abT])
            P.op("act", I_act(abT[32:40, ws], banks[b][32:40, :], AF.Exp, scale=-1.0),
                 reads=[bregs[b]], writes=[r_abT])
            P.op("dve", I_ts(abT[32:40, ws], abT[32:40, ws], 1.0, None, ALU.add), reads=[r_abT], writes=[r_abT])
            P.op("dve", I_recip(abT[32:40, ws], abT[32:40, ws]), reads=[r_abT], writes=[r_abT])
        src_, dst_ = abT, tA
        for st_ in (1, 2, 4, 8, 16, 32):
            sv = src_[0:8, :].rearrange("p (n c) -> p n c", c=64)
            dv = dst_[0:8, :].rearrange("p (n c) -> p n c", c=64)
            P.op("dve", I_tt(dv[:, :, st_:64], sv[:, :, st_:64], sv[:, :, 0:64 - st_], ALU.add),
                 reads=[r_abT] + r_tA_all, writes=[r_abT] + r_tA_all)
            P.op("dve", I_copy(dv[:, :, 0:st_], sv[:, :, 0:st_]), reads=[r_abT] + r_tA_all, writes=[r_abT] + r_tA_all)
            src_, dst_ = dst_, src_
        assert src_ is abT
        for blk in range(NB):
            P.op("pe", I_mmgroup(banks[2][:, blk * 16:(blk + 1) * 16],
                                 [(hT[:, k, blk * 128:(blk + 1) * 128], wabt[:, k, :]) for k in range(KC)]),
                 reads=[r_wab] + [self.r_h[k][blk // 4] for k in range(KC)], writes=[bregs[2]])
        abv = banks[2][:, 0:256].rearrange("p (b c) -> p b c", c=16)
        t1, r_t1 = T()
        t2, r_t2 = T()
        v3 = lambda a: a.rearrange("p (b c) -> p b c", c=8)
        P.op("dve", I_tt(v3(t1), abv[:, :, 0:8], v3(dtb_rep), ALU.add), reads=[bregs[2], r_small], writes=[r_t1])
        P.op("act", I_act(t1, t1, AF.Exp), reads=[r_t1], writes=[r_t1])
        P.op("act", I_act(t1, t1, AF.Ln, bias=1.0), reads=[r_t1], writes=[r_t1])
        P.op("dve", I_tt(t1, t1, nega[:, 1:129], ALU.mult), reads=[r_t1, r_nega], writes=[r_t1])
        P.op("act", I_act(v3(beta_tok), abv[:, :, 8:16], AF.Exp, scale=-1.0), reads=[bregs[2]], writes=[r_tok])
        P.op("dve", I_ts(beta_tok, beta_tok, 1.0, None, ALU.add), reads=[r_tok], writes=[r_tok])
        P.op("dve", I_recip(beta_tok, beta_tok), reads=[r_tok], writes=[r_tok])
        P.op("pe", I_mm(qb[(4, 0)], TRI, t1), reads=[r_cst, r_t1], writes=[qr[(4, 0)]])
        P.op("act", I_act(gc_tok, qb[(4, 0)], AF.Copy), reads=[qr[(4, 0)]], writes=[r_tok])
        P.op("pe", I_mm(qb[(4, 1)], SELL, gc_tok), reads=[r_cst, r_tok], writes=[qr[(4, 1)]])
        P.op("dve", I_tt(t2, qb[(4, 1)], gc_tok, ALU.subtract), reads=[qr[(4, 1)], r_tok], writes=[r_t2])
        P.op("act", I_act(kt_tok, t2, AF.Exp), reads=[r_t2], writes=[r_tok])
        P.op("act", I_act(t2, gc_tok, AF.Exp), reads=[r_tok], writes=[r_t2])
        P.op("dve", I_tt(bg_tok, beta_tok, t2, ALU.mult), reads=[r_tok, r_t2], writes=[r_tok])

        pjc = [0]

        def gen_h2(h):
            hs = h % 2
            wq = wh[(2 * h) % 3]
            wv = wh[(2 * h + 1) % 3]
            r_wq, r_wv = r_wh[(2 * h) % 3], r_wh[(2 * h + 1) % 3]
            taps = hv[hs]
            items = [(X, w) for X in range(3) for w in range(NW)]
            pb = {}

            def stA(i):
                X, w = items[i]
                wsrc, r_wsrc = (wq, r_wq) if X < 2 else (wv, r_wv)
                co = (X % 2) * 128
                ws = slice(w * WIN, (w + 1) * WIN)
                b = 4 + i % 2
                pb[i] = b
                P.op("pe", I_mmgroup(banks[b][:], [(wsrc[:, k, co:co + 128], hT[:, k, ws]) for k in range(KC)]),
                     reads=[r_wsrc] + [self.r_h[k][w] for k in range(KC)], writes=[bregs[b]])

            def stB(i):
                X, w = items[i]
                b = pb[i]
                P.op("act", I_act(raw[:, 4 + w * WIN:4 + (w + 1) * WIN], banks[b][:], AF.Copy),
                     reads=[bregs[b]], writes=[r_raw_w[1 + w]])

            def stC(i):
                X, w = items[i]
                ws = slice(w * WIN, (w + 1) * WIN)
                P.op("dve", I_ts(acc[:, ws], raw[:, 4 + w * WIN:4 + (w + 1) * WIN],
                                 taps[:, X * 4 + 3:X * 4 + 4], None, ALU.mult),
                     reads=[r_raw_w[1 + w], r_hv[hs]], writes=[r_acc_w[w]])
                for tp in (2, 1, 0):
                    P.op("dve", I_stt(acc[:, ws], raw[:, 1 + tp + w * WIN:1 + tp + (w + 1) * WIN],
                                      taps[:, X * 4 + tp:X * 4 + tp + 1], acc[:, ws], ALU.mult, ALU.add),
                         reads=[r_raw_w[w], r_raw_w[1 + w], r_hv[hs], r_acc_w[w]], writes=[r_acc_w[w]])

            def stD1(i):
                X, w = items[i]
                ws = slice(w * WIN, (w + 1) * WIN)
                if X == 2:
                    P.op("act", I_act(vT[:, ws], acc[:, ws], AF.Silu), reads=[r_acc_w[w]], writes=[r_v])
                else:
                    P.op("act", I_act(acc[:, ws], acc[:, ws], AF.Silu), reads=[r_acc_w[w]], writes=[r_acc_w[w]])
                    P.op("act", I_act(sq[1], acc[:, ws], AF.Square), reads=[r_acc_w[w]], writes=[r_sq[1]])

            def stD2(i):
                X, w = items[i]
                if X < 2:
                    P.op("pe", I_mm(banks[3][:], ones[:], sq[1]), reads=[r_sq[1], self.r_const], writes=[bregs[3]])

            def stD3(i):
                X, w = items[i]
                if X < 2:
                    P.op("act", I_act(rt, banks[3][:], AF.Ln, bias=EPS, scale=1.0), reads=[bregs[3]], writes=[r_rt])
                    P.op("act", I_act(rstd, rt, AF.Exp, scale=-0.5), reads=[r_rt], writes=[r_rstd])

            def stD4(i):
                X, w = items[i]
                if X < 2:
                    ws = slice(w * WIN, (w + 1) * WIN)
                    dst, r_dst = (qT, r_q) if X == 0 else (kT, r_k)
                    sc = (128.0 ** -0.5) if X == 0 else 1.0
                    P.op("dve", I_stt(dst[:, ws], acc[:, ws], sc, rstd, ALU.mult, ALU.mult),
                         reads=[r_acc_w[w], r_rstd], writes=[r_dst])

            stages = [stA, stB, stC, stD1, stD2, stD3, stD4]
            n = len(items)
            for t in range(n + len(stages) - 1):
                for si in range(len(stages) - 1, -1, -1):
                    i = t - si
                    if 0 <= i < n:
                        stages[si](i)
                        yield

        def emit_z(h):
            wv = wh[(2 * h + 1) % 3]
            r_wv = r_wh[(2 * h + 1) % 3]
            for w in range(NW):
                ws = slice(w * WIN, (w + 1) * WIN)
                b = pjc[0] % 2
                pjc[0] += 1
                P.op("pe", I_mmgroup(banks[b][:], [(wv[:, k, 128:256], hT[:, k, ws]) for k in range(KC)]),
                     reads=[r_wv] + [self.r_h[k][w] for k in range(KC)], writes=[bregs[b]])
                P.op("act", I_act(zt, banks[b][:], AF.Silu), reads=[bregs[b]], writes=[r_zt])
                P.op("dve", I_ts(zsT[:, ws], zt, normw_col, None, ALU.mult), reads=[r_zt, r_small], writes=[r_zs])

        def gen_scan(h):
            hs = h % 2
            P.op("dve", I_memset(Sf, 0.0), writes=[r_Sf])
            P.op("dve", I_memset(Sb, 0.0), writes=[r_Sb])
            P.op("dve", I_memset(vnew, 0.0), writes=[r_vnew])
            deferred = []
            for n in range(2 * NB):
                if n > 0:
                    yield
                blk, half = n // 2, n % 2
                ps = slice(64 * half, 64 * half + 64)
                cs = slice(n * 64, (n + 1) * 64)
                w = n // 8
                ob, r_ob = banks[2], bregs[2]
                bs = slice(blk * 128, (blk + 1) * 128)
                P.op("pe", I_mm(banks[7][:, 0:128], wT[:, bs], Sb), reads=[r_wT[blk], r_Sb], writes=[qr[(7, 0)]])
                yield
                P.op("dve", I_tt(vnew[ps, :], u[ps, blk, :], banks[7][ps, 0:128], ALU.subtract),
                     reads=[r_u[blk], qr[(7, 0)]], writes=[r_vnew])
                yield
                oc = (n % 8) * 64
                P.op("pe", I_mm(qb[(7, 1)], ktail[:, blk, half, :], vnew[:, :]), reads=[r_kt[blk], r_vnew],
                     writes=[qr[(7, 1)]])
                yield
                P.op("pe", I_mmgroup(ob[:, oc:oc + 64], [(Sb, qdT[:, cs]),
                                                        (vnew[:, :], attnT[:, blk, 64 * half:64 * half + 64])]),
                     reads=[r_Sb, r_qdb[blk], r_vnew, r_at[blk]], writes=[r_ob])
                yield
                P.op("dve", I_stt(Sb, Sf, cdv[:, n:n + 1], qb[(7, 1)], ALU.mult, ALU.add),
                     reads=[r_Sf, r_cdv, qr[(7, 1)]], writes=[r_Sb])
                yield
                P.op("dve", I_stt(Sf, Sf, cdv[:, n:n + 1], qb[(7, 1)], ALU.mult, ALU.add),
                     reads=[r_Sf, r_cdv, qr[(7, 1)]], writes=[r_Sf])
                if n % 8 == 7:
                    ws = slice(w * WIN, (w + 1) * WIN)

                    def s0(ob=ob, r_ob=r_ob):
                        P.op("act", I_act(osb, ob[:], AF.Copy), reads=[r_ob], writes=[r_osb])
                        P.op("act", I_act(sq[0], osb, AF.Square), reads=[r_osb], writes=[r_sq[0]])

                    def s1():
                        P.op("pe", I_mm(banks[0][:], ones[:], sq[0]), reads=[r_sq[0], self.r_const],
                             writes=[bregs[0]])

                    def s2():
                        P.op("act", I_act(rt2, banks[0][:], AF.Ln, bias=EPS, scale=1.0 / 128), reads=[bregs[0]],
                             writes=[r_rt2])
                        P.op("act", I_act(rstd2, rt2, AF.Exp, scale=-0.5), reads=[r_rt2], writes=[r_rstd2])

                    def s3(ws=ws):
                        P.op("dve", I_tt(osb, osb, rstd2, ALU.mult), reads=[r_osb, r_rstd2], writes=[r_osb])
                        P.op("dve", I_tt(ogT, osb, zsT[:, ws], ALU.mult), reads=[r_osb, r_zs], writes=[r_og])

                    def mk_proj(dch, ws=ws, w=w):
                        def f():
                            P.op("pe", I_mm(banks[1][:], wo[hs][:, dch * 128:(dch + 1) * 128], ogT),
                                 reads=[r_wo[hs], r_og], writes=[bregs[1]])
                            P.op("dve", I_tt(xT[:, dch, ws], banks[1][:], xT[:, dch, ws], ALU.add),
                                 reads=[bregs[1], self.r_x[dch][w]], writes=[self.r_x[dch][w]])
                        return f
                    s0()
                    deferred.extend([s1, s2, s3] + [mk_proj(dch) for dch in range(KC)])
                else:
                    for _ in range(2):
                        if deferred:
                            deferred.pop(0)()
            while deferred:
                deferred.pop(0)()

        prev_scan = None
        for h in range(8):
            g2 = gen_h2(h)
            if prev_scan is not None:
                a_live, b_live = True, True
                while a_live or b_live:
                    if a_live:
                        try:
                            next(prev_scan)
                        except StopIteration:
                            a_live = False
                    for _ in range(H2_PER_SCAN if a_live else 1000):
                        if b_live:
                            try:
                                next(g2)
                            except StopIteration:
                                b_live = False
            else:
                for _ in g2:
                    pass
            emit_z(h)
            if h + 1 < 8:
                load_head(h + 1)
            for eng_ in ("pe", "act", "dve"):
                P.wait_all(eng_, r_raw_w + r_acc_w)
            Wt = [raw[:, 4 + i * 512:4 + (i + 1) * 512] for i in range(4)] + \
                 [acc[:, i * 512:(i + 1) * 512] for i in range(4)]
            v4 = lambda a: a.rearrange("p (b i) -> p b i", b=4)
            RR = (lambda a: a.bitcast(FP32R)) if DBL_R else (lambda a: a)
            bmid = lambda a: a.unsqueeze(1).broadcast_to([128, 4, 128])
            for gi in range(NB // 4):
                b0 = gi * 4
                gsl = slice(b0 * 128, (b0 + 4) * 128)
                bsl = [slice((b0 + bi) * 128, (b0 + bi + 1) * 128) for bi in range(4)]
                qs_ = [slice(bi * 128, (bi + 1) * 128) for bi in range(4)]

                def tokb(t, pp=slice(0, 128)):
                    return t[pp, b0 * 8 + h:(b0 + 3) * 8 + h + 1:8].unsqueeze(2).broadcast_to([pp.stop - pp.start, 4, 128])
                W_dd, W_eg, W_gm, W_gs, W_B, W_A, W_M, W_X = Wt
                rW = r_W
                P.op("pe", I_multi([(banks[4][:, qs_[bi]], SEL[0:8, h, :], abT[0:8, bsl[bi]]) for bi in range(4)]),
                     reads=[r_cst, r_abT], writes=[bregs[4]])
                P.op("pe", I_multi([(banks[5][:, qs_[bi]], SEL[32:40, h, :], abT[32:40, bsl[bi]]) for bi in range(4)]),
                     reads=[r_cst, r_abT], writes=[bregs[5]])
                P.op("pe", I_multi([(banks[6][:, qs_[bi]], kT[:, bsl[bi]], kT[:, bsl[bi]]) for bi in range(4)]),
                     reads=[r_k], writes=[bregs[6]])
                P.op("pe", I_multi([(banks[7][:, qs_[bi]], kT[:, bsl[bi]], qT[:, bsl[bi]]) for bi in range(4)]),
                     reads=[r_k, r_q], writes=[bregs[7]])
                P.op("dve", I_tt(v4(W_dd), v4(banks[4][:]), tokb(gc_tok), ALU.subtract), reads=[bregs[4], r_tok],
                     writes=[rW[0]])
                P.op("dve", I_ts(W_dd, W_dd, 0.0, None, ALU.min), reads=[rW[0]], writes=[rW[0]])
                P.op("act", I_act(W_dd, W_dd, AF.Exp), reads=[rW[0]], writes=[rW[0]])
                P.op("act", I_act(W_eg, banks[4][:], AF.Exp), reads=[bregs[4]], writes=[rW[1]])
                P.op("dve", I_tt(v4(W_gm), v4(W_dd), bmid(MUI), ALU.mult), reads=[rW[0], r_cst], writes=[rW[2]])
                P.op("dve", I_tt(attnT[:, b0:b0 + 4, :], v4(banks[7][:]), v4(W_gm), ALU.mult),
                     reads=[bregs[7], rW[2]], writes=[r_at[b0 + bi] for bi in range(4)])
                P.op("dve", I_tt(v4(W_gs), v4(W_dd), bmid(MUS), ALU.mult), reads=[rW[0], r_cst], writes=[rW[3]])
                P.op("dve", I_tt(W_gs, banks[5][:], W_gs, ALU.mult), reads=[bregs[5], rW[3]], writes=[rW[3]])
                P.op("dve", I_tt(RR(W_B), banks[6][:], W_gs, ALU.mult), reads=[bregs[6], rW[3]], writes=[rW[4]])
                P.op("dve", I_tt(qdT[:, gsl], qT[:, gsl], W_eg, ALU.mult), reads=[r_q, rW[1]],
                     writes=[r_qdb[b0 + bi] for bi in range(4)])
                P.op("act", I_act(cdv[:, 2 * b0:2 * b0 + 8], W_eg[:, 63:512:64], AF.Copy), reads=[rW[1]],
                     writes=[r_cdv])
                P.op("pe", I_multi_tr([(banks[0][:, qs_[bi]], W_B[:, qs_[bi]], IDN) for bi in range(4)]),
                     reads=[rW[4], r_cst], writes=[bregs[0]])
                P.op("act", I_act(RR(W_A), banks[0][:], AF.Copy), reads=[bregs[0]], writes=[rW[5]])
                P.op("dve", I_tt(v4(RR(W_M)), bmid(IDN), v4(W_B), ALU.subtract), reads=[r_cst, rW[4]], writes=[rW[6]])
                Bc, iB, Ac, iA = W_B, 4, W_A, 5
                free = [(W_dd, 0), (W_eg, 1)]
                for lev in range(1, 6):
                    An, iAn = free.pop(0)
                    P.op("pe", I_multi([(banks[1][:, qs_[bi]], Bc[:, qs_[bi]], Ac[:, qs_[bi]]) for bi in range(4)], r=True),
                         reads=[rW[iB], rW[iA]], writes=[bregs[1]])
                    if lev < 5:
                        Bn, iBn = free.pop(0)
                        P.op("pe", I_multi([(banks[2][:, qs_[bi]], Ac[:, qs_[bi]], Bc[:, qs_[bi]])
                                            for bi in range(4)], r=True),
                             reads=[rW[iB], rW[iA]], writes=[bregs[2]])
                    P.op("dve", I_copy(RR(An), banks[1][:]), reads=[bregs[1]], writes=[rW[iAn]])
                    if lev < 5:
                        P.op("act", I_act(RR(Bn), banks[2][:], AF.Copy), reads=[bregs[2]], writes=[rW[iBn]])
                    P.op("pe", I_multi([(banks[3][:, qs_[bi]], An[:, qs_[bi]], W_M[:, qs_[bi]]) for bi in range(4)], r=True),
                         reads=[rW[iAn], rW[6]], writes=[bregs[3]])
                    P.op("dve", I_tt(RR(W_M), banks[3][:], W_M, ALU.add), reads=[bregs[3], rW[6]], writes=[rW[6]])
                    free.append((Ac, iA))
                    Ac, iA = An, iAn
                    if lev < 5:
                        free.append((Bc, iB))
                        Bc, iB = Bn, iBn
                ktp = banks[0][:].bitcast(BF16)[:, 0:512]
                vtp = banks[1][:].bitcast(BF16)[:, 0:512]
                P.op("pe", I_multi_tr([(ktp[:, qs_[bi]], kT[:, bsl[bi]], idb) for bi in range(4)]),
                     reads=[r_k, r_cst], writes=[bregs[0]])
                P.op("pe", I_multi_tr([(vtp[:, qs_[bi]], vT[:, bsl[bi]], idb) for bi in range(4)]),
                     reads=[r_v, r_cst], writes=[bregs[1]])
                W_rw, W_ru = W_gm, W_gs
                P.op("dve", I_tt(v4(RR(W_rw)), v4(ktp), tokb(bg_tok), ALU.mult), reads=[bregs[0], r_tok], writes=[rW[2]])
                P.op("dve", I_tt(v4(RR(W_ru)), v4(vtp), tokb(beta_tok), ALU.mult), reads=[bregs[1], r_tok], writes=[rW[3]])
                for hf in range(2):
                    pp = slice(64 * hf, 64 * hf + 64)
                    P.op("dve", I_tt(ktail[pp, b0:b0 + 4, hf, :], v4(ktp)[pp], tokb(kt_tok, pp), ALU.mult),
                         reads=[bregs[0], r_tok], writes=[r_kt[b0 + bi] for bi in range(4)])
                P.op("pe", I_multi([(banks[2][:, qs_[bi]], W_M[:, qs_[bi]], W_ru[:, qs_[bi]]) for bi in range(4)], r=True),
                     reads=[rW[6], rW[3]], writes=[bregs[2]])
                P.op("act", I_act(u[:, b0:b0 + 4, :], v4(banks[2][:]), AF.Copy), reads=[bregs[2]],
                     writes=[r_u[b0 + bi] for bi in range(4)])
                P.op("pe", I_multi([(banks[3][:, qs_[bi]], W_rw[:, qs_[bi]], W_M[:, qs_[bi]]) for bi in range(4)], r=True),
                     reads=[rW[6], rW[2]], writes=[bregs[3]])
                P.op("dve", I_copy(wT[:, gsl], banks[3][:]), reads=[bregs[3]],
                     writes=[r_wT[b0 + bi] for bi in range(4)])
            for eng_ in ("pe", "act", "dve"):
                P.wait_all(eng_, r_W)
            prev_scan = gen_scan(h)
        for _ in prev_scan:
            pass
        allregs = r_raw_w + r_acc_w + [r_cst, r_small, r_nega, r_abT, r_tok, r_wab, r_q, r_k, r_v, r_qd, r_zs, r_cdv,
                   r_Sf, r_Sb, r_vnew, r_osb, r_og, r_rt2, r_rstd2]
        self.phase_barrier([allregs, r_wh, r_wo, r_hv, r_u, r_wT, r_at, r_kt, r_qdb, r_tmp, r_W, bregs])
        ar.reset(m)

    def phase_barrier(self, reg_lists):
        regs = [r for rl in reg_lists for r in rl]
        for n in ("pe", "act", "dve", "pool", "sp"):
            self.P.wait_all(n, regs)

    def build(self):
        nc, P, st = self.nc, self.P, self.stack
        self.s_wgu = [P.dma_src() for _ in range(4)]
        self.s_wd = [P.dma_src() for _ in range(3)]
        self.s_awb = [P.dma_src() for _ in range(3)]
        self.s_awout = P.dma_src()
        self.s_eb = [P.dma_src() for _ in range(2)]
        self.s_bc = [P.dma_src() for _ in range(3)]
        self.s_wh = [P.dma_src() for _ in range(3)]
        self.s_wo = [P.dma_src() for _ in range(2)]
        self.s_hv = [P.dma_src() for _ in range(2)]
        self.emit_consts()
        for s in range(self.n_seq):
            self.emit_load_x(s)
            for pi, ph in enumerate(self.phases):
                if ph[0] == "ffn":
                    nxt = self.phases[pi + 1] if pi + 1 < len(self.phases) else None
                    prv = self.phases[pi - 1] if pi > 0 else None
                    self.emit_ffn(ph[1], ph[2],
                                  pre_normed=(prv is not None and prv[0] == "ffn"),
                                  next_norm_idx=(nxt[2] if (nxt is not None and nxt[0] == "ffn") else None))
                elif ph[0] == "mixa":
                    self.emit_mixa(ph[1], ph[2])
                elif ph[0] == "mixb":
                    self.emit_mixb(ph[1], ph[2])
                elif ph[0] == "final":
                    self.emit_rmsnorm(12, range(NW), out_f32_inplace=True)
                else:
                    raise ValueError(ph)
            self.emit_store_x(s)
        P.wait_all("sp", [r for k in range(KC) for r in self.r_x[k]])
        with nc.Block() as block:
            P.replay(block)
        st.close()
        return nc


def full_phases():
    ph = []
    for i in range(DEPTH):
        ph.append(("ffn", 2 * i, 3 * i))
        ph.append(("mixa" if i % 2 == 0 else "mixb", i // 2, 3 * i + 1))
        ph.append(("ffn", 2 * i + 1, 3 * i + 2))
    ph.append(("final",))
    return ph


def prep_weights(inp):
    f32 = np.float32
    out = {}
    ng = np.concatenate([np.asarray(inp["norm_g"], f32).reshape(12, D), np.asarray(inp["final_g"], f32).reshape(1, D)], 0)
    out["normg"] = np.ascontiguousarray(ng.reshape(13, KC, 128).transpose(2, 0, 1).reshape(128, 13 * KC))
    wg = np.asarray(inp["ffn_w_gate"], f32).reshape(2 * DEPTH, KC, 128, FC, 128)
    wu = np.asarray(inp["ffn_w_up"], f32).reshape(2 * DEPTH, KC, 128, FC, 128)
    wgu = np.stack([wg, wu], 0)
    wgu = wgu.transpose(1, 4, 3, 0, 2, 5)
    out["wgu"] = np.ascontiguousarray(wgu).reshape(2 * DEPTH, FC, 128, 2 * KC * 128)
    wd = np.asarray(inp["ffn_w_down"], f32).reshape(2 * DEPTH, FC, 128, KC, 128)
    wd = wd.transpose(0, 3, 2, 1, 4)
    out["wd"] = np.ascontiguousarray(wd).reshape(2 * DEPTH, KC, 128, FC * 128)
    awin = np.asarray(inp["a_w_in"], f32).reshape(2, KC, 128, 3, 3, 4, 128)
    awin = awin.transpose(0, 5, 2, 1, 3, 4, 6)
    out["awin"] = np.ascontiguousarray(awin).reshape(2, 4, 128, KC, 3 * 384)
    awout = np.asarray(inp["a_w_out"], f32).reshape(2, 4, 128, D).transpose(0, 2, 1, 3)
    out["awout"] = np.ascontiguousarray(awout)
    out["bt"] = bias_table(np.asarray(inp["rel_bias"], f32))
    bw = np.asarray(inp["b_w_in"], f32)
    w4 = bw[:, :, :4096].reshape(2, KC, 128, 4, 8, 128).transpose(0, 4, 2, 1, 3, 5)
    out["bwin"] = np.ascontiguousarray(w4).reshape(2, 8, 128, KC, 512)
    wa = bw[:, :, 4096:4104].reshape(2, KC, 128, 8).transpose(0, 2, 1, 3)
    wb_ = bw[:, :, 4104:4112].reshape(2, KC, 128, 8).transpose(0, 2, 1, 3)
    wab = np.zeros((2, 128, KC, 64), f32)
    wab[..., 0:8] = wa
    wab[..., 32:40] = wb_
    out["wab"] = wab
    out["wabt"] = np.ascontiguousarray(np.concatenate([wa, wb_], -1))
    cw = np.asarray(inp["b_conv_w"], f32).reshape(2, 4, 3, 8, 128).transpose(0, 3, 4, 2, 1)
    out["bconv"] = np.ascontiguousarray(cw).reshape(2, 8, 128, 12)
    sm = np.zeros((2, 128, 259), f32)
    dtb = np.asarray(inp["b_dt_bias"], f32)
    alog = np.asarray(inp["b_a_log"], f32)
    sm[:, 0:8, 0] = dtb
    sm[:, 0:8, 1] = alog
    sm[:, :, 2] = np.asarray(inp["b_norm_w"], f32)
    sm[:, :, 3:131] = np.tile(dtb, (1, 16))[:, None, :]
    sm[:, :, 131:259] = np.tile(alog, (1, 16))[:, None, :]
    out["bsmall"] = sm
    out["bwout"] = np.ascontiguousarray(np.asarray(inp["b_w_out"], f32).reshape(2, 8, 128, D))
    out["bconst"] = b_consts()
    return out


def b_consts():
    i = np.arange(128)
    same = (i[:, None] // 64) == (i[None, :] // 64)
    c = np.zeros((128, 13 * 128), np.float32)
    c[:, 0:128] = (same & (i[:, None] <= i[None, :]))
    c[:, 128:256] = (i[:, None] == (i[None, :] // 64) * 64 + 63)
    c[:, 256:384] = np.eye(128)
    c[:, 384:512] = (same & (i[None, :] >= i[:, None]))
    c[:, 512:640] = (same & (i[None, :] > i[:, None]))
    sel = np.zeros((128, 8, 128), np.float32)
    for h in range(8):
        sel[h, h, :] = 1.0
        sel[32 + h, h, :] = 1.0
    c[:, 640:] = sel.reshape(128, 1024)
    return c


def t5_bucket_np(dist):
    n = dist.astype(np.float32)
    large = np.float32(16.0) + (np.log(np.maximum(n, np.float32(1.0)) / np.float32(16.0)).astype(np.float32)
                                / np.float32(np.log(2048.0 / 16.0)) * np.float32(16.0)).astype(np.float32)
    large = np.minimum(large.astype(np.int32), 31)
    return np.where(dist < 16, dist, large)


def bias_table(rel_bias):
    kk = np.arange(128)[:, None]
    jj = np.arange(256)[None, :]
    rel = jj - kk
    valid = (rel >= 0) & (rel <= 128)
    bt = np.full((4, 128, 6, 256), -30000.0, np.float32)
    for g, dil in enumerate((1, 4, 16)):
        bk = t5_bucket_np(np.maximum(rel, 0) * dil)
        for pair in range(4):
            for hp in range(2):
                tab = rel_bias[:, g * 8 + 2 * pair + hp][bk]
                bt[pair, :, g * 2 + hp, :] = np.where(valid, tab, np.float32(-30000.0))
    return bt


def x_to_dev(x_seqs):
    n = x_seqs.shape[0]
    return np.ascontiguousarray(x_seqs.reshape(n, S, KC, 128).transpose(0, 2, 3, 1))


def x_from_dev(o):
    n = o.shape[0]
    return np.ascontiguousarray(o.transpose(0, 3, 1, 2).reshape(n, S, D))


_PROG_CACHE = {}


def get_prog(n_seq, phases):
    key = (n_seq, tuple(phases))
    if key not in _PROG_CACHE:
        _PROG_CACHE[key] = Builder(n_seq, list(phases)).build()
    return _PROG_CACHE[key]


def run(inputs, n_seq_per_core, phases, n_cores=N_CORES, trace=False, x_override=None):
    w = prep_weights(inputs)
    x = np.asarray(inputs["x"] if x_override is None else x_override, np.float32)
    nc = get_prog(n_seq_per_core, phases)
    in_maps = []
    for c in range(n_cores):
        m = dict(w)
        m["xT"] = x_to_dev(x[c * n_seq_per_core:(c + 1) * n_seq_per_core])
        in_maps.append(m)
    res = run_bass_kernel_spmd(nc, in_maps, core_ids=list(range(n_cores)), trace=trace)
    outs = [x_from_dev(r["outT"]) for r in res.results]
    return np.concatenate(outs, 0), res


def kernel(**inputs):
    out, _ = run(inputs, 4, full_phases())
    return out
```

```python
import numpy as np
import concourse.bass as bass
import concourse.mybir as mybir
from concourse.bass_utils import run_bass_kernel_spmd

F32 = mybir.dt.float32
BF16 = mybir.dt.bfloat16
FP32R = mybir.dt.float32r
H2_PER_SCAN = 2
DBL_R = False
AF = mybir.ActivationFunctionType
ALU = mybir.AluOpType

D = 1024
S = 2048
DFF = 2816
KC = D // 128
FC = DFF // 128
WIN = 512
NW = S // WIN
DEPTH = 4
EPS = 1e-6
N_CORES = 8
MIXB_STOP = None
OP_LIMIT = None


class Src:
    def __init__(self, sem, step):
        self.sem = sem
        self.step = step
        self.val = 0


class Reg:
    __slots__ = ("name", "w", "r")

    def __init__(self, name=""):
        self.name = name
        self.w = None
        self.r = {}


class Eng:
    def __init__(self, name, src):
        self.name = name
        self.src = src
        self.seen = {}
        self.ops = []


class Prog:
    def __init__(self, nc, stack):
        self.nc = nc
        self.stack = stack
        self.engs = {}
        for n in ("pe", "act", "dve", "pool", "sp"):
            sem = stack.enter_context(nc.semaphore("sem_" + n))
            self.engs[n] = Eng(n, Src(sem, 1))
        self.n_dma_src = 0

    def dma_src(self):
        self.n_dma_src += 1
        sem = self.stack.enter_context(self.nc.semaphore("dsem%d" % self.n_dma_src))
        return Src(sem, 16)

    def _waits(self, eng, reads, writes):
        deps = {}
        for r in reads:
            if r.w is not None:
                s, v = r.w
                if deps.get(s, 0) < v:
                    deps[s] = v
        for w in writes:
            if w.w is not None:
                s, v = w.w
                if deps.get(s, 0) < v:
                    deps[s] = v
            for s, v in w.r.items():
                if deps.get(s, 0) < v:
                    deps[s] = v
        for s, v in deps.items():
            if s is eng.src and eng.name == "pe":
                continue
            if eng.seen.get(s, 0) >= v:
                continue
            eng.seen[s] = v
            eng.ops.append(("wait", s.sem, v))

    def op(self, eng, fn, reads=(), writes=()):
        if getattr(self, "count_active", False):
            self.cnt = getattr(self, "cnt", 0) + 1
            if OP_LIMIT is not None and self.cnt > OP_LIMIT:
                return
        e = self.engs[eng]
        self._waits(e, reads, writes)
        e.src.val += 1
        v = e.src.val
        e.ops.append(("ins", fn, e.src.sem, 1))
        for r in reads:
            if r.r.get(e.src, 0) < v:
                r.r[e.src] = v
        for w in writes:
            w.w = (e.src, v)
            w.r = {}

    def dma(self, eng, src, out, in_, reads=(), writes=(), **kw):
        e = self.engs[eng]
        self._waits(e, reads, writes)
        src.val += 16
        v = src.val
        e.ops.append(("ins", lambda q: q.dma_start(out=out, in_=in_, **kw), src.sem, 16))
        for r in reads:
            if r.r.get(src, 0) < v:
                r.r[src] = v
        for w in writes:
            w.w = (src, v)
            w.r = {}

    def wait_all(self, eng, regs):
        e = self.engs[eng]
        self._waits(e, (), regs)

    def replay(self, block):
        handles = {"pe": block.tensor, "act": block.scalar, "dve": block.vector,
                   "pool": block.gpsimd, "sp": block.sync}
        for n, e in self.engs.items():
            ops = e.ops

            def body(q, ops=ops):
                for o in ops:
                    if o[0] == "wait":
                        q.wait_ge(o[1], o[2])
                    else:
                        o[1](q).then_inc(o[2], o[3])
            handles[n](body)


class Arena:
    def __init__(self, ap_bf16, nbytes):
        self.ap = ap_bf16
        self.nbytes = nbytes
        self.off = 0

    def mark(self):
        return self.off

    def reset(self, m=0):
        self.off = m

    def alloc(self, shape, dtype, parts=128):
        n = 1
        for s in shape:
            n *= s
        esz = 4 if dtype == F32 else 2
        nb = (n * esz + 31) // 32 * 32
        assert self.off + nb <= self.nbytes, ("arena overflow", self.off, nb, self.nbytes)
        a = self.ap[0:parts, self.off // 2:(self.off + n * esz) // 2]
        self.off += nb
        if dtype == F32:
            a = a.bitcast(F32)
        if len(shape) == 2:
            a = a.rearrange("p (a b) -> p a b", a=shape[0])
        elif len(shape) == 3:
            a = a.rearrange("p (a b c) -> p a b c", a=shape[0], b=shape[1])
        elif len(shape) == 4:
            a = a.rearrange("p (a b c d) -> p a b c d", a=shape[0], b=shape[1], c=shape[2])
        return a


def I_act(out, in_, func, **kw):
    return lambda q: q.activation(out=out, in_=in_, func=func, **kw)


def I_tt(out, in0, in1, op):
    return lambda q: q.tensor_tensor(out=out, in0=in0, in1=in1, op=op)


def I_stt(out, in0, scalar, in1, op0, op1):
    return lambda q: q.scalar_tensor_tensor(out=out, in0=in0, scalar=scalar, in1=in1, op0=op0, op1=op1)


def I_ts(out, in0, s1, s2, op0, op1=None):
    if op1 is None:
        return lambda q: q.tensor_scalar(out=out, in0=in0, scalar1=s1, scalar2=None, op0=op0)
    return lambda q: q.tensor_scalar(out=out, in0=in0, scalar1=s1, scalar2=s2, op0=op0, op1=op1)


def I_recip(out, in_):
    return lambda q: q.reciprocal(out=out, in_=in_)


def I_copy(out, in_):
    return lambda q: q.tensor_copy(out=out, in_=in_)


def I_memset(out, val):
    return lambda q: q.memset(out, val)


def I_mm(out, lhsT, rhs, start=True, stop=True):
    return lambda q: q.matmul(out, lhsT=lhsT, rhs=rhs, start=start, stop=stop)


def I_mmgroup(out, pairs):
    pairs = list(pairs)

    def f(q):
        ins = None
        n = len(pairs)
        for i, (l, r) in enumerate(pairs):
            ins = q.matmul(out, lhsT=l, rhs=r, start=(i == 0), stop=(i == n - 1))
        return ins
    return f


def I_multi(items, r=False):
    items = list(items)
    if r and DBL_R:
        items = [(o, l.bitcast(FP32R), rr.bitcast(FP32R)) for (o, l, rr) in items]

    def f(q):
        ins = None
        for (o, l, r) in items:
            ins = q.matmul(o, lhsT=l, rhs=r, start=True, stop=True)
        return ins
    return f


def I_multi_tr(items):
    items = list(items)

    def f(q):
        ins = None
        for (o, i, ident) in items:
            ins = q.transpose(o, i, ident)
        return ins
    return f


def I_transpose(out, in_, ident):
    return lambda q: q.transpose(out, in_, ident)


class Builder:
    def __init__(self, n_seq, phases):
        from contextlib import ExitStack
        self.n_seq = n_seq
        self.phases = phases
        self.stack = ExitStack()
        nc = self.nc = bass.Bass("TRN2", target_bir_lowering=False)
        st = self.stack
        self.P = Prog(nc, st)
        dt = nc.dram_tensor
        self.d_xT = dt("xT", [n_seq, KC, 128, S], F32, kind="ExternalInput").ap()
        self.d_out = dt("outT", [n_seq, KC, 128, S], F32, kind="ExternalOutput").ap()
        self.d_normg = dt("normg", [128, 13 * KC], F32, kind="ExternalInput").ap()
        self.d_wgu = dt("wgu", [2 * DEPTH, FC, 128, 2 * KC * 128], F32, kind="ExternalInput").ap()
        self.d_wd = dt("wd", [2 * DEPTH, KC, 128, FC * 128], F32, kind="ExternalInput").ap()
        self.d_awin = dt("awin", [2, 4, 128, KC, 3 * 384], F32, kind="ExternalInput").ap()
        self.d_awout = dt("awout", [2, 128, 4, D], F32, kind="ExternalInput").ap()
        self.d_bt = dt("bt", [4, 128, 6, 256], F32, kind="ExternalInput").ap()
        self.d_bwin = dt("bwin", [2, 8, 128, KC, 512], F32, kind="ExternalInput").ap()
        self.d_wab = dt("wab", [2, 128, KC, 64], F32, kind="ExternalInput").ap()
        self.d_wabt = dt("wabt", [2, 128, KC, 16], F32, kind="ExternalInput").ap()
        self.d_bconv = dt("bconv", [2, 8, 128, 12], F32, kind="ExternalInput").ap()
        self.d_bsmall = dt("bsmall", [2, 128, 259], F32, kind="ExternalInput").ap()
        self.d_bwout = dt("bwout", [2, 8, 128, D], F32, kind="ExternalInput").ap()
        self.d_bconst = dt("bconst", [128, 13 * 128], F32, kind="ExternalInput").ap()
        sb = lambda name, shape, dtype: st.enter_context(nc.sbuf_tensor(name, shape, dtype))
        self.xT = sb("xTs", [128, KC, S], F32)
        self.hT = sb("hTs", [128, KC, S], BF16)
        self.normg = sb("normg_s", [128, 13 * KC], F32)
        self.ones_bf = sb("ones_bf", [128, 128], BF16)
        self.n_sq = [sb("n_sq%d" % i, [128, WIN], BF16) for i in range(2)]
        self.n_rt = sb("n_rt", [128, WIN], F32)
        self.n_rstd = sb("n_rstd", [128, WIN], F32)
        self.r_nsq = [Reg("sq0"), Reg("sq1")]
        self.r_nrt, self.r_nrstd = Reg("rt"), Reg("rstd")
        ARENA = (nc.sbuf_bytes_remaining - 2048) // 64 * 64
        self.arena_t = sb("arena", [128, ARENA // 2], BF16)
        self.arena = Arena(self.arena_t, ARENA)
        self.banks = [st.enter_context(nc.psum_tensor("bank%d" % i, [128, 512], F32)) for i in range(8)]
        self.bank_regs = [Reg("bank%d" % i) for i in range(8)]
        self.r_x = [[Reg("x%d_%d" % (k, w)) for w in range(NW)] for k in range(KC)]
        self.r_h = [[Reg("h%d_%d" % (k, w)) for w in range(NW)] for k in range(KC)]
        self.r_const = Reg("const")
        self.s_io = [self.P.dma_src() for _ in range(KC)]
        self.s_const = self.P.dma_src()

    def emit_consts(self):
        P = self.P
        P.dma("sp", self.s_const, self.normg[:], self.d_normg[:, :], writes=[self.r_const])
        ones = self.ones_bf
        P.op("dve", I_memset(ones[:], 1.0), writes=[self.r_const])

    def emit_load_x(self, s):
        P = self.P
        for k in range(KC):
            P.dma("sp", self.s_io[k], self.xT[:, k, :], self.d_xT[s, k, :, :], writes=self.r_x[k])

    def emit_store_x(self, s, src_is_h=False):
        P = self.P
        for k in range(KC):
            P.dma("sp", self.s_io[k], self.d_out[s, k, :, :], self.xT[:, k, :], reads=self.r_x[k])

    def gen_rmsnorm(self, norm_idx, windows, out_f32_inplace=False):
        P = self.P
        sq = [t[:] for t in self.n_sq]
        r_sq = self.r_nsq
        rt, rstd = self.n_rt[:], self.n_rstd[:]
        r_rt, r_rstd = self.r_nrt, self.r_nrstd
        xT, hT, ones, normg = self.xT, self.hT, self.ones_bf, self.normg
        bank = self.banks[7]
        r_bank = self.bank_regs[7]
        for w in windows:
            ws = slice(w * WIN, (w + 1) * WIN)
            for k in range(KC):
                b = k % 2
                P.op("act", I_act(sq[b], xT[:, k, ws], AF.Square),
                     reads=[self.r_x[k][w]], writes=[r_sq[b]])
                P.op("pe", I_mm(bank[:], ones[:], sq[b], start=(k == 0), stop=(k == KC - 1)),
                     reads=[r_sq[b], self.r_const], writes=[r_bank])
                yield
            P.op("act", I_act(rt, bank[:], AF.Ln, bias=EPS, scale=1.0 / D),
                 reads=[r_bank], writes=[r_rt])
            P.op("act", I_act(rstd, rt, AF.Exp, scale=-0.5), reads=[r_rt], writes=[r_rstd])
            yield
            for k in range(KC):
                g_ap = normg[:, norm_idx * KC + k:norm_idx * KC + k + 1]
                if out_f32_inplace:
                    P.op("dve", I_stt(xT[:, k, ws], xT[:, k, ws], g_ap, rstd, ALU.mult, ALU.mult),
                         reads=[r_rstd, self.r_const], writes=[self.r_x[k][w]])
                else:
                    P.op("dve", I_stt(hT[:, k, ws], xT[:, k, ws], g_ap, rstd, ALU.mult, ALU.mult),
                         reads=[self.r_x[k][w], r_rstd, self.r_const], writes=[self.r_h[k][w]])
                yield

    def emit_rmsnorm(self, norm_idx, windows, out_f32_inplace=False):
        for _ in self.gen_rmsnorm(norm_idx, windows, out_f32_inplace):
            pass

    def emit_ffn(self, ffn_idx, norm_idx, pre_normed=False, next_norm_idx=None, next_inplace=False):
        P = self.P
        ar = self.arena
        xT, hT = self.xT, self.hT
        HW = 2
        NH = NW // HW
        m = ar.mark()
        aT = ar.alloc([FC, HW * WIN], BF16)
        r_a = [[Reg("a") for _ in range(HW)] for _ in range(FC)]
        NGU = 4
        wgu = [ar.alloc([2, KC, 128], BF16) for _ in range(NGU)]
        r_wgu = [Reg("wgu%d" % i) for i in range(NGU)]
        s_wgu = self.s_wgu
        NWD = 3
        wd = [ar.alloc([FC, 128], BF16) for _ in range(NWD)]
        r_wd = [Reg("wd%d" % i) for i in range(NWD)]
        s_wd = self.s_wd
        sil = [ar.alloc([WIN], F32) for _ in range(2)]
        r_sil = [Reg("sil0"), Reg("sil1")]
        it = 0

        def load_wgu(f):
            sl = f % NGU
            P.dma("pool", s_wgu[sl], wgu[sl].rearrange("p a b c -> p (a b c)"),
                  self.d_wgu[ffn_idx, f, :, :], writes=[r_wgu[sl]])

        def load_wd(gd):
            sl = gd % NWD
            d = gd % KC
            hf = FC // 2
            P.dma("pool", s_wd[sl], wd[sl][:, 0:hf, :].rearrange("p a b -> p (a b)"),
                  self.d_wd[ffn_idx, d, :, 0:hf * 128], writes=[r_wd[sl]])
            P.dma("pool", s_wd[sl], wd[sl][:, hf:FC, :].rearrange("p a b -> p (a b)"),
                  self.d_wd[ffn_idx, d, :, hf * 128:FC * 128], writes=[r_wd[sl]])

        for f in range(NGU):
            load_wgu(f)
        for d in range(NWD):
            load_wd(d)
        if not pre_normed:
            self.emit_rmsnorm(norm_idx, [0, 1])
        for half in range(NH):
            wins = [half * HW + i for i in range(HW)]
            for f in range(FC):
                sl = f % NGU
                for wi, w in enumerate(wins):
                    ws = slice(w * WIN, (w + 1) * WIN)
                    pb = it % 2
                    it += 1
                    bg, bu = self.banks[pb * 2], self.banks[pb * 2 + 1]
                    rg, ru = self.bank_regs[pb * 2], self.bank_regs[pb * 2 + 1]
                    rd = [r_wgu[sl]] + [self.r_h[k][w] for k in range(KC)]
                    P.op("pe", I_mmgroup(bg[:], [(wgu[sl][:, 0, k, :], hT[:, k, ws]) for k in range(KC)]),
                         reads=rd, writes=[rg])
                    P.op("pe", I_mmgroup(bu[:], [(wgu[sl][:, 1, k, :], hT[:, k, ws]) for k in range(KC)]),
                         reads=rd, writes=[ru])
                    sb_ = it % 2
                    P.op("act", I_act(sil[sb_], bg[:], AF.Silu), reads=[rg], writes=[r_sil[sb_]])
                    P.op("dve", I_tt(aT[:, f, wi * WIN:(wi + 1) * WIN], sil[sb_], bu[:], ALU.mult),
                         reads=[r_sil[sb_], ru], writes=[r_a[f][wi]])
                if f + NGU < FC:
                    load_wgu(f + NGU)
            if half + 1 < NH:
                for f in range(NGU):
                    load_wgu(f)
                side = self.gen_rmsnorm(norm_idx, [(half + 1) * HW + i for i in range(HW)])
            elif next_norm_idx is not None:
                side = self.gen_rmsnorm(next_norm_idx, [0, 1], out_f32_inplace=next_inplace)
            else:
                side = iter(())
            for d in range(KC):
                gd = half * KC + d
                sl = gd % NWD
                for wi, w in enumerate(wins):
                    ws = slice(w * WIN, (w + 1) * WIN)
                    pb = it % 2
                    it += 1
                    by, ry = self.banks[4 + pb], self.bank_regs[4 + pb]
                    P.op("pe", I_mmgroup(by[:], [(wd[sl][:, f, :], aT[:, f, wi * WIN:(wi + 1) * WIN]) for f in range(FC)]),
                         reads=[r_wd[sl]] + [r_a[f][wi] for f in range(FC)], writes=[ry])
                    P.op("dve", I_stt(xT[:, d, ws], by[:], 0.5, xT[:, d, ws], ALU.mult, ALU.add),
                         reads=[ry, self.r_x[d][w]], writes=[self.r_x[d][w]])
                    for _ in range(3):
                        next(side, None)
                if gd + NWD < NH * KC:
                    load_wd(gd + NWD)
            for _ in side:
                pass
        self.phase_barrier([r_wgu, r_wd, r_sil, [r for rr in r_a for r in rr]])
        ar.reset(m)

    def emit_mixa(self, j, norm_idx, pre_normed=False):
        P = self.P
        ar = self.arena
        xT, hT = self.xT, self.hT
        banks, bregs = self.banks, self.bank_regs
        self.emit_rmsnorm(norm_idx, [2, 3] if pre_normed else range(NW))
        r_hall = [self.r_h[k][w] for k in range(KC) for w in range(NW)]
        m = ar.mark()
        NWB = 3
        wbuf = [ar.alloc([KC, 384], BF16) for _ in range(NWB)]
        r_wbuf = [Reg("awb%d" % i) for i in range(NWB)]
        wout = ar.alloc([4, D], BF16)
        r_wout = Reg("awout")
        ebuf = [ar.alloc([6, 256], F32) for _ in range(2)]
        r_ebuf = [Reg("eb0"), Reg("eb1")]
        qT = [ar.alloc([S], BF16) for _ in range(2)]
        kT = [ar.alloc([S], BF16) for _ in range(2)]
        r_qk = [Reg("qk0"), Reg("qk1")]
        vbuf = [ar.alloc([16, 128], BF16) for _ in range(2)]
        r_v = [Reg("v0"), Reg("v1")]
        accn = ar.alloc([S], F32)
        accd = ar.alloc([S], F32)
        r_accn, r_accd = Reg("accn"), Reg("accd")
        oT = ar.alloc([S], BF16)
        r_oT = Reg("oT")
        NE = 4
        ebf = [ar.alloc([256], F32) for _ in range(NE)]
        r_ebf = [Reg("E%d" % i) for i in range(NE)]
        NPT = 8
        pT = [ar.alloc([256], BF16) for _ in range(NPT)]
        r_pT = [Reg("pT%d" % i) for i in range(NPT)]
        ones = self.ones_bf

        P.dma("pool", self.s_awout, wout, self.d_awout[j, :, :, :], writes=[r_wout])
        wl = [(pair, g) for pair in range(4) for g in range(3)]

        def load_w(i):
            pair, g = wl[i]
            sl = i % NWB
            P.dma("pool", self.s_awb[sl], wbuf[sl], self.d_awin[j, pair, :, :, g * 384:(g + 1) * 384],
                  writes=[r_wbuf[sl]])

        def load_eb(pair):
            sl = pair % 2
            P.dma("sp", self.s_eb[sl], ebuf[sl], self.d_bt[pair, :, :, :], writes=[r_ebuf[sl]])
            P.op("act", I_act(ebuf[sl], ebuf[sl], AF.Exp), reads=[r_ebuf[sl]], writes=[r_ebuf[sl]])

        for i in range(NWB):
            load_w(i)
        load_eb(0)
        pj = 0
        sj = 0
        ej = 0
        pj_t = 0
        nj = 0
        for i, (pair, g) in enumerate(wl):
            dil = (1, 4, 16)[g]
            L = S // dil
            wsl = i % NWB
            qs = i % 2
            if g == 0 and pair + 1 < 4:
                load_eb(pair + 1)
            eb = ebuf[pair % 2]
            r_eb = r_ebuf[pair % 2]
            for which, dst in ((0, qT[qs]), (1, kT[qs])):
                for w in range(NW):
                    ws = slice(w * WIN, (w + 1) * WIN)
                    b = pj % 2
                    pj += 1
                    P.op("pe", I_mmgroup(banks[b][:], [(wbuf[wsl][:, k, which * 128:(which + 1) * 128], hT[:, k, ws])
                                                       for k in range(KC)]),
                         reads=[r_wbuf[wsl]] + [self.r_h[k][w] for k in range(KC)], writes=[bregs[b]])
                    nl = WIN // dil
                    if dil == 1:
                        o_ap = dst[:, ws]
                        i_ap = banks[b][:]
                    else:
                        o_ap = dst.rearrange("p (c l) -> p c l", c=dil)[:, :, w * nl:(w + 1) * nl]
                        i_ap = banks[b][:].rearrange("p (l c) -> p c l", c=dil)
                    if which == 0:
                        P.op("act", I_act(o_ap, i_ap, AF.Copy, scale=0.125), reads=[bregs[b]], writes=[r_qk[qs]])
                    else:
                        P.op("dve", I_copy(o_ap, i_ap), reads=[bregs[b]], writes=[r_qk[qs]])
            nbl = L // 128
            for bq in range(4):
                b = pj % 2
                pj += 1
                for bi in range(4):
                    blk = bq * 4 + bi
                    c, n = blk // nbl, blk % nbl
                    t0 = n * 128 * dil + c
                    tsl = slice(t0, t0 + 127 * dil + 1, dil)
                    tw = set(t // WIN for t in (t0, t0 + 127 * dil))
                    P.op("pe", I_mmgroup(banks[b][:, bi * 128:(bi + 1) * 128],
                                         [(hT[:, k, tsl], wbuf[wsl][:, k, 256:384]) for k in range(KC)]),
                         reads=[r_wbuf[wsl]] + [self.r_h[k][w] for k in range(KC) for w in tw],
                         writes=[bregs[b]])
                P.op("dve", I_copy(vbuf[qs][:, bq * 4:(bq + 1) * 4, :].rearrange("p a b -> p (a b)"), banks[b][:]),
                     reads=[bregs[b]], writes=[r_v[qs]])
            if i + NWB < len(wl):
                load_w(i + NWB)
            groups = []
            for c in range(dil):
                for qg in range(max(1, nbl // 4)):
                    if nbl == 1:
                        if c % 4 != 0:
                            continue
                        groups.append(([(c + cc, 0) for cc in range(4)], c, qg))
                    else:
                        groups.append(([(c, qg * 4 + qq) for qq in range(4)], c, qg))
            seq = []
            for gi_, (items, c, qg) in enumerate(groups):
                for hp in range(2):
                    for col, (cc, qb) in enumerate(items):
                        seq.append((gi_, hp, col, cc, qb))
            tile_of = {}
            uses_left = {}
            slot_owner = [None] * NPT

            def need_of(cc, qb):
                return [kb for kb in (qb - 1, qb) if kb >= 0]

            def n_uses(cc, kb):
                return (1 if kb < nbl else 0) + (1 if kb + 1 < nbl else 0)

            def stage1(hp, cc, kb):
                nonlocal sj, ej, pj_t
                key = (hp, cc, kb)
                if key in tile_of:
                    return
                ps = slice(64 * hp, 64 * hp + 64)
                base = cc * L
                nq = min(256, L - kb * 128)
                sb_ = 2 + sj % 2
                sj += 1
                P.op("pe", I_mm(banks[sb_][:, 0:nq],
                                kT[qs][ps, base + kb * 128: base + kb * 128 + 128],
                                qT[qs][ps, base + kb * 128: base + kb * 128 + nq]),
                     reads=[r_qk[qs]], writes=[bregs[sb_]])
                e_i = ej % NE
                ej += 1
                P.op("act", I_act(ebf[e_i][:, 0:nq], banks[sb_][:, 0:nq], AF.Exp),
                     reads=[bregs[sb_]], writes=[r_ebf[e_i]])
                p_i = pj_t % NPT
                pj_t += 1
                assert slot_owner[p_i] is None or uses_left[slot_owner[p_i]] == 0, "pT ring too small"
                slot_owner[p_i] = key
                uses_left[key] = n_uses(cc, kb)
                P.op("pool", I_tt(pT[p_i][:, 0:nq], ebf[e_i][:, 0:nq], eb[:, g * 2 + hp, 0:nq], ALU.mult),
                     reads=[r_ebf[e_i], r_eb], writes=[r_pT[p_i]])
                tile_of[key] = p_i

            LA = 2
            gbank = {}
            for i_, (gi_, hp, col, cc, qb) in enumerate(seq):
                for la in range(i_, min(i_ + LA + 1, len(seq))):
                    _, hp2, _, cc2, qb2 = seq[la]
                    for kb in need_of(cc2, qb2):
                        stage1(hp2, cc2, kb)
                if gi_ not in gbank:
                    gbank[gi_] = nj % 2
                    nj += 1
                bsel = gbank[gi_]
                bn, bd = banks[4 + bsel], banks[6 + bsel]
                rbn, rbd = bregs[4 + bsel], bregs[6 + bsel]
                ps = slice(64 * hp, 64 * hp + 64)
                pairs_n, pairs_d, rd = [], [], [r_v[qs], self.r_const]
                for kb in need_of(cc, qb):
                    key = (hp, cc, kb)
                    p_i = tile_of[key]
                    assert slot_owner[p_i] == key
                    uses_left[key] -= 1
                    off = (qb - kb) * 128
                    blk = cc * nbl + kb
                    pairs_n.append((vbuf[qs][:, blk, 64 * hp:64 * hp + 64], pT[p_i][:, off:off + 128]))
                    pairs_d.append((ones[:, 0:64], pT[p_i][:, off:off + 128]))
                    rd.append(r_pT[p_i])
                P.op("pe", I_mmgroup(bn[ps, col * 128:(col + 1) * 128], pairs_n), reads=rd, writes=[rbn])
                P.op("pe", I_mmgroup(bd[ps, col * 128:(col + 1) * 128], pairs_d), reads=rd, writes=[rbd])
                last_of_group = (i_ + 1 == len(seq)) or (seq[i_ + 1][0] != gi_)
                if last_of_group:
                    items, c, qg = groups[gi_]
                    if dil == 1:
                        an = accn[:, qg * 512:(qg + 1) * 512]
                        ad = accd[:, qg * 512:(qg + 1) * 512]
                        sn, sd = bn[:], bd[:]
                    elif nbl == 1:
                        an = accn.rearrange("p (l c) -> p c l", c=dil)[:, c:c + 4, :]
                        ad = accd.rearrange("p (l c) -> p c l", c=dil)[:, c:c + 4, :]
                        sn = bn[:].rearrange("p (c l) -> p c l", c=4)
                        sd = bd[:].rearrange("p (c l) -> p c l", c=4)
                    else:
                        an = accn.rearrange("p (l c) -> p c l", c=dil)[:, c, qg * 512:(qg + 1) * 512]
                        ad = accd.rearrange("p (l c) -> p c l", c=dil)[:, c, qg * 512:(qg + 1) * 512]
                        sn, sd = bn[:], bd[:]
                    if g == 0:
                        P.op("dve", I_copy(an, sn), reads=[rbn], writes=[r_accn])
                        P.op("act", I_act(ad, sd, AF.Copy), reads=[rbd], writes=[r_accd])
                    else:
                        P.op("dve", I_tt(an, sn, an, ALU.add), reads=[rbn, r_accn], writes=[r_accn])
                        P.op("dve", I_tt(ad, sd, ad, ALU.add), reads=[rbd, r_accd], writes=[r_accd])
            if g == 2:
                P.op("act", I_act(accd, accd, AF.Ln), reads=[r_accd], writes=[r_accd])
                P.op("act", I_act(accd, accd, AF.Exp, scale=-1.0), reads=[r_accd], writes=[r_accd])
                P.op("dve", I_tt(oT, accn, accd, ALU.mult), reads=[r_accn, r_accd], writes=[r_oT])
                for dch in range(KC):
                    for w in range(NW):
                        ws = slice(w * WIN, (w + 1) * WIN)
                        b = pj % 2
                        pj += 1
                        P.op("pe", I_mm(banks[b][:], wout[:, pair, dch * 128:(dch + 1) * 128], oT[:, ws]),
                             reads=[r_wout, r_oT], writes=[bregs[b]])
                        P.op("dve", I_tt(xT[:, dch, ws], banks[b][:], xT[:, dch, ws], ALU.add),
                             reads=[bregs[b], self.r_x[dch][w]], writes=[self.r_x[dch][w]])
        self.phase_barrier([r_wbuf, [r_wout], r_ebuf, r_qk, r_v, [r_accn, r_accd, r_oT], r_ebf, r_pT])
        ar.reset(m)

    def emit_mixb(self, j, norm_idx, pre_normed=False):
        P = self.P
        ar = self.arena
        xT, hT = self.xT, self.hT
        banks, bregs = self.banks, self.bank_regs
        self.emit_rmsnorm(norm_idx, [2, 3] if pre_normed else range(NW))
        m = ar.mark()
        NB = S // 128
        cst = ar.alloc([13 * 128], F32)
        r_cst = Reg("bcst")
        TRI, SELL, IDN, MUI, MUS = [cst[:, i * 128:(i + 1) * 128] for i in range(5)]
        SEL = cst[:, 5 * 128:13 * 128].rearrange("p (h m) -> p h m", h=8)
        idb = ar.alloc([128], BF16)
        small = ar.alloc([3 + 256], F32)
        r_small = Reg("bsmall")
        dtb_col, alog_col, normw_col = small[:, 0:1], small[:, 1:2], small[:, 2:3]
        dtb_rep, alog_rep = small[:, 3:131], small[:, 131:259]
        nega = ar.alloc([1 + 128], F32)
        r_nega = Reg("nega")
        abT = ar.alloc([S], F32)
        r_abT = Reg("abT")
        tok = ar.alloc([4, 128], F32)
        gc_tok, beta_tok, bg_tok, kt_tok = [tok[:, i, :] for i in range(4)]
        r_tok = Reg("tok")
        wab = ar.alloc([KC, 64], BF16)
        wabt = ar.alloc([KC, 16], BF16)
        r_wab = Reg("wab")
        wh = [ar.alloc([KC, 256], BF16) for _ in range(3)]
        r_wh = [Reg("wh%d" % i) for i in range(3)]
        wo = [ar.alloc([D], BF16) for _ in range(2)]
        r_wo = [Reg("wo0"), Reg("wo1")]
        hv = [ar.alloc([12], F32) for _ in range(2)]
        r_hv = [Reg("hv0"), Reg("hv1")]
        raw = ar.alloc([S + 4], F32)
        r_raw_w = [Reg("raw%d" % i) for i in range(NW + 1)]
        r_raw = r_raw_w[0]
        acc = ar.alloc([S], F32)
        r_acc_w = [Reg("acc%d" % i) for i in range(NW)]
        r_acc = r_acc_w[0]
        qT = ar.alloc([S], BF16)
        kT = ar.alloc([S], BF16)
        vT = ar.alloc([S], BF16)
        qdT = ar.alloc([S], BF16)
        zsT = ar.alloc([S], BF16)
        r_q, r_k, r_v, r_qd, r_zs = Reg("q"), Reg("k"), Reg("v"), Reg("qd"), Reg("zs")
        u = ar.alloc([NB, 128], BF16)
        wT = ar.alloc([S], BF16)
        attnT = ar.alloc([NB, 128], BF16)
        ktail = ar.alloc([NB, 2, 128], BF16)
        r_u = [Reg("u%d" % b) for b in range(NB)]
        r_wT = [Reg("wT%d" % b) for b in range(NB)]
        r_at = [Reg("at%d" % b) for b in range(NB)]
        r_kt = [Reg("kt%d" % b) for b in range(NB)]
        r_qdb = [Reg("qd%d" % b) for b in range(NB)]
        cdv = ar.alloc([2 * NB], F32)
        r_cdv = Reg("cdv")
        r_W = [Reg("W%d" % i) for i in range(8)]
        NT = 3
        tmp = [ar.alloc([128], F32) for _ in range(NT)]
        r_tmp = [Reg("t%d" % i) for i in range(NT)]
        tcount = [0]

        def T():
            i = tcount[0] % NT
            tcount[0] += 1
            return tmp[i], r_tmp[i]
        Sf = ar.alloc([128], F32)
        Sb = ar.alloc([128], BF16)
        r_Sf, r_Sb = Reg("Sf"), Reg("Sb")
        vnew = ar.alloc([128], BF16)
        r_vnew = Reg("vnew")
        osb = ar.alloc([WIN], F32)
        r_osb = Reg("osb")
        rt2 = ar.alloc([WIN], F32)
        rstd2 = ar.alloc([WIN], F32)
        r_rt2, r_rstd2 = Reg("rt2"), Reg("rstd2")
        ogT = ar.alloc([WIN], BF16)
        r_og = Reg("og")
        zt, r_zt = osb, r_osb
        ones = self.ones_bf
        sq, r_sq = [t[:] for t in self.n_sq], self.r_nsq
        rt, rstd, r_rt, r_rstd = self.n_rt[:], self.n_rstd[:], self.r_nrt, self.r_nrstd
        place = {(4, 0): (4, 0), (4, 1): (5, 2), (4, 2): (4, 2), (4, 3): (4, 3),
                 (5, 0): (5, 0), (5, 1): (5, 1), (5, 2): (0, 0), (5, 3): (1, 0),
                 (6, 0): (6, 0), (6, 1): (6, 1), (6, 2): (2, 0), (6, 3): (3, 0),
                 (7, 0): (7, 0), (7, 1): (6, 2)}
        qb = {}
        qr = {}
        for key, (b, q) in place.items():
            qb[key] = banks[b][:, q * 128:(q + 1) * 128]
            qr[key] = bregs[b]

        def qbf(b, q):
            b, q = place[(b, q)]
            return banks[b][:].bitcast(BF16)[:, q * 256:q * 256 + 128]

        P.dma("sp", self.s_bc[0], cst, self.d_bconst[:, :], writes=[r_cst])
        P.dma("sp", self.s_bc[1], small, self.d_bsmall[j, :, :], writes=[r_small])
        P.dma("pool", self.s_bc[2], wab, self.d_wab[j, :, :, :], writes=[r_wab])
        P.dma("pool", self.s_bc[2], wabt, self.d_wabt[j, :, :, :], writes=[r_wab])
        P.op("dve", I_copy(idb, IDN), reads=[r_cst], writes=[r_cst])
        P.op("dve", I_memset(raw[:, 0:4], 0.0), writes=[r_raw])
        P.op("dve", I_memset(ktail.rearrange("p a b c -> p (a b c)"), 0.0), writes=r_kt)
        P.op("act", I_act(nega[:, 0:1], alog_col, AF.Exp), reads=[r_small], writes=[r_nega])
        P.op("act", I_act(nega[:, 1:129], alog_rep, AF.Exp), reads=[r_small], writes=[r_nega])
        P.op("dve", I_ts(nega, nega, -1.0, None, ALU.mult), reads=[r_nega], writes=[r_nega])

        def load_head(h):
            sl = h % 2
            for half in range(2):
                i = (2 * h + half) % 3
                P.dma("pool", self.s_wh[i], wh[i], self.d_bwin[j, h, :, :, half * 256:(half + 1) * 256],
                      writes=[r_wh[i]])
            P.dma("pool", self.s_wo[sl], wo[sl], self.d_bwout[j, h, :, :], writes=[r_wo[sl]])
            P.dma("sp", self.s_hv[sl], hv[sl], self.d_bconv[j, h, :, :], writes=[r_hv[sl]])

        load_head(0)
        tA, r_tA = acc, r_acc
        r_tA_all = r_acc_w
        for w in range(NW):
            ws = slice(w * WIN, (w + 1) * WIN)
            b = w % 2
            P.op("pe", I_mmgroup(banks[b][0:64, :], [(wab[:, k, :], hT[:, k, ws]) for k in range(KC)]),
                 reads=[r_wab] + [self.r_h[k][w] for k in range(KC)], writes=[bregs[b]])
            P.op("act", I_act(tA[0:8, ws], banks[b][0:8, :], AF.Exp, bias=dtb_col[0:8, :]),
                 reads=[bregs[b], r_small], writes=r_tA_all)
            P.op("act", I_act(tA[0:8, ws], tA[0:8, ws], AF.Ln, bias=1.0), reads=r_tA_all, writes=r_tA_all)
            P.op("dve", I_ts(abT[0:8, ws], tA[0:8, ws], nega[0:8, 0:1], None, ALU.mult),
                 reads=r_tA_all + [r_nega], writes=[r_abT])
            P.op("act", I_act(abT[32:40, ws], banks[b][32:40, :], AF.Exp, scale=-1.0),
                 reads=[bregs[b]], writes=[r_abT])
            P.op("dve", I_ts(abT[32:40, ws], abT[32:40, ws], 1.0, None, ALU.add), reads=[r_abT], writes=[r_abT])
            P.op("dve", I_recip(abT[32:40, ws], abT[32:40, ws]), reads=[r_abT], writes=[r_abT])
        src_, dst_ = abT, tA
        for st_ in (1, 2, 4, 8, 16, 32):
            sv = src_[0:8, :].rearrange("p (n c) -> p n c", c=64)
            dv = dst_[0:8, :].rearrange("p (n c) -> p n c", c=64)
            P.op("dve", I_tt(dv[:, :, st_:64], sv[:, :, st_:64], sv[:, :, 0:64 - st_], ALU.add),
                 reads=[r_abT] + r_tA_all, writes=[r_abT] + r_tA_all)
            P.op("dve", I_copy(dv[:, :, 0:st_], sv[:, :, 0:st_]), reads=[r_abT] + r_tA_all, writes=[r_abT] + r_tA_all)
            src_, dst_ = dst_, src_
        assert src_ is abT
        for blk in range(NB):
            P.op("pe", I_mmgroup(banks[2][:, blk * 16:(blk + 1) * 16],
                                 [(hT[:, k, blk * 128:(blk + 1) * 128], wabt[:, k, :]) for k in range(KC)]),
                 reads=[r_wab] + [self.r_h[k][blk // 4] for k in range(KC)], writes=[bregs[2]])
        abv = banks[2][:, 0:256].rearrange("p (b c) -> p b c", c=16)
        t1, r_t1 = T()
        t2, r_t2 = T()
        v3 = lambda a: a.rearrange("p (b c) -> p b c", c=8)
        P.op("dve", I_tt(v3(t1), abv[:, :, 0:8], v3(dtb_rep), ALU.add), reads=[bregs[2], r_small], writes=[r_t1])
        P.op("act", I_act(t1, t1, AF.Exp), reads=[r_t1], writes=[r_t1])
        P.op("act", I_act(t1, t1, AF.Ln, bias=1.0), reads=[r_t1], writes=[r_t1])
        P.op("dve", I_tt(t1, t1, nega[:, 1:129], ALU.mult), reads=[r_t1, r_nega], writes=[r_t1])
        P.op("act", I_act(v3(beta_tok), abv[:, :, 8:16], AF.Exp, scale=-1.0), reads=[bregs[2]], writes=[r_tok])
        P.op("dve", I_ts(beta_tok, beta_tok, 1.0, None, ALU.add), reads=[r_tok], writes=[r_tok])
        P.op("dve", I_recip(beta_tok, beta_tok), reads=[r_tok], writes=[r_tok])
        P.op("pe", I_mm(qb[(4, 0)], TRI, t1), reads=[r_cst, r_t1], writes=[qr[(4, 0)]])
        P.op("act", I_act(gc_tok, qb[(4, 0)], AF.Copy), reads=[qr[(4, 0)]], writes=[r_tok])
        P.op("pe", I_mm(qb[(4, 1)], SELL, gc_tok), reads=[r_cst, r_tok], writes=[qr[(4, 1)]])
        P.op("dve", I_tt(t2, qb[(4, 1)], gc_tok, ALU.subtract), reads=[qr[(4, 1)], r_tok], writes=[r_t2])
        P.op("act", I_act(kt_tok, t2, AF.Exp), reads=[r_t2], writes=[r_tok])
        P.op("act", I_act(t2, gc_tok, AF.Exp), reads=[r_tok], writes=[r_t2])
        P.op("dve", I_tt(bg_tok, beta_tok, t2, ALU.mult), reads=[r_tok, r_t2], writes=[r_tok])

        pjc = [0]

        def gen_h2(h):
            hs = h % 2
            wq = wh[(2 * h) % 3]
            wv = wh[(2 * h + 1) % 3]
            r_wq, r_wv = r_wh[(2 * h) % 3], r_wh[(2 * h + 1) % 3]
            taps = hv[hs]
            items = [(X, w) for X in range(3) for w in range(NW)]
            pb = {}

            def stA(i):
                X, w = items[i]
                wsrc, r_wsrc = (wq, r_wq) if X < 2 else (wv, r_wv)
                co = (X % 2) * 128
                ws = slice(w * WIN, (w + 1) * WIN)
                b = 4 + i % 2
                pb[i] = b
                P.op("pe", I_mmgroup(banks[b][:], [(wsrc[:, k, co:co + 128], hT[:, k, ws]) for k in range(KC)]),
                     reads=[r_wsrc] + [self.r_h[k][w] for k in range(KC)], writes=[bregs[b]])

            def stB(i):
                X, w = items[i]
                b = pb[i]
                P.op("act", I_act(raw[:, 4 + w * WIN:4 + (w + 1) * WIN], banks[b][:], AF.Copy),
                     reads=[bregs[b]], writes=[r_raw_w[1 + w]])

            def stC(i):
                X, w = items[i]
                ws = slice(w * WIN, (w + 1) * WIN)
                P.op("dve", I_ts(acc[:, ws], raw[:, 4 + w * WIN:4 + (w + 1) * WIN],
                                 taps[:, X * 4 + 3:X * 4 + 4], None, ALU.mult),
                     reads=[r_raw_w[1 + w], r_hv[hs]], writes=[r_acc_w[w]])
                for tp in (2, 1, 0):
                    P.op("dve", I_stt(acc[:, ws], raw[:, 1 + tp + w * WIN:1 + tp + (w + 1) * WIN],
                                      taps[:, X * 4 + tp:X * 4 + tp + 1], acc[:, ws], ALU.mult, ALU.add),
                         reads=[r_raw_w[w], r_raw_w[1 + w], r_hv[hs], r_acc_w[w]], writes=[r_acc_w[w]])

            def stD1(i):
                X, w = items[i]
                ws = slice(w * WIN, (w + 1) * WIN)
                if X == 2:
                    P.op("act", I_act(vT[:, ws], acc[:, ws], AF.Silu), reads=[r_acc_w[w]], writes=[r_v])
                else:
                    P.op("act", I_act(acc[:, ws], acc[:, ws], AF.Silu), reads=[r_acc_w[w]], writes=[r_acc_w[w]])
                    P.op("act", I_act(sq[1], acc[:, ws], AF.Square), reads=[r_acc_w[w]], writes=[r_sq[1]])

            def stD2(i):
                X, w = items[i]
                if X < 2:
                    P.op("pe", I_mm(banks[3][:], ones[:], sq[1]), reads=[r_sq[1], self.r_const], writes=[bregs[3]])

            def stD3(i):
                X, w = items[i]
                if X < 2:
                    P.op("act", I_act(rt, banks[3][:], AF.Ln, bias=EPS, scale=1.0), reads=[bregs[3]], writes=[r_rt])
                    P.op("act", I_act(rstd, rt, AF.Exp, scale=-0.5), reads=[r_rt], writes=[r_rstd])

            def stD4(i):
                X, w = items[i]
                if X < 2:
                    ws = slice(w * WIN, (w + 1) * WIN)
                    dst, r_dst = (qT, r_q) if X == 0 else (kT, r_k)
                    sc = (128.0 ** -0.5) if X == 0 else 1.0
                    P.op("dve", I_stt(dst[:, ws], acc[:, ws], sc, rstd, ALU.mult, ALU.mult),
                         reads=[r_acc_w[w], r_rstd], writes=[r_dst])

            stages = [stA, stB, stC, stD1, stD2, stD3, stD4]
            n = len(items)
            for t in range(n + len(stages) - 1):
                for si in range(len(stages) - 1, -1, -1):
                    i = t - si
                    if 0 <= i < n:
                        stages[si](i)
                        yield

        def emit_z(h):
            wv = wh[(2 * h + 1) % 3]
            r_wv = r_wh[(2 * h + 1) % 3]
            for w in range(NW):
                ws = slice(w * WIN, (w + 1) * WIN)
                b = pjc[0] % 2
                pjc[0] += 1
                P.op("pe", I_mmgroup(banks[b][:], [(wv[:, k, 128:256], hT[:, k, ws]) for k in range(KC)]),
                     reads=[r_wv] + [self.r_h[k][w] for k in range(KC)], writes=[bregs[b]])
                P.op("act", I_act(zt, banks[b][:], AF.Silu), reads=[bregs[b]], writes=[r_zt])
                P.op("dve", I_ts(zsT[:, ws], zt, normw_col, None, ALU.mult), reads=[r_zt, r_small], writes=[r_zs])

        def gen_scan(h):
            hs = h % 2
            P.op("dve", I_memset(Sf, 0.0), writes=[r_Sf])
            P.op("dve", I_memset(Sb, 0.0), writes=[r_Sb])
            P.op("dve", I_memset(vnew, 0.0), writes=[r_vnew])
            deferred = []
            for n in range(2 * NB):
                if n > 0:
                    yield
                blk, half = n // 2, n % 2
                ps = slice(64 * half, 64 * half + 64)
                cs = slice(n * 64, (n + 1) * 64)
                w = n // 8
                ob, r_ob = banks[2], bregs[2]
                bs = slice(blk * 128, (blk + 1) * 128)
                P.op("pe", I_mm(banks[7][:, 0:128], wT[:, bs], Sb), reads=[r_wT[blk], r_Sb], writes=[qr[(7, 0)]])
                yield
                P.op("dve", I_tt(vnew[ps, :], u[ps, blk, :], banks[7][ps, 0:128], ALU.subtract),
                     reads=[r_u[blk], qr[(7, 0)]], writes=[r_vnew])
                yield
                oc = (n % 8) * 64
                P.op("pe", I_mm(qb[(7, 1)], ktail[:, blk, half, :], vnew[:, :]), reads=[r_kt[blk], r_vnew],
                     writes=[qr[(7, 1)]])
                yield
                P.op("pe", I_mmgroup(ob[:, oc:oc + 64], [(Sb, qdT[:, cs]),
                                                        (vnew[:, :], attnT[:, blk, 64 * half:64 * half + 64])]),
                     reads=[r_Sb, r_qdb[blk], r_vnew, r_at[blk]], writes=[r_ob])
                yield
                P.op("dve", I_stt(Sb, Sf, cdv[:, n:n + 1], qb[(7, 1)], ALU.mult, ALU.add),
                     reads=[r_Sf, r_cdv, qr[(7, 1)]], writes=[r_Sb])
                yield
                P.op("dve", I_stt(Sf, Sf, cdv[:, n:n + 1], qb[(7, 1)], ALU.mult, ALU.add),
                     reads=[r_Sf, r_cdv, qr[(7, 1)]], writes=[r_Sf])
                if n % 8 == 7:
                    ws = slice(w * WIN, (w + 1) * WIN)

                    def s0(ob=ob, r_ob=r_ob):
                        P.op("act", I_act(osb, ob[:], AF.Copy), reads=[r_ob], writes=[r_osb])
                        P.op("act", I_act(sq[0], osb, AF.Square), reads=[r_osb], writes=[r_sq[0]])

                    def s1():
                        P.op("pe", I_mm(banks[0][:], ones[:], sq[0]), reads=[r_sq[0], self.r_const],
                             writes=[bregs[0]])

                    def s2():
                        P.op("act", I_act(rt2, banks[0][:], AF.Ln, bias=EPS, scale=1.0 / 128), reads=[bregs[0]],
                             writes=[r_rt2])
                        P.op("act", I_act(rstd2, rt2, AF.Exp, scale=-0.5), reads=[r_rt2], writes=[r_rstd2])

                    def s3(ws=ws):
                        P.op("dve", I_tt(osb, osb, rstd2, ALU.mult), reads=[r_osb, r_rstd2], writes=[r_osb])
                        P.op("dve", I_tt(ogT, osb, zsT[:, ws], ALU.mult), reads=[r_osb, r_zs], writes=[r_og])

                    def mk_proj(dch, ws=ws, w=w):
                        def f():
                            P.op("pe", I_mm(banks[1][:], wo[hs][:, dch * 128:(dch + 1) * 128], ogT),
                                 reads=[r_wo[hs], r_og], writes=[bregs[1]])
                            P.op("dve", I_tt(xT[:, dch, ws], banks[1][:], xT[:, dch, ws], ALU.add),
                                 reads=[bregs[1], self.r_x[dch][w]], writes=[self.r_x[dch][w]])
                        return f
                    s0()
                    deferred.extend([s1, s2, s3] + [mk_proj(dch) for dch in range(KC)])
                else:
                    for _ in range(2):
                        if deferred:
                            deferred.pop(0)()
            while deferred:
                deferred.pop(0)()

        prev_scan = None
        for h in range(8):
            g2 = gen_h2(h)
            if prev_scan is not None:
                a_live, b_live = True, True
                while a_live or b_live:
                    if a_live:
                        try:
                            next(prev_scan)
                        except StopIteration:
                            a_live = False
                    for _ in range(H2_PER_SCAN if a_live else 1000):
                        if b_live:
                            try:
                                next(g2)
                            except StopIteration:
                                b_live = False
            else:
                for _ in g2:
                    pass
            emit_z(h)
            if h + 1 < 8:
                load_head(h + 1)
            for eng_ in ("pe", "act", "dve"):
                P.wait_all(eng_, r_raw_w + r_acc_w)
            Wt = [raw[:, 4 + i * 512:4 + (i + 1) * 512] for i in range(4)] + \
                 [acc[:, i * 512:(i + 1) * 512] for i in range(4)]
            v4 = lambda a: a.rearrange("p (b i) -> p b i", b=4)
            RR = (lambda a: a.bitcast(FP32R)) if DBL_R else (lambda a: a)
            bmid = lambda a: a.unsqueeze(1).broadcast_to([128, 4, 128])
            for gi in range(NB // 4):
                b0 = gi * 4
                gsl = slice(b0 * 128, (b0 + 4) * 128)
                bsl = [slice((b0 + bi) * 128, (b0 + bi + 1) * 128) for bi in range(4)]
                qs_ = [slice(bi * 128, (bi + 1) * 128) for bi in range(4)]

                def tokb(t, pp=slice(0, 128)):
                    return t[pp, b0 * 8 + h:(b0 + 3) * 8 + h + 1:8].unsqueeze(2).broadcast_to([pp.stop - pp.start, 4, 128])
                W_dd, W_eg, W_gm, W_gs, W_B, W_A, W_M, W_X = Wt
                rW = r_W
                P.op("pe", I_multi([(banks[4][:, qs_[bi]], SEL[0:8, h, :], abT[0:8, bsl[bi]]) for bi in range(4)]),
                     reads=[r_cst, r_abT], writes=[bregs[4]])
                P.op("pe", I_multi([(banks[5][:, qs_[bi]], SEL[32:40, h, :], abT[32:40, bsl[bi]]) for bi in range(4)]),
                     reads=[r_cst, r_abT], writes=[bregs[5]])
                P.op("pe", I_multi([(banks[6][:, qs_[bi]], kT[:, bsl[bi]], kT[:, bsl[bi]]) for bi in range(4)]),
                     reads=[r_k], writes=[bregs[6]])
                P.op("pe", I_multi([(banks[7][:, qs_[bi]], kT[:, bsl[bi]], qT[:, bsl[bi]]) for bi in range(4)]),
                     reads=[r_k, r_q], writes=[bregs[7]])
                P.op("dve", I_tt(v4(W_dd), v4(banks[4][:]), tokb(gc_tok), ALU.subtract), reads=[bregs[4], r_tok],
                     writes=[rW[0]])
                P.op("dve", I_ts(W_dd, W_dd, 0.0, None, ALU.min), reads=[rW[0]], writes=[rW[0]])
                P.op("act", I_act(W_dd, W_dd, AF.Exp), reads=[rW[0]], writes=[rW[0]])
                P.op("act", I_act(W_eg, banks[4][:], AF.Exp), reads=[bregs[4]], writes=[rW[1]])
                P.op("dve", I_tt(v4(W_gm), v4(W_dd), bmid(MUI), ALU.mult), reads=[rW[0], r_cst], writes=[rW[2]])
                P.op("dve", I_tt(attnT[:, b0:b0 + 4, :], v4(banks[7][:]), v4(W_gm), ALU.mult),
                     reads=[bregs[7], rW[2]], writes=[r_at[b0 + bi] for bi in range(4)])
                P.op("dve", I_tt(v4(W_gs), v4(W_dd), bmid(MUS), ALU.mult), reads=[rW[0], r_cst], writes=[rW[3]])
                P.op("dve", I_tt(W_gs, banks[5][:], W_gs, ALU.mult), reads=[bregs[5], rW[3]], writes=[rW[3]])
                P.op("dve", I_tt(RR(W_B), banks[6][:], W_gs, ALU.mult), reads=[bregs[6], rW[3]], writes=[rW[4]])
                P.op("dve", I_tt(qdT[:, gsl], qT[:, gsl], W_eg, ALU.mult), reads=[r_q, rW[1]],
                     writes=[r_qdb[b0 + bi] for bi in range(4)])
                P.op("act", I_act(cdv[:, 2 * b0:2 * b0 + 8], W_eg[:, 63:512:64], AF.Copy), reads=[rW[1]],
                     writes=[r_cdv])
                P.op("pe", I_multi_tr([(banks[0][:, qs_[bi]], W_B[:, qs_[bi]], IDN) for bi in range(4)]),
                     reads=[rW[4], r_cst], writes=[bregs[0]])
                P.op("act", I_act(RR(W_A), banks[0][:], AF.Copy), reads=[bregs[0]], writes=[rW[5]])
                P.op("dve", I_tt(v4(RR(W_M)), bmid(IDN), v4(W_B), ALU.subtract), reads=[r_cst, rW[4]], writes=[rW[6]])
                Bc, iB, Ac, iA = W_B, 4, W_A, 5
                free = [(W_dd, 0), (W_eg, 1)]
                for lev in range(1, 6):
                    An, iAn = free.pop(0)
                    P.op("pe", I_multi([(banks[1][:, qs_[bi]], Bc[:, qs_[bi]], Ac[:, qs_[bi]]) for bi in range(4)], r=True),
                         reads=[rW[iB], rW[iA]], writes=[bregs[1]])
                    if lev < 5:
                        Bn, iBn = free.pop(0)
                        P.op("pe", I_multi([(banks[2][:, qs_[bi]], Ac[:, qs_[bi]], Bc[:, qs_[bi]])
                                            for bi in range(4)], r=True),
                             reads=[rW[iB], rW[iA]], writes=[bregs[2]])
                    P.op("dve", I_copy(RR(An), banks[1][:]), reads=[bregs[1]], writes=[rW[iAn]])
                    if lev < 5:
                        P.op("act", I_act(RR(Bn), banks[2][:], AF.Copy), reads=[bregs[2]], writes=[rW[iBn]])
                    P.op("pe", I_multi([(banks[3][:, qs_[bi]], An[:, qs_[bi]], W_M[:, qs_[bi]]) for bi in range(4)], r=True),
                         reads=[rW[iAn], rW[6]], writes=[bregs[3]])
                    P.op("dve", I_tt(RR(W_M), banks[3][:], W_M, ALU.add), reads=[bregs[3], rW[6]], writes=[rW[6]])
                    free.append((Ac, iA))
                    Ac, iA = An, iAn
                    if lev < 5:
                        free.append((Bc, iB))
                        Bc, iB = Bn, iBn
                ktp = banks[0][:].bitcast(BF16)[:, 0:512]
                vtp = banks[1][:].bitcast(BF16)[:, 0:512]
                P.op("pe", I_multi_tr([(ktp[:, qs_[bi]], kT[:, bsl[bi]], idb) for bi in range(4)]),
                     reads=[r_k, r_cst], writes=[bregs[0]])
                P.op("pe", I_multi_tr([(vtp[:, qs_[bi]], vT[:, bsl[bi]], idb) for bi in range(4)]),
                     reads=[r_v, r_cst], writes=[bregs[1]])
                W_rw, W_ru = W_gm, W_gs
                P.op("dve", I_tt(v4(RR(W_rw)), v4(ktp), tokb(bg_tok), ALU.mult), reads=[bregs[0], r_tok], writes=[rW[2]])
                P.op("dve", I_tt(v4(RR(W_ru)), v4(vtp), tokb(beta_tok), ALU.mult), reads=[bregs[1], r_tok], writes=[rW[3]])
                for hf in range(2):
                    pp = slice(64 * hf, 64 * hf + 64)
                    P.op("dve", I_tt(ktail[pp, b0:b0 + 4, hf, :], v4(ktp)[pp], tokb(kt_tok, pp), ALU.mult),
                         reads=[bregs[0], r_tok], writes=[r_kt[b0 + bi] for bi in range(4)])
                P.op("pe", I_multi([(banks[2][:, qs_[bi]], W_M[:, qs_[bi]], W_ru[:, qs_[bi]]) for bi in range(4)], r=True),
                     reads=[rW[6], rW[3]], writes=[bregs[2]])
                P.op("act", I_act(u[:, b0:b0 + 4, :], v4(banks[2][:]), AF.Copy), reads=[bregs[2]],
                     writes=[r_u[b0 + bi] for bi in range(4)])
                P.op("pe", I_multi([(banks[3][:, qs_[bi]], W_rw[:, qs_[bi]], W_M[:, qs_[bi]]) for bi in range(4)], r=True),
                     reads=[rW[6], rW[2]], writes=[bregs[3]])
                P.op("dve", I_copy(wT[:, gsl], banks[3][:]), reads=[bregs[3]],
                     writes=[r_wT[b0 + bi] for bi in range(4)])
            for eng_ in ("pe", "act", "dve"):
                P.wait_all(eng_, r_W)
            prev_scan = gen_scan(h)
        for _ in prev_scan:
            pass
        allregs = r_raw_w + r_acc_w + [r_cst, r_small, r_nega, r_abT, r_tok, r_wab, r_q, r_k, r_v, r_qd, r_zs, r_cdv,
                   r_Sf, r_Sb, r_vnew, r_osb, r_og, r_rt2, r_rstd2]
        self.phase_barrier([allregs, r_wh, r_wo, r_hv, r_u, r_wT, r_at, r_kt, r_qdb, r_tmp, r_W, bregs])
        ar.reset(m)

    def phase_barrier(self, reg_lists):
        regs = [r for rl in reg_lists for r in rl]
        for n in ("pe", "act", "dve", "pool", "sp"):
            self.P.wait_all(n, regs)

    def build(self):
        nc, P, st = self.nc, self.P, self.stack
        self.s_wgu = [P.dma_src() for _ in range(4)]
        self.s_wd = [P.dma_src() for _ in range(3)]
        self.s_awb = [P.dma_src() for _ in range(3)]
        self.s_awout = P.dma_src()
        self.s_eb = [P.dma_src() for _ in range(2)]
        self.s_bc = [P.dma_src() for _ in range(3)]
        self.s_wh = [P.dma_src() for _ in range(3)]
        self.s_wo = [P.dma_src() for _ in range(2)]
        self.s_hv = [P.dma_src() for _ in range(2)]
        self.emit_consts()
        for s in range(self.n_seq):
            self.emit_load_x(s)
            for pi, ph in enumerate(self.phases):
                if ph[0] == "ffn":
                    nxt = self.phases[pi + 1] if pi + 1 < len(self.phases) else None
                    prv = self.phases[pi - 1] if pi > 0 else None
                    nni, nip = None, False
                    if nxt is not None and nxt[0] in ("ffn", "mixa", "mixb"):
                        nni = nxt[2]
                    elif nxt is not None and nxt[0] == "final":
                        nni, nip = 12, True
                    self.emit_ffn(ph[1], ph[2],
                                  pre_normed=(prv is not None and prv[0] == "ffn"),
                                  next_norm_idx=nni, next_inplace=nip)
                elif ph[0] == "mixa":
                    prv = self.phases[pi - 1] if pi > 0 else None
                    self.emit_mixa(ph[1], ph[2], pre_normed=(prv is not None and prv[0] == "ffn"))
                elif ph[0] == "mixb":
                    prv = self.phases[pi - 1] if pi > 0 else None
                    self.emit_mixb(ph[1], ph[2], pre_normed=(prv is not None and prv[0] == "ffn"))
                elif ph[0] == "final":
                    prv = self.phases[pi - 1] if pi > 0 else None
                    self.emit_rmsnorm(12, [2, 3] if (prv is not None and prv[0] == "ffn") else range(NW),
                                      out_f32_inplace=True)
                else:
                    raise ValueError(ph)
            self.emit_store_x(s)
        P.wait_all("sp", [r for k in range(KC) for r in self.r_x[k]])
        with nc.Block() as block:
            P.replay(block)
        st.close()
        return nc


def full_phases():
    ph = []
    for i in range(DEPTH):
        ph.append(("ffn", 2 * i, 3 * i))
        ph.append(("mixa" if i % 2 == 0 else "mixb", i // 2, 3 * i + 1))
        ph.append(("ffn", 2 * i + 1, 3 * i + 2))
    ph.append(("final",))
    return ph


def prep_weights(inp):
    f32 = np.float32
    out = {}
    ng = np.concatenate([np.asarray(inp["norm_g"], f32).reshape(12, D), np.asarray(inp["final_g"], f32).reshape(1, D)], 0)
    out["normg"] = np.ascontiguousarray(ng.reshape(13, KC, 128).transpose(2, 0, 1).reshape(128, 13 * KC))
    wg = np.asarray(inp["ffn_w_gate"], f32).reshape(2 * DEPTH, KC, 128, FC, 128)
    wu = np.asarray(inp["ffn_w_up"], f32).reshape(2 * DEPTH, KC, 128, FC, 128)
    wgu = np.stack([wg, wu], 0)
    wgu = wgu.transpose(1, 4, 3, 0, 2, 5)
    out["wgu"] = np.ascontiguousarray(wgu).reshape(2 * DEPTH, FC, 128, 2 * KC * 128)
    wd = np.asarray(inp["ffn_w_down"], f32).reshape(2 * DEPTH, FC, 128, KC, 128)
    wd = wd.transpose(0, 3, 2, 1, 4)
    out["wd"] = np.ascontiguousarray(wd).reshape(2 * DEPTH, KC, 128, FC * 128)
    awin = np.asarray(inp["a_w_in"], f32).reshape(2, KC, 128, 3, 3, 4, 128)
    awin = awin.transpose(0, 5, 2, 1, 3, 4, 6)
    out["awin"] = np.ascontiguousarray(awin).reshape(2, 4, 128, KC, 3 * 384)
    awout = np.asarray(inp["a_w_out"], f32).reshape(2, 4, 128, D).transpose(0, 2, 1, 3)
    out["awout"] = np.ascontiguousarray(awout)
    out["bt"] = bias_table(np.asarray(inp["rel_bias"], f32))
    bw = np.asarray(inp["b_w_in"], f32)
    w4 = bw[:, :, :4096].reshape(2, KC, 128, 4, 8, 128).transpose(0, 4, 2, 1, 3, 5)
    out["bwin"] = np.ascontiguousarray(w4).reshape(2, 8, 128, KC, 512)
    wa = bw[:, :, 4096:4104].reshape(2, KC, 128, 8).transpose(0, 2, 1, 3)
    wb_ = bw[:, :, 4104:4112].reshape(2, KC, 128, 8).transpose(0, 2, 1, 3)
    wab = np.zeros((2, 128, KC, 64), f32)
    wab[..., 0:8] = wa
    wab[..., 32:40] = wb_
    out["wab"] = wab
    out["wabt"] = np.ascontiguousarray(np.concatenate([wa, wb_], -1))
    cw = np.asarray(inp["b_conv_w"], f32).reshape(2, 4, 3, 8, 128).transpose(0, 3, 4, 2, 1)
    out["bconv"] = np.ascontiguousarray(cw).reshape(2, 8, 128, 12)
    sm = np.zeros((2, 128, 259), f32)
    dtb = np.asarray(inp["b_dt_bias"], f32)
    alog = np.asarray(inp["b_a_log"], f32)
    sm[:, 0:8, 0] = dtb
    sm[:, 0:8, 1] = alog
    sm[:, :, 2] = np.asarray(inp["b_norm_w"], f32)
    sm[:, :, 3:131] = np.tile(dtb, (1, 16))[:, None, :]
    sm[:, :, 131:259] = np.tile(alog, (1, 16))[:, None, :]
    out["bsmall"] = sm
    out["bwout"] = np.ascontiguousarray(np.asarray(inp["b_w_out"], f32).reshape(2, 8, 128, D))
    out["bconst"] = b_consts()
    return out


def b_consts():
    i = np.arange(128)
    same = (i[:, None] // 64) == (i[None, :] // 64)
    c = np.zeros((128, 13 * 128), np.float32)
    c[:, 0:128] = (same & (i[:, None] <= i[None, :]))
    c[:, 128:256] = (i[:, None] == (i[None, :] // 64) * 64 + 63)
    c[:, 256:384] = np.eye(128)
    c[:, 384:512] = (same & (i[None, :] >= i[:, None]))
    c[:, 512:640] = (same & (i[None, :] > i[:, None]))
    sel = np.zeros((128, 8, 128), np.float32)
    for h in range(8):
        sel[h, h, :] = 1.0
        sel[32 + h, h, :] = 1.0
    c[:, 640:] = sel.reshape(128, 1024)
    return c


def t5_bucket_np(dist):
    n = dist.astype(np.float32)
    large = np.float32(16.0) + (np.log(np.maximum(n, np.float32(1.0)) / np.float32(16.0)).astype(np.float32)
                                / np.float32(np.log(2048.0 / 16.0)) * np.float32(16.0)).astype(np.float32)
    large = np.minimum(large.astype(np.int32), 31)
    return np.where(dist < 16, dist, large)


def bias_table(rel_bias):
    kk = np.arange(128)[:, None]
    jj = np.arange(256)[None, :]
    rel = jj - kk
    valid = (rel >= 0) & (rel <= 128)
    bt = np.full((4, 128, 6, 256), -30000.0, np.float32)
    for g, dil in enumerate((1, 4, 16)):
        bk = t5_bucket_np(np.maximum(rel, 0) * dil)
        for pair in range(4):
            for hp in range(2):
                tab = rel_bias[:, g * 8 + 2 * pair + hp][bk]
                bt[pair, :, g * 2 + hp, :] = np.where(valid, tab, np.float32(-30000.0))
    return bt


def x_to_dev(x_seqs):
    n = x_seqs.shape[0]
    return np.ascontiguousarray(x_seqs.reshape(n, S, KC, 128).transpose(0, 2, 3, 1))


def x_from_dev(o):
    n = o.shape[0]
    return np.ascontiguousarray(o.transpose(0, 3, 1, 2).reshape(n, S, D))


_PROG_CACHE = {}


def get_prog(n_seq, phases):
    key = (n_seq, tuple(phases))
    if key not in _PROG_CACHE:
        _PROG_CACHE[key] = Builder(n_seq, list(phases)).build()
    return _PROG_CACHE[key]


def run(inputs, n_seq_per_core, phases, n_cores=N_CORES, trace=False, x_override=None):
    w = prep_weights(inputs)
    x = np.asarray(inputs["x"] if x_override is None else x_override, np.float32)
    nc = get_prog(n_seq_per_core, phases)
    in_maps = []
    for c in range(n_cores):
        m = dict(w)
        m["xT"] = x_to_dev(x[c * n_seq_per_core:(c + 1) * n_seq_per_core])
        in_maps.append(m)
    res = run_bass_kernel_spmd(nc, in_maps, core_ids=list(range(n_cores)), trace=trace)
    outs = [x_from_dev(r["outT"]) for r in res.results]
    return np.concatenate(outs, 0), res


def kernel(**inputs):
    out, _ = run(inputs, 4, full_phases())
    return out
```
